# Optimizing a Trainium2 kernel written in Bass

```python
import math
import jax, jax.numpy as jnp
from jax import lax
import numpy as np

D_MODEL = 1024
BATCH = 16
SEQ = 2048
DEPTH = 4
DEC_BATCH = 4
DEC_SEQ = 8192
PAST_LEN = 128

N_MIXERS = 2
N_ATTN_LAYERS = (DEPTH + 1) // 2
N_RET_LAYERS = DEPTH // 2
PLE_DIM = 256
D_FF = 4 * D_MODEL
A_HEAD_DIM = 64
A_HEADS = D_MODEL // (2 * A_HEAD_DIM)
A_QK = A_HEADS * 2 * A_HEAD_DIM
A_V = A_HEADS * 2 * A_HEAD_DIM
ROPE_THETA = 10000.0
Q_BLOCK = 128
R_HEADS = 4
R_DK = D_MODEL // R_HEADS
R_DV = 2 * D_MODEL // R_HEADS
R_QK = R_HEADS * R_DK
R_V = R_HEADS * R_DV
CHUNK = 128
XPOS_BASE = 10000.0

kernel_name = "hybrid_diffattn_retnet_encoder"


def rms_norm(x, w, eps=1e-6):
    xf = x.astype(jnp.float32)
    y = xf * lax.rsqrt(jnp.mean(xf * xf, axis=-1, keepdims=True) + eps)
    if w is not None:
        y = y * w.astype(jnp.float32)
    return y.astype(x.dtype)


def rope_tables(S, dim, dtype):
    inv_freq = 1.0 / (ROPE_THETA ** (jnp.arange(0, dim, 2, dtype=jnp.float32) / dim))
    ang = jnp.arange(S, dtype=jnp.float32)[:, None] * inv_freq[None, :]
    ang = jnp.concatenate([ang, ang], axis=-1)
    return jnp.cos(ang).astype(dtype), jnp.sin(ang).astype(dtype)


def apply_rope_half(x, cos, sin):
    x1, x2 = jnp.split(x, 2, axis=-1)
    return x * cos + jnp.concatenate([-x2, x1], axis=-1) * sin


def xpos_tables(S, dim, dtype):
    angle = 1.0 / (XPOS_BASE ** jnp.linspace(0.0, 1.0, dim // 2, dtype=jnp.float32))
    angle = jnp.repeat(angle, 2)
    ang = jnp.arange(S, dtype=jnp.float32)[:, None] * angle[None, :]
    return jnp.cos(ang).astype(dtype), jnp.sin(ang).astype(dtype)


def theta_shift(x, cos, sin):
    x1 = x[..., 0::2]
    x2 = x[..., 1::2]
    rot = jnp.stack([-x2, x1], axis=-1).reshape(x.shape)
    return x * cos + rot * sin


def diff_attention(h, w_qkv, w_o, lq1, lk1, lq2, lk2, subln, layer_idx):
    B, S, _ = h.shape
    dt = h.dtype
    qkv = h @ w_qkv
    q, k, v = jnp.split(qkv, [A_QK, 2 * A_QK], axis=-1)
    q = q.reshape(B, S, A_HEADS, 2, A_HEAD_DIM)
    k = k.reshape(B, S, A_HEADS, 2, A_HEAD_DIM)
    v = v.reshape(B, S, A_HEADS, 2 * A_HEAD_DIM)
    cos, sin = rope_tables(S, A_HEAD_DIM, dt)
    cos = cos[:, None, None, :]
    sin = sin[:, None, None, :]
    q = apply_rope_half(q, cos, sin) * (A_HEAD_DIM ** -0.5)
    k = apply_rope_half(k, cos, sin)
    q = q.transpose(0, 2, 3, 1, 4)
    k = k.transpose(0, 2, 3, 1, 4)
    v = v.transpose(0, 2, 1, 3)
    lam_init = 0.8 - 0.6 * math.exp(-0.3 * layer_idx)
    lam = (jnp.exp(jnp.sum(lq1.astype(jnp.float32) * lk1.astype(jnp.float32)))
           - jnp.exp(jnp.sum(lq2.astype(jnp.float32) * lk2.astype(jnp.float32)))
           + lam_init)
    nb = S // Q_BLOCK
    qb = q.reshape(B, A_HEADS, 2, nb, Q_BLOCK, A_HEAD_DIM).transpose(3, 0, 1, 2, 4, 5)

    def block(qi):
        s = jnp.einsum('bhcqd,bhckd->bhcqk', qi, k).astype(jnp.float32)
        pr = jax.nn.softmax(s, axis=-1)
        a = (pr[:, :, 0] - lam * pr[:, :, 1]).astype(v.dtype)
        return jnp.einsum('bhqk,bhkv->bhqv', a, v)

    o = lax.map(block, qb)
    o = o.transpose(1, 0, 3, 2, 4).reshape(B, S, A_HEADS, 2 * A_HEAD_DIM)
    o = rms_norm(o, subln, 1e-5) * (1.0 - lam_init)
    return o.reshape(B, S, A_V) @ w_o


def retention_chunked(q, k, v, lg, inclusive):
    B, H, S, dk = q.shape
    dv = v.shape[-1]
    dt = q.dtype
    n = S // CHUNK
    pos = jnp.arange(CHUNK, dtype=jnp.float32)
    diff = pos[:, None] - pos[None, :]
    mask = (diff >= 0) if inclusive else (diff > 0)
    dmask = jnp.exp(jnp.where(mask[None], diff[None] * lg[:, None, None], -jnp.inf)).astype(dt)
    xi = jnp.exp((pos + 1.0)[None, :] * lg[:, None]).astype(dt)[..., None]
    zeta = jnp.exp((CHUNK - 1.0 - pos)[None, :] * lg[:, None]).astype(dt)[..., None]
    cdecay = jnp.exp(CHUNK * lg).astype(dt)[:, None, None]

    def to_chunks(t):
        return t.reshape(B, H, n, CHUNK, t.shape[-1]).transpose(2, 0, 1, 3, 4)

    def step(R, inp):
        qc, kc, vc = inp
        inner = jnp.einsum('bhqk,bhkv->bhqv', jnp.einsum('bhqd,bhkd->bhqk', qc, kc) * dmask, vc)
        cross = jnp.einsum('bhqd,bhdv->bhqv', qc, R) * xi
        R_new = R * cdecay + jnp.einsum('bhkd,bhkv->bhdv', kc * zeta, vc)
        return R_new, inner + cross

    R0 = jnp.zeros((B, H, dk, dv), dt)
    _, out = lax.scan(step, R0, (to_chunks(q), to_chunks(k), to_chunks(v)))
    return out.transpose(1, 2, 0, 3, 4).reshape(B, H, S, dv)


def retention(h, w_in, w_out, decay):
    B, S, _ = h.shape
    dt = h.dtype
    qkvg = h @ w_in
    q, k, v, g = jnp.split(qkvg, [R_QK, 2 * R_QK, 2 * R_QK + R_V], axis=-1)
    q = q.reshape(B, S, R_HEADS, R_DK)
    k = k.reshape(B, S, R_HEADS, R_DK)
    v = v.reshape(B, S, R_HEADS, R_DV)
    cos, sin = xpos_tables(S, R_DK, dt)
    cos = cos[:, None, :]
    sin = sin[:, None, :]
    q = theta_shift(q, cos, sin).transpose(0, 2, 1, 3)
    k = (theta_shift(k, cos, sin) * (R_DK ** -0.5)).transpose(0, 2, 1, 3)
    v = v.transpose(0, 2, 1, 3)
    lg = -jnp.exp(decay.astype(jnp.float32))
    fwd = retention_chunked(q, k, v, lg[0], True)
    bwd = jnp.flip(retention_chunked(jnp.flip(q, 2), jnp.flip(k, 2), jnp.flip(v, 2), lg[1], False), 2)
    o = (fwd + bwd).transpose(0, 2, 1, 3)
    o = rms_norm(o, None, 1e-6).reshape(B, S, R_V)
    return (jax.nn.silu(g) * o) @ w_out


def encoder_trunk(x, p, norm_pre_mix, norm_post_mix, norm_pre_mlp, norm_post_mlp,
                  attn_w_qkv, attn_w_o, attn_lambda_q1, attn_lambda_k1, attn_lambda_q2,
                  attn_lambda_k2, attn_subln, ret_w_in, ret_w_out, ret_decay,
                  mlp_w_in, mlp_w_out, ple_w_proj, ple_w_gate):
    for i in range(DEPTH):
        j = i // N_MIXERS
        h = rms_norm(x, norm_pre_mix[i])
        if i % N_MIXERS == 0:
            m = diff_attention(h, attn_w_qkv[j], attn_w_o[j], attn_lambda_q1[j], attn_lambda_k1[j],
                               attn_lambda_q2[j], attn_lambda_k2[j], attn_subln[j], i)
        else:
            m = retention(h, ret_w_in[j], ret_w_out[j], ret_decay[j])
        x = x + rms_norm(m, norm_post_mix[i])
        h = rms_norm(x, norm_pre_mlp[i])
        m = jnp.square(jax.nn.relu(h @ mlp_w_in[i])) @ mlp_w_out[i]
        x = x + rms_norm(m, norm_post_mlp[i])
        gate = jax.nn.sigmoid(rms_norm(x, None) @ ple_w_gate[i])
        x = x + (p[i] @ ple_w_proj[i]) * gate
    return x


def setup_inputs(seed: int = 0) -> dict:
    key = jax.random.key(seed)
    ks = jax.random.split(key, 24)
    f32 = jnp.float32

    def nrm(k, shape, scale):
        return jax.random.normal(k, shape, f32) * scale

    def gain(k, shape):
        return 1.0 + 0.01 * jax.random.normal(k, shape, f32)

    base = np.log(-np.log(1.0 - 2.0 ** (-5.0 - np.arange(R_HEADS)))).astype(np.float32)
    ret_decay = jnp.asarray(base)[None, None, :] + 0.05 * jax.random.normal(ks[15], (N_RET_LAYERS, 2, R_HEADS), f32)
    return {
        "x_prompt": nrm(ks[0], (BATCH, SEQ, D_MODEL), 1.0),
        "x_sample": nrm(ks[1], (DEC_BATCH, DEC_SEQ, D_MODEL), 1.0),
        "p_prompt": nrm(ks[2], (DEPTH, BATCH, SEQ, PLE_DIM), 1.0),
        "p_sample": nrm(ks[3], (DEPTH, DEC_BATCH, DEC_SEQ, PLE_DIM), 1.0),
        "norm_pre_mix": gain(ks[4], (DEPTH, D_MODEL)),
        "norm_post_mix": gain(ks[5], (DEPTH, D_MODEL)),
        "norm_pre_mlp": gain(ks[6], (DEPTH, D_MODEL)),
        "norm_post_mlp": gain(ks[7], (DEPTH, D_MODEL)),
        "attn_w_qkv": nrm(ks[8], (N_ATTN_LAYERS, D_MODEL, 2 * A_QK + A_V), D_MODEL ** -0.5),
        "attn_w_o": nrm(ks[9], (N_ATTN_LAYERS, A_V, D_MODEL), A_V ** -0.5),
        "attn_lambda_q1": nrm(ks[10], (N_ATTN_LAYERS, A_HEAD_DIM), 0.1),
        "attn_lambda_k1": nrm(ks[11], (N_ATTN_LAYERS, A_HEAD_DIM), 0.1),
        "attn_lambda_q2": nrm(ks[12], (N_ATTN_LAYERS, A_HEAD_DIM), 0.1),
        "attn_lambda_k2": nrm(ks[13], (N_ATTN_LAYERS, A_HEAD_DIM), 0.1),
        "attn_subln": gain(ks[14], (N_ATTN_LAYERS, 2 * A_HEAD_DIM)),
        "ret_w_in": nrm(ks[16], (N_RET_LAYERS, D_MODEL, 2 * R_QK + 2 * R_V), D_MODEL ** -0.5),
        "ret_w_out": nrm(ks[17], (N_RET_LAYERS, R_V, D_MODEL), R_V ** -0.5),
        "ret_decay": ret_decay,
        "mlp_w_in": nrm(ks[18], (DEPTH, D_MODEL, D_FF), D_MODEL ** -0.5),
        "mlp_w_out": nrm(ks[19], (DEPTH, D_FF, D_MODEL), D_FF ** -0.5),
        "ple_w_proj": nrm(ks[20], (DEPTH, PLE_DIM, D_MODEL), PLE_DIM ** -0.5),
        "ple_w_gate": nrm(ks[21], (DEPTH, D_MODEL, D_MODEL), D_MODEL ** -0.5),
    }


def reference(x_prompt, x_sample, p_prompt, p_sample, norm_pre_mix, norm_post_mix,
              norm_pre_mlp, norm_post_mlp, attn_w_qkv, attn_w_o, attn_lambda_q1,
              attn_lambda_k1, attn_lambda_q2, attn_lambda_k2, attn_subln, ret_w_in,
              ret_w_out, ret_decay, mlp_w_in, mlp_w_out, ple_w_proj, ple_w_gate):
    y_prompt = encoder_trunk(x_prompt, p_prompt, norm_pre_mix, norm_post_mix, norm_pre_mlp,
                             norm_post_mlp, attn_w_qkv, attn_w_o, attn_lambda_q1, attn_lambda_k1,
                             attn_lambda_q2, attn_lambda_k2, attn_subln, ret_w_in, ret_w_out,
                             ret_decay, mlp_w_in, mlp_w_out, ple_w_proj, ple_w_gate)
    y_sample = encoder_trunk(x_sample, p_sample, norm_pre_mix, norm_post_mix, norm_pre_mlp,
                             norm_post_mlp, attn_w_qkv, attn_w_o, attn_lambda_q1, attn_lambda_k1,
                             attn_lambda_q2, attn_lambda_k2, attn_subln, ret_w_in, ret_w_out,
                             ret_decay, mlp_w_in, mlp_w_out, ple_w_proj, ple_w_gate)
    return (y_prompt, y_sample)
```

```python
import math
from contextlib import ExitStack
import numpy as np
import concourse.bass as bass
import concourse.mybir as mybir
from concourse.bass_utils import run_bass_kernel_spmd

F32 = mybir.dt.float32
BF16 = mybir.dt.bfloat16
AF = mybir.ActivationFunctionType
ALU = mybir.AluOpType
AX = mybir.AxisListType

D = 1024
DFF = 4096
PLE = 256
TT = 512
NEG = -30000.0


class Sem:
    def __init__(self, nc, name):
        self.h = nc.alloc_semaphore(name)
        self.cnt = 0


class Eng:
    def __init__(self, nc, name, h, same_raw):
        self.name = name
        self.h = h
        self.sem = Sem(nc, "e_" + name)
        self.waited = {}
        self.same_raw = same_raw


class Buf:
    def __init__(self, t=None):
        self.t = t
        self.w = {}
        self.r = {}
        self.dsem = None


class Rot:
    def __init__(self, items):
        self.items = items
        self.i = 0

    def next(self):
        b = self.items[self.i % len(self.items)]
        self.i += 1
        return b


class KB:
    def __init__(self, NT, depth=4, debug=False):
        self.debug = debug
        self.nc = nc = bass.Bass("TRN2", target_bir_lowering=False)
        self.NT = NT
        self.NTT = NT // TT
        self.NCH = NT // 128
        self.depth = depth
        self.pe = Eng(nc, "pe", nc.tensor, False)
        self.act = Eng(nc, "act", nc.scalar, True)
        self.dve = Eng(nc, "dve", nc.vector, True)
        self.pool = Eng(nc, "pool", nc.gpsimd, True)
        self.sp = Eng(nc, "sp", nc.sync, False)
        self.engs = [self.pe, self.act, self.dve, self.pool, self.sp]
        self.all_sems = [e.sem for e in self.engs]
        self.free_dsems = []
        self.phase_dsems = []
        self.uid = 0
        self.st = None
        self.alt = 0

    def _need(self, r, w):
        need = {}
        for b in r:
            for S, v in b.w.items():
                if v > need.get(S, 0):
                    need[S] = v
        for b in w:
            for S, v in b.w.items():
                if v > need.get(S, 0):
                    need[S] = v
            for S, v in b.r.items():
                if v > need.get(S, 0):
                    need[S] = v
        return need

    def _waits(self, E, need, r):
        for S, v in need.items():
            if S is E.sem:
                continue
            if v > E.waited.get(S, 0):
                E.h.wait_ge(S.h, v)
                E.waited[S] = v
        if E.same_raw:
            raw = 0
            for b in r:
                v = b.w.get(E.sem, 0)
                if v > raw:
                    raw = v
            if raw > E.sem.cnt - 2 and raw > E.waited.get(E.sem, 0):
                E.h.wait_ge(E.sem.h, raw)
                E.waited[E.sem] = raw

    def op(self, E, fn, r=(), w=()):
        self._waits(E, self._need(r, w), r)
        ins = fn()
        E.sem.cnt += 1
        ins.then_inc(E.sem.h, 1)
        v = E.sem.cnt
        for b in r:
            b.r[E.sem] = v
        for b in w:
            b.w[E.sem] = v
        return ins

    def dma(self, out, in_, r=(), w=(), sb=None):
        Q = self.sp
        if sb.dsem is None:
            if self.free_dsems:
                sb.dsem = self.free_dsems.pop()
            else:
                self.uid += 1
                sb.dsem = Sem(self.nc, "d%d" % self.uid)
                self.all_sems.append(sb.dsem)
            self.phase_dsems.append(sb.dsem)
        self._waits(Q, self._need(r, w), ())
        ins = Q.h.dma_start(out=out, in_=in_)
        S = sb.dsem
        S.cnt += 16
        ins.then_inc(S.h, 16)
        for b in r:
            b.r[S] = S.cnt
        for b in w:
            b.w[S] = S.cnt

    def barrier(self):
        for E in self.engs:
            for S in self.all_sems:
                if S is E.sem:
                    continue
                if S.cnt > E.waited.get(S, 0):
                    E.h.wait_ge(S.h, S.cnt)
                    E.waited[S] = S.cnt

    def sb(self, shape, dt, name="t"):
        self.uid += 1
        t = self.st.enter_context(self.nc.sbuf_tensor("%s_%d" % (name, self.uid), list(shape), dt))
        return Buf(t)

    def rot(self, n, shape, dt, name="r"):
        return Rot([self.sb(shape, dt, name) for _ in range(n)])

    def begin_phase(self):
        self.st = ExitStack()
        self.st.__enter__()
        self.rt = self.rot(2, [128, TT], F32, "rt")

    def rsqrt(self, src_ap, src_bufs, out_ap, out_buf, addc, n=TT):
        nc = self.nc
        t = self.rt.next()
        self.op(self.act, lambda: nc.scalar.activation(out=t.t[:, :n], in_=src_ap, func=AF.Ln, bias=self.cbias.t[:, self.cbias_col[addc]:self.cbias_col[addc] + 1]), r=src_bufs + [self.cbias], w=[t])
        self.op(self.act, lambda: nc.scalar.activation(out=out_ap, in_=t.t[:, :n], func=AF.Exp, scale=-0.5), r=[t], w=[out_buf])

    def end_phase(self):
        self.barrier()
        self.free_dsems.extend(self.phase_dsems)
        self.phase_dsems = []
        self.st.__exit__(None, None, None)
        self.st = None

    def psn(self):
        b = self.psb[self.psi % 8]
        self.psi += 1
        return b

    def psn2(self):
        if self.psi % 2:
            self.psi += 1
        a = self.psb[self.psi % 8]
        b = self.psb[(self.psi + 1) % 8]
        self.psi += 2
        return a, b

    def ew(self):
        self.alt += 1
        return self.dve if self.alt % 2 else self.pool

    def mm(self, out, lhsT, rhs, start=True, stop=True):
        return self.nc.tensor.matmul(out, lhsT=lhsT, rhs=rhs, start=start, stop=stop)

    def declare(self):
        nc, NT, NCH = self.nc, self.NT, self.NCH
        di = lambda n, s: nc.dram_tensor(n, list(s), F32, kind="ExternalInput").ap()
        self.x_in = di("x", [NT, D])
        self.p_in = di("p", [self.depth, NT, PLE])
        self.gains = di("gains", [128, 16 * 8])
        self.subln = di("subln", [128, 2])
        self.lamv = di("lamv", [128, 2 * 4 * 64])
        self.decay = di("decay", [128, 16])
        self.w_qkv = di("w_qkv", [2, D, 3 * D])
        self.w_o = di("w_o", [2, D, D])
        self.w_rin = di("w_rin", [2, D, 6 * D])
        self.w_rout = di("w_rout", [2, 2 * D, D])
        self.w_1 = di("w_1", [4, D, DFF])
        self.w_2 = di("w_2", [4, DFF, D])
        self.w_pp = di("w_pp", [4, PLE, D])
        self.w_pg = di("w_pg", [4, D, D])
        self.c_ident = di("c_ident", [128, 128])
        self.c_ropeA = di("c_ropeA", [2, 128, NT])
        self.c_ropeR = di("c_ropeR", [2, 128, NT])
        self.c_maskb = di("c_maskb", [128, self.NTT * (NCH // 2)])
        self.c_tftb = di("c_tftb", [2, 128, 128])
        self.c_pos = di("c_pos", [2, 128, 128])
        self.c_col = di("c_col", [128, 4])
        self.c_mfb = di("c_mfb", [2, 128, NCH])
        self.y_out = nc.dram_tensor("y", [NT, D], F32, kind="ExternalOutput").ap()
        ds = lambda n, s, dt: nc.dram_tensor(n, list(s), dt, kind="ExternalOutput" if self.debug else "Internal").ap()
        self.xT = ds("s_xT", [self.NTT, 128, 8, TT], F32)
        self.QTs = ds("s_QT", [8, 128, NT], BF16)
        self.KTs = ds("s_KT", [8, 128, NT], BF16)
        self.VS = ds("s_V", [NT, D], BF16)
        self.AT = ds("s_AT", [self.NTT, 128, 16, TT], BF16)
        self.H2s = ds("s_H2", [self.NTT, 128, 8, TT], BF16)
        self.Us = ds("s_U", [self.NTT, 128, 32, TT], BF16)
        self.RK = ds("s_RK", [NT, D], BF16)
        self.RV = ds("s_RV", [NT, 2 * D], BF16)
        self.RG = ds("s_RG", [NT, 2 * D], BF16)
        self.OB = ds("s_OB", [NT, 2 * D], BF16)
        mk = lambda n: [Buf() for _ in range(n)]
        self.k_xT = mk(self.NTT)
        self.k_QT = mk(8)
        self.k_KT = mk(8)
        self.k_V = mk(1)
        self.k_AT = mk(self.NTT)
        self.k_H2 = mk(self.NTT)
        self.k_U = mk(self.NTT)
        self.k_R = mk(NCH)
        self.k_RQ = mk(self.NTT)
        self.k_OB = mk(NCH)

    def build(self):
        nc = self.nc
        self.declare()
        with ExitStack() as gst:
            self.st = gst
            ps = nc.alloc_psum_tensor("ps", [128, 8, 512], F32)
            self.ps = ps
            self.psbf = ps[:, :, :].bitcast(BF16)
            self.psb = [Buf(ps[:, b, :]) for b in range(8)]
            self.psi = 0
            self.ones = self.sb([128, 128], BF16, "ones")
            self.cbias = self.sb([128, 4], F32, "cbias")
            self.cbias_col = {D * 1e-6: 0, 128 * 1e-5: 1, 512 * 1e-6: 2}
            for v, cidx in self.cbias_col.items():
                self.op(self.dve, lambda: nc.vector.memset(self.cbias.t[:, cidx:cidx + 1], float(v)), w=[self.cbias])
            self.onesf = self.sb([128, 128], F32, "onesf")
            self.identf = self.sb([128, 128], F32, "identf")
            self.identb = self.sb([128, 128], BF16, "identb")
            self.gs = self.sb([128, 128], F32, "gs")
            self.sgs = self.sb([128, 2], F32, "sgs")
            self.lamt = self.sb([128, 8], F32, "lamt")
            self.lg = self.sb([128, 16], F32, "lg")
            self.maskb = self.sb([128, self.NTT * (self.NCH // 2)], F32, "maskb")
            lv = self.sb([128, 512], F32, "lv")
            pr = self.sb([128, 64], F32, "pr")
            sraw = self.sb([128, 2], F32, "sraw")
            dv, act = self.dve, self.act
            self.dma(out=self.identf.t[:], in_=self.c_ident, w=[self.identf], sb=self.identf)
            self.dma(out=self.gs.t[:], in_=self.gains, w=[self.gs], sb=self.gs)
            self.dma(out=sraw.t[:], in_=self.subln, w=[sraw], sb=sraw)
            self.dma(out=lv.t[:], in_=self.lamv, w=[lv], sb=lv)
            self.dma(out=self.lg.t[:], in_=self.decay, w=[self.lg], sb=self.lg)
            self.dma(out=self.maskb.t[:], in_=self.c_maskb, w=[self.maskb], sb=self.maskb)
            self.op(dv, lambda: nc.vector.memset(self.ones.t[:], 1.0), w=[self.ones])
            self.op(dv, lambda: nc.vector.memset(self.onesf.t[:], 1.0), w=[self.onesf])
            self.op(dv, lambda: nc.vector.tensor_copy(out=self.identb.t[:], in_=self.identf.t[:]), r=[self.identf], w=[self.identb])
            self.op(dv, lambda: nc.vector.tensor_scalar(out=self.gs.t[:], in0=self.gs.t[:], scalar1=32.0, scalar2=None, op0=ALU.mult), r=[self.gs], w=[self.gs])
            self.op(act, lambda: nc.scalar.activation(out=self.lg.t[:], in_=self.lg.t[:], func=AF.Exp), r=[self.lg], w=[self.lg])
            self.op(dv, lambda: nc.vector.tensor_scalar(out=self.lg.t[:], in0=self.lg.t[:], scalar1=-1.0, scalar2=None, op0=ALU.mult), r=[self.lg], w=[self.lg])
            for j in range(2):
                lam_init = 0.8 - 0.6 * math.exp(-0.3 * (2 * j))
                for e in range(2):
                    a0 = (j * 4 + 2 * e) * 64
                    self.op(dv, lambda: nc.vector.tensor_tensor(out=pr.t[:], in0=lv.t[:, a0:a0 + 64], in1=lv.t[:, a0 + 64:a0 + 128], op=ALU.mult), r=[lv], w=[pr])
                    self.op(dv, lambda: nc.vector.reduce_sum(out=self.lamt.t[:, j * 4 + 2 + e:j * 4 + 3 + e], in_=pr.t[:], axis=AX.X), r=[pr], w=[self.lamt])
                self.op(act, lambda: nc.scalar.activation(out=self.lamt.t[:, j * 4 + 2:j * 4 + 4], in_=self.lamt.t[:, j * 4 + 2:j * 4 + 4], func=AF.Exp), r=[self.lamt], w=[self.lamt])
                self.op(dv, lambda: nc.vector.tensor_tensor(out=self.lamt.t[:, j * 4:j * 4 + 1], in0=self.lamt.t[:, j * 4 + 2:j * 4 + 3], in1=self.lamt.t[:, j * 4 + 3:j * 4 + 4], op=ALU.subtract), r=[self.lamt], w=[self.lamt])
                self.op(dv, lambda: nc.vector.tensor_scalar(out=self.lamt.t[:, j * 4:j * 4 + 1], in0=self.lamt.t[:, j * 4:j * 4 + 1], scalar1=lam_init, scalar2=None, op0=ALU.add), r=[self.lamt], w=[self.lamt])
                self.op(dv, lambda: nc.vector.tensor_scalar(out=self.lamt.t[:, j * 4 + 1:j * 4 + 2], in0=self.lamt.t[:, j * 4:j * 4 + 1], scalar1=-1.0, scalar2=None, op0=ALU.mult), r=[self.lamt], w=[self.lamt])
                self.op(dv, lambda: nc.vector.tensor_scalar(out=self.sgs.t[:, j:j + 1], in0=sraw.t[:, j:j + 1], scalar1=math.sqrt(128.0) * (1.0 - lam_init), scalar2=None, op0=ALU.mult), r=[sraw], w=[self.sgs])
            self.barrier()
            for l in range(self.depth):
                if l % 2 == 0:
                    self.phase_PD_attn(l)
                    self.phase_attn(l)
                else:
                    self.phase_PD_ret(l)
                    self.phase_ret(l)
                self.phase_PA(l)
                self.phase_PB(l)
                self.phase_PC(l)
            self.barrier()
            self.st = None
        return nc

    def load_weight(self, Wd, Wb, stage, sw=2048):
        nc = self.nc
        K, Fd = Wd.shape
        i = 0
        for c in range(K // 128):
            for f0 in range(0, Fd, sw):
                fs = min(sw, Fd - f0)
                s = stage.next()
                self.dma(out=s.t[:, :fs], in_=Wd[c * 128:(c + 1) * 128, f0:f0 + fs], w=[s], sb=s)
                k = i % 3
                i += 1
                if k == 0:
                    self.op(self.dve, lambda: nc.vector.tensor_copy(out=Wb.t[:, c, f0:f0 + fs], in_=s.t[:, :fs]), r=[s], w=[Wb])
                elif k == 1:
                    self.op(self.pool, lambda: nc.gpsimd.tensor_copy(out=Wb.t[:, c, f0:f0 + fs], in_=s.t[:, :fs]), r=[s], w=[Wb])
                else:
                    self.op(self.act, lambda: nc.scalar.activation(out=Wb.t[:, c, f0:f0 + fs], in_=s.t[:, :fs], func=AF.Copy), r=[s], w=[Wb])

    def tsl(self, tt):
        return slice(tt * TT, (tt + 1) * TT)

    def load_xT(self, X, tt):
        self.dma(out=X.t[:], in_=self.xT[tt], r=[self.k_xT[tt]], w=[X], sb=X)

    def store_xT(self, X, tt):
        self.dma(out=self.xT[tt], in_=X.t[:], r=[X], w=[self.k_xT[tt]], sb=X)

    def norm_rstd(self, X, nch, Dn, eps, sq, rstd):
        bank = self.norm_ss(X, nch, sq)
        self.rsqrt(bank.t, [bank], rstd.t[:], rstd, Dn * eps)

    def norm_ss(self, X, nch, sq):
        nc = self.nc
        self.op(self.act, lambda: nc.scalar.activation(out=sq.t[:, :nch, :], in_=X.t[:, :nch, :], func=AF.Square), r=[X], w=[sq])
        bank = self.psn()

        def f():
            for c in range(nch):
                ins = self.mm(bank.t, self.ones.t[:], sq.t[:, c, :], c == 0, c == nch - 1)
            return ins
        self.op(self.pe, f, r=[sq, self.ones], w=[bank])
        return bank

    def scale_norm(self, H, X, gcol, rstd):
        nc = self.nc
        for c in range(8):
            E = self.dve
            sc = 32.0 if gcol is None else self.gs.t[:, gcol + c:gcol + c + 1]
            rr = [X, rstd] + ([] if gcol is None else [self.gs])
            self.op(E, lambda: E.h.scalar_tensor_tensor(out=H.t[:, c, :], in0=X.t[:, c, :], scalar=sc, in1=rstd.t[:], op0=ALU.mult, op1=ALU.mult), r=rr, w=[H])

    def resid_norm(self, X, M, gcol, rstd, Mr):
        nc = self.nc
        self.op(self.dve, lambda: nc.vector.tensor_tensor(out=Mr.t[:], in0=M.t[:], in1=rstd.t[:].unsqueeze(1).to_broadcast([128, 8, TT]), op=ALU.mult), r=[M, rstd], w=[Mr])
        for c in range(8):
            E = self.dve
            self.op(E, lambda: E.h.scalar_tensor_tensor(out=X.t[:, c, :], in0=Mr.t[:, c, :], scalar=self.gs.t[:, gcol + c:gcol + c + 1], in1=X.t[:, c, :], op0=ALU.mult, op1=ALU.add), r=[Mr, X, self.gs], w=[X])

    def proj_fm(self, W, nk, A, M, col0=0):
        for dc in range(8):
            self.proj_piece(W, nk, A, M, dc, col0)

    def proj_piece(self, W, nk, A, M, dc, col0=0):
        nc = self.nc
        bank = self.psn()

        def f():
            for k in range(nk):
                ins = self.mm(bank.t, W.t[:, k, col0 + dc * 128:col0 + (dc + 1) * 128], A.t[:, k, :], k == 0, k == nk - 1)
            return ins
        self.op(self.pe, f, r=[W, A], w=[bank])
        self.op(self.act, lambda: nc.scalar.activation(out=M.t[:, dc, :], in_=bank.t, func=AF.Copy), r=[bank], w=[M])

    def interleave(self, pieces, steps):
        n = max(len(pieces), len(steps))
        for k in range(n):
            if k < len(pieces):
                pieces[k]()
            if k < len(steps):
                steps[k]()

    def x_from_input(self, X, tt, xin):
        nc = self.nc
        self.dma(out=xin.t[:], in_=self.x_in[self.tsl(tt), :].rearrange("(s p) f -> p s f", p=128), w=[xin], sb=xin)
        for c in range(8):
            bank = self.psn()

            def f():
                for s in range(4):
                    ins = nc.tensor.transpose(bank.t[:, s * 128:(s + 1) * 128], xin.t[:, s, c * 128:(c + 1) * 128], self.identf.t[:])
                return ins
            self.op(self.pe, f, r=[xin, self.identf], w=[bank])
            if c % 2:
                self.op(self.act, lambda: nc.scalar.activation(out=X.t[:, c, :], in_=bank.t, func=AF.Copy), r=[bank], w=[X])
            else:
                self.op(self.dve, lambda: nc.vector.tensor_copy(out=X.t[:, c, :], in_=bank.t), r=[bank], w=[X])

    def rope_pair(self, bankA, bankB, cs, sn, Ap, Bp, tmp, scale):
        nc = self.nc
        Af, Bf, t1, t2, t3, t4 = [tmp.next() for _ in range(6)]
        self.op(self.act, lambda: nc.scalar.activation(out=Af.t[:], in_=bankA.t, func=AF.Copy, scale=scale), r=[bankA], w=[Af])
        self.op(self.act, lambda: nc.scalar.activation(out=Bf.t[:], in_=bankB.t, func=AF.Copy, scale=scale), r=[bankB], w=[Bf])
        self.op(self.dve, lambda: nc.vector.tensor_tensor(out=t1.t[:], in0=Af.t[:], in1=cs.t[:], op=ALU.mult), r=[Af, cs], w=[t1])
        self.op(self.dve, lambda: nc.vector.tensor_tensor(out=t2.t[:], in0=Bf.t[:], in1=sn.t[:], op=ALU.mult), r=[Bf, sn], w=[t2])
        self.op(self.pool, lambda: nc.gpsimd.tensor_tensor(out=t3.t[:], in0=Bf.t[:], in1=cs.t[:], op=ALU.mult), r=[Bf, cs], w=[t3])
        self.op(self.pool, lambda: nc.gpsimd.tensor_tensor(out=t4.t[:], in0=Af.t[:], in1=sn.t[:], op=ALU.mult), r=[Af, sn], w=[t4])
        self.op(self.dve, lambda: nc.vector.tensor_tensor(out=Ap, in0=t1.t[:], in1=t2.t[:], op=ALU.subtract), r=[t1, t2], w=[self._apbuf])
        self.op(self.pool, lambda: nc.gpsimd.tensor_tensor(out=Bp, in0=t3.t[:], in1=t4.t[:], op=ALU.add), r=[t3, t4], w=[self._bpbuf])

    def phase_PD_attn(self, l):
        nc = self.nc
        j = l // 2
        self.begin_phase()
        W = self.sb([128, 8, 3 * D], BF16, "W")
        stage = self.rot(2, [128, 2048], F32, "stg")
        self.load_weight(self.w_qkv[j], W, stage)
        Xp = self.rot(2, [128, 8, TT], F32, "X")
        xin = self.rot(1, [128, 4, D], F32, "xin") if l == 0 else None
        sq = self.sb([128, 8, TT], BF16, "sq")
        rs = self.rot(2, [128, TT], F32, "rs")
        Hp = self.rot(2, [128, 8, TT], BF16, "H")
        csp = self.rot(2, [128, TT], F32, "cs")
        snp = self.rot(2, [128, TT], F32, "sn")
        tmp = self.rot(12, [128, TT], F32, "tmp")
        ABp = self.rot(4, [128, 2, TT], BF16, "AB")
        Vp = self.rot(2, [128, 4, D], BF16, "Vt")
        def S1(tt):
            ts = self.tsl(tt)
            X = Xp.next()
            if l == 0:
                self.x_from_input(X, tt, xin.next())
                self.store_xT(X, tt)
            else:
                self.load_xT(X, tt)
            r = rs.next()
            self.norm_rstd(X, 8, D, 1e-6, sq, r)
            H = Hp.next()
            self.scale_norm(H, X, (l * 4 + 0) * 8, r)
            return (H,)

        cst = {}

        def S2(tt, H, part):
            ts = self.tsl(tt)
            if part == 0:
                cs, sn = csp.next(), snp.next()
                self.dma(out=cs.t[:], in_=self.c_ropeA[0, :, ts], w=[cs], sb=cs)
                self.dma(out=sn.t[:], in_=self.c_ropeA[1, :, ts], w=[sn], sb=sn)
                cst["cs"], cst["sn"] = cs, sn
            cs, sn = cst["cs"], cst["sn"]
            for which in ((0,) if part == 0 else (1,)):
                dst = self.QTs if which == 0 else self.KTs
                ktok = self.k_QT if which == 0 else self.k_KT
                for jj in range(4):
                    bA, bB = self.psn(), self.psn()
                    for bank, ch in ((bA, 2 * jj), (bB, 2 * jj + 1)):
                        c0 = which * D + ch * 128

                        def f():
                            for k in range(8):
                                ins = self.mm(bank.t, W.t[:, k, c0:c0 + 128], H.t[:, k, :], k == 0, k == 7)
                            return ins
                        self.op(self.pe, f, r=[W, H], w=[bank])
                    AB = ABp.next()
                    self._apbuf = AB
                    self._bpbuf = AB
                    self.rope_pair(bA, bB, cs, sn, AB.t[:, 0, :], AB.t[:, 1, :], tmp, 1.0)
                    for rr in range(4):
                        m = 4 * jj + rr
                        hd, cc = m // 2, m % 2
                        for ab in range(2):
                            self.dma(out=dst[hd, cc * 64 + ab * 32:cc * 64 + ab * 32 + 32, ts], in_=AB.t[32 * rr:32 * rr + 32, ab, :], r=[AB], w=[ktok[hd]], sb=AB)
            if part == 0:
                return
            Vt = Vp.next()
            for s in range(4):
                for hf in range(2):
                    bank = self.psn()
                    c0 = 2 * D + hf * 512

                    def f():
                        for k in range(8):
                            ins = self.mm(bank.t, H.t[:, k, s * 128:(s + 1) * 128], W.t[:, k, c0:c0 + 512], k == 0, k == 7)
                        return ins
                    self.op(self.pe, f, r=[W, H], w=[bank])
                    if hf:
                        self.op(self.act, lambda: nc.scalar.activation(out=Vt.t[:, s, hf * 512:(hf + 1) * 512], in_=bank.t, func=AF.Copy), r=[bank], w=[Vt])
                    else:
                        self.op(self.dve, lambda: nc.vector.tensor_copy(out=Vt.t[:, s, hf * 512:(hf + 1) * 512], in_=bank.t), r=[bank], w=[Vt])
            self.dma(out=self.VS[ts, :].rearrange("(s p) f -> p s f", p=128), in_=Vt.t[:], r=[Vt], w=[self.k_V[0]], sb=Vt)

        cur = S1(0)
        for tt in range(self.NTT):
            S2(tt, cur[0], 0)
            nxt = S1(tt + 1) if tt + 1 < self.NTT else None
            S2(tt, cur[0], 1)
            cur = nxt
        self.end_phase()

    def phase_attn(self, l):
        nc = self.nc
        j = l // 2
        NT, NTT, NCH = self.NT, self.NTT, self.NCH
        NKP = NCH // 2
        dv = self.dve
        self.begin_phase()
        KTp = self.rot(2, [128, 2, NT], BF16, "KT")
        for kb_ in KTp.items:
            self.op(self.pool, lambda: nc.gpsimd.memset(kb_.t[:], 0.0), w=[kb_])
        QTp = self.rot(2, [128, NT], BF16, "QT")
        Vp = self.rot(2, [128, NCH, 128], BF16, "Vh")
        Pp = self.rot(6, [128, 2, TT], BF16, "P")
        Ocp = self.rot(3, [128, TT], F32, "Oc")
        zsp = self.rot(2, [128, TT], F32, "zs")
        rzp = self.rot(3, [128, TT], F32, "rz")
        t0p = self.rot(2, [128, TT], F32, "t0")
        t1p = self.rot(2, [128, TT], F32, "t1")
        ddp = self.rot(2, [128, TT], F32, "dd")
        sqp = self.rot(2, [128, TT], BF16, "sqd")
        rrp = self.rot(2, [128, TT], F32, "rr")
        yp = self.rot(2, [128, TT], BF16, "yt")
        ZEp = [self.sb([128, 2, TT], F32, "ZE") for _ in range(2)]
        ZOp = [[self.sb([128, TT], F32, "ZO") for _ in range(2)] for _ in range(2)]
        slots = [(self.psb[2 * k], self.psb[2 * k + 1], self.ps[:, 2 * k:2 * k + 2, :]) for k in range(3)]
        O, Z = self.psb[6], self.psb[7]
        heads = {}
        slot_of = {}
        sctr = [0]

        def next_slot():
            sl = slots[sctr[0] % 3]
            sctr[0] += 1
            return sl

        def load_head(h):
            KT, QT, V = KTp.next(), QTp.next(), Vp.next()
            self.dma(out=KT.t[0:64, 0, :], in_=self.KTs[h, 0:64, :], r=[self.k_KT[h]], w=[KT], sb=KT)
            self.dma(out=KT.t[64:128, 1, :], in_=self.KTs[h, 64:128, :], r=[self.k_KT[h]], w=[KT], sb=KT)
            self.dma(out=QT.t[:], in_=self.QTs[h], r=[self.k_QT[h]], w=[QT], sb=QT)
            self.dma(out=V.t[:], in_=self.VS[:, h * 128:(h + 1) * 128].rearrange("(kb p) d -> p kb d", p=128), r=[self.k_V[0]], w=[V], sb=V)
            heads[h] = (KT, QT, V)

        items = [(h, qt, c, kp) for h in range(8) for qt in range(NTT) for c in range(2) for kp in range(NKP)]
        state = {"t0": None}
        pending = []
        big = NKP >= 16
        KZ = 12 if big else 5
        D_FZ, D_RZ, D_B, D_C = (2, 4, 5, 9) if big else (1, 1, 1, 2)

        def emit_qk(i):
            h, qt, c, kp = items[i]
            if qt == 0 and c == 0 and kp == 0 and h == 0:
                load_head(0)
            if qt == 0 and c == 0 and kp == 3 and h + 1 < 8:
                load_head(h + 1)
            KT, QT, V = heads[h]
            b0, b1, sap = slot_of[i] = next_slot()

            def f():
                for jx in range(2):
                    kb = 2 * kp + jx
                    ins = self.mm(sap[:, jx, :], KT.t[:, c, kb * 128:(kb + 1) * 128], QT.t[:, qt * TT:(qt + 1) * TT], True, True)
                return ins
            self.op(self.pe, f, r=[KT, QT], w=[b0, b1])

        def emit_exp(i):
            h, qt, c, kp = items[i]
            b0, b1, sap = slot_of.pop(i)
            P = Pp.next()
            col = qt * NKP + kp
            self.op(self.act, lambda: nc.scalar.activation(out=P.t[:], in_=sap, func=AF.Exp, bias=self.maskb.t[:, col:col + 1], scale=0.125), r=[b0, b1, self.maskb], w=[P])
            return P

        def emit_av(i, P):
            h, qt, c, kp = items[i]
            KT, QT, V = heads[h]
            g = (h * NTT + qt) * 2 + c
            ZE, ZO = ZEp[g % 2], ZOp[g % 2]
            peZ = (kp >= KZ)
            if kp == 0:
                self.op(dv, lambda: nc.vector.tensor_copy(out=ZE.t[:], in_=P.t[:]), r=[P], w=[ZE])
            elif not peZ:
                self.op(dv, lambda: nc.vector.tensor_tensor(out=ZE.t[:], in0=ZE.t[:], in1=P.t[:], op=ALU.add), r=[P, ZE], w=[ZE])
            else:
                ZOk = ZO[kp % 2]
                if kp < KZ + 2:
                    self.op(dv, lambda: nc.vector.tensor_copy(out=ZOk.t[:], in_=P.t[:, 0, :]), r=[P], w=[ZOk])
                else:
                    self.op(dv, lambda: nc.vector.tensor_tensor(out=ZOk.t[:], in0=ZOk.t[:], in1=P.t[:, 0, :], op=ALU.add), r=[P, ZOk], w=[ZOk])

            def f():
                for jx in range(2):
                    kb = 2 * kp + jx
                    first = (kp == 0 and jx == 0)
                    last = (kp == NKP - 1 and jx == 1)
                    ins = self.mm(O.t, V.t[:, kb, :], P.t[:, jx, :], first, last)
                if peZ:
                    ins = self.mm(Z.t, self.ones.t[:], P.t[:, 1, :], kp == KZ, False)
                return ins
            self.op(self.pe, f, r=[V, P, self.ones], w=[O, Z] if peZ else [O])
            if kp != NKP - 1:
                return
            Oc = Ocp.next()
            self.op(dv, lambda: nc.vector.tensor_copy(out=Oc.t[:], in_=O.t), r=[O], w=[Oc])
            zs = zsp.next()
            self.op(dv, lambda: nc.vector.tensor_tensor(out=zs.t[:], in0=ZE.t[:, 0, :], in1=ZE.t[:, 1, :], op=ALU.add), r=[ZE], w=[zs])
            for kk in (KZ, KZ + 1):
                if NKP > kk:
                    ZOk = ZO[kk % 2]
                    self.op(dv, lambda: nc.vector.tensor_tensor(out=zs.t[:], in0=zs.t[:], in1=ZOk.t[:], op=ALU.add), r=[zs, ZOk], w=[zs])
            rz = rzp.next()

            def stageFZ():
                self.op(self.pe, lambda: self.mm(Z.t, self.onesf.t[:], zs.t[:], NKP <= KZ, True), r=[zs, self.onesf], w=[Z])

            def stageRZ():
                self.op(dv, lambda: nc.vector.reciprocal(out=rz.t[:], in_=Z.t), r=[Z], w=[rz])
            pending.append((i + D_FZ, stageFZ))
            pending.append((i + D_RZ, stageRZ))

            def stageB():
                if c == 0:
                    t0 = t0p.next()
                    self.op(dv, lambda: nc.vector.tensor_tensor(out=t0.t[:], in0=Oc.t[:], in1=rz.t[:], op=ALU.mult), r=[Oc, rz], w=[t0])
                    state["t0"] = t0
                    return
                t0 = state["t0"]
                t1 = t1p.next()
                self.op(dv, lambda: nc.vector.tensor_tensor(out=t1.t[:], in0=Oc.t[:], in1=rz.t[:], op=ALU.mult), r=[Oc, rz], w=[t1])
                dd = ddp.next()
                self.op(dv, lambda: nc.vector.scalar_tensor_tensor(out=dd.t[:], in0=t1.t[:], scalar=self.lamt.t[:, j * 4 + 1:j * 4 + 2], in1=t0.t[:], op0=ALU.mult, op1=ALU.add), r=[t1, t0, self.lamt], w=[dd])
                sqd = sqp.next()
                self.op(dv, lambda: nc.vector.tensor_tensor(out=sqd.t[:], in0=dd.t[:], in1=dd.t[:], op=ALU.mult), r=[dd], w=[sqd])
                self.op(self.pe, lambda: self.mm(Z.t, self.ones.t[:], sqd.t[:], True, True), r=[sqd, self.ones], w=[Z])

                def stageC():
                    rr = rrp.next()
                    self.rsqrt(Z.t, [Z], rr.t[:], rr, 128 * 1e-5)
                    yt = yp.next()
                    self.op(dv, lambda: nc.vector.scalar_tensor_tensor(out=yt.t[:], in0=dd.t[:], scalar=self.sgs.t[:, j:j + 1], in1=rr.t[:], op0=ALU.mult, op1=ALU.mult), r=[dd, rr, self.sgs], w=[yt])
                    self.dma(out=self.AT[qt, :, h, :], in_=yt.t[:], r=[yt], w=[self.k_AT[qt]], sb=yt)
                pending.append((i + D_C, stageC))
            pending.append((i + D_B, stageB))

        def run_pending(upto):
            pending.sort(key=lambda e: e[0])
            while pending and (upto is None or pending[0][0] <= upto):
                pending.pop(0)[1]()
                pending.sort(key=lambda e: e[0])

        n = len(items)
        emit_qk(0)
        emit_qk(1)
        for i in range(n):
            if i + 2 < n:
                emit_qk(i + 2)
            P = emit_exp(i)
            emit_av(i, P)
            run_pending(i)
        run_pending(None)
        self.end_phase()

    def phase_PD_ret(self, l):
        nc = self.nc
        j = l // 2
        self.begin_phase()
        W = self.sb([128, 8, 6 * D], BF16, "W")
        stage = self.rot(2, [128, 1024], F32, "stg")
        self.load_weight(self.w_rin[j], W, stage, 1024)
        Xp = self.rot(1, [128, 8, TT], F32, "X")
        sq = self.sb([128, 8, TT], BF16, "sq")
        rs = self.rot(2, [128, TT], F32, "rs")
        Hp = self.rot(1, [128, 8, TT], BF16, "H")
        csp = self.rot(1, [128, TT], F32, "cs")
        snp = self.rot(1, [128, TT], F32, "sn")
        tmp = self.rot(6, [128, TT], F32, "tmp")
        QKp = self.rot(2, [128, 8, TT], BF16, "QK")
        Ktp = self.rot(1, [128, 4, D], BF16, "Ktm")
        Vp = self.rot(1, [128, 4, 2 * D], BF16, "Vt")
        for tt in range(self.NTT):
            ts = self.tsl(tt)
            X = Xp.next()
            self.load_xT(X, tt)
            r = rs.next()
            self.norm_rstd(X, 8, D, 1e-6, sq, r)
            H = Hp.next()
            self.scale_norm(H, X, (l * 4 + 0) * 8, r)
            cs, sn = csp.next(), snp.next()
            self.dma(out=cs.t[:], in_=self.c_ropeR[0, :, ts], w=[cs], sb=cs)
            self.dma(out=sn.t[:], in_=self.c_ropeR[1, :, ts], w=[sn], sb=sn)
            KR = None
            for which in range(2):
                dst = self.QTs if which == 0 else self.KTs
                QK = QKp.next()
                for hh in range(4):
                    bA, bB = self.psn(), self.psn()
                    for bank, ch in ((bA, 2 * hh), (bB, 2 * hh + 1)):
                        c0 = which * D + ch * 128

                        def f():
                            for k in range(8):
                                ins = self.mm(bank.t, W.t[:, k, c0:c0 + 128], H.t[:, k, :], k == 0, k == 7)
                            return ins
                        self.op(self.pe, f, r=[W, H], w=[bank])
                    self._apbuf = QK
                    self._bpbuf = QK
                    self.rope_pair(bA, bB, cs, sn, QK.t[:, 2 * hh, :], QK.t[:, 2 * hh + 1, :], tmp, 1.0 if which == 0 else 1.0 / 16.0)
                self.dma(out=dst[:, :, ts].rearrange("c p t -> p c t"), in_=QK.t[:], r=[QK], w=[self.k_RQ[tt]], sb=QK)
                if which == 1:
                    KR = QK
            Ktm = Ktp.next()
            for s in range(4):
                b0, b1 = self.psn2()
                bi = self.psb.index(b0)

                def f():
                    for ch in range(8):
                        ins = nc.tensor.transpose(self.psbf[:, bi, ch * 128:(ch + 1) * 128], KR.t[:, ch, s * 128:(s + 1) * 128], self.identb.t[:])
                    return ins
                self.op(self.pe, f, r=[KR, self.identb], w=[b0])
                self.op(self.dve, lambda: nc.vector.tensor_copy(out=Ktm.t[:, s, :], in_=self.psbf[:, bi, :]), r=[b0], w=[Ktm])
            self.dma(out=self.RK[ts, :].rearrange("(s p) f -> p s f", p=128), in_=Ktm.t[:], r=[Ktm], w=[self.k_R[tt * 4 + q] for q in range(4)], sb=Ktm)
            for which, dst in ((2, self.RV), (3, self.RG)):
                Vt = Vp.next()
                for s in range(4):
                    for hf in range(4):
                        bank = self.psn()
                        c0 = (2 * D if which == 2 else 4 * D) + hf * 512

                        def f():
                            for k in range(8):
                                ins = self.mm(bank.t, H.t[:, k, s * 128:(s + 1) * 128], W.t[:, k, c0:c0 + 512], k == 0, k == 7)
                            return ins
                        self.op(self.pe, f, r=[W, H], w=[bank])
                        if which == 3:
                            self.op(self.act, lambda: nc.scalar.activation(out=Vt.t[:, s, hf * 512:(hf + 1) * 512], in_=bank.t, func=AF.Silu), r=[bank], w=[Vt])
                        elif hf % 2:
                            self.op(self.act, lambda: nc.scalar.activation(out=Vt.t[:, s, hf * 512:(hf + 1) * 512], in_=bank.t, func=AF.Copy), r=[bank], w=[Vt])
                        else:
                            self.op(self.dve, lambda: nc.vector.tensor_copy(out=Vt.t[:, s, hf * 512:(hf + 1) * 512], in_=bank.t), r=[bank], w=[Vt])
                self.dma(out=dst[ts, :].rearrange("(s p) f -> p s f", p=128), in_=Vt.t[:], r=[Vt], w=[self.k_R[tt * 4 + q] for q in range(4)], sb=Vt)
        self.end_phase()

    def phase_ret(self, l):
        nc = self.nc
        j = l // 2
        NCH = self.NCH
        dv, act, pool, pe = self.dve, self.act, self.pool, self.pe
        self.begin_phase()
        TF = self.sb([128, 128], F32, "TF")
        TB = self.sb([128, 128], F32, "TB")
        P1 = self.sb([128, 128], F32, "P1")
        P2 = self.sb([128, 128], F32, "P2")
        CC = self.sb([128, 4], F32, "CC")
        MF = self.sb([128, NCH], F32, "MF")
        MB = self.sb([128, NCH], F32, "MB")
        for bfr, src in ((TF, self.c_tftb[0]), (TB, self.c_tftb[1]), (P1, self.c_pos[0]), (P2, self.c_pos[1]), (CC, self.c_col), (MF, self.c_mfb[0]), (MB, self.c_mfb[1])):
            self.dma(out=bfr.t[:], in_=src, w=[bfr], sb=bfr)
        DT = self.sb([128, 4, 128], F32, "DT")
        e1 = self.sb([128, 128], F32, "e1")
        e2 = self.sb([128, 128], F32, "e2")
        XIF = self.sb([128, 8, 128], BF16, "XIF")
        XIB = self.sb([128, 8, 128], BF16, "XIB")
        ZF = self.sb([128, 4, NCH], F32, "ZF")
        ZB = self.sb([128, 4, NCH], F32, "ZB")
        CDF = self.sb([128, 4, NCH], F32, "CDF")
        CDB = self.sb([128, 4, NCH], F32, "CDB")
        sc = self.sb([128, 16], F32, "sc")
        for h in range(4):
            lf = self.lg.t[:, j * 8 + h:j * 8 + h + 1]
            lb = self.lg.t[:, j * 8 + 4 + h:j * 8 + 4 + h + 1]
            self.op(act, lambda: nc.scalar.activation(out=e1.t[:], in_=TF.t[:], func=AF.Exp, scale=lf), r=[TF, self.lg], w=[e1])
            self.op(act, lambda: nc.scalar.activation(out=e2.t[:], in_=TB.t[:], func=AF.Exp, scale=lb), r=[TB, self.lg], w=[e2])
            self.op(dv, lambda: nc.vector.tensor_tensor(out=DT.t[:, h, :], in0=e1.t[:], in1=e2.t[:], op=ALU.add), r=[e1, e2], w=[DT])
            for q in range(2):
                self.op(act, lambda: nc.scalar.activation(out=XIF.t[:, 2 * h + q, :], in_=P1.t[:], func=AF.Exp, scale=lf), r=[P1, self.lg], w=[XIF])
                self.op(act, lambda: nc.scalar.activation(out=XIB.t[:, 2 * h + q, :], in_=P2.t[:], func=AF.Exp, scale=lb), r=[P2, self.lg], w=[XIB])
            self.op(act, lambda: nc.scalar.activation(out=sc.t[:, 4 * h + 0:4 * h + 1], in_=CC.t[:, 0:1], func=AF.Exp, scale=lf), r=[CC, self.lg], w=[sc])
            self.op(act, lambda: nc.scalar.activation(out=sc.t[:, 4 * h + 1:4 * h + 2], in_=CC.t[:, 2:3], func=AF.Exp, scale=lf), r=[CC, self.lg], w=[sc])
            self.op(act, lambda: nc.scalar.activation(out=sc.t[:, 4 * h + 2:4 * h + 3], in_=CC.t[:, 1:2], func=AF.Exp, scale=lb), r=[CC, self.lg], w=[sc])
            self.op(act, lambda: nc.scalar.activation(out=sc.t[:, 4 * h + 3:4 * h + 4], in_=CC.t[:, 2:3], func=AF.Exp, scale=lb), r=[CC, self.lg], w=[sc])
            for k, (dst, msk) in enumerate(((ZF, MF), (CDF, MF), (ZB, MB), (CDB, MB))):
                self.op(dv, lambda: nc.vector.tensor_scalar(out=dst.t[:, h, :], in0=msk.t[:], scalar1=sc.t[:, 4 * h + k:4 * h + k + 1], scalar2=None, op0=ALU.mult), r=[msk, sc], w=[dst])

        Rf = [self.sb([128, 2, 512], F32, "Rf") for _ in range(4)]
        Rb = [self.sb([128, 2, 512], BF16, "Rb") for _ in range(4)]
        QGp = self.rot(2, [128, 8, TT], BF16, "QG")
        KGp = self.rot(2, [128, 8, TT], BF16, "KG")
        Ktp = self.rot(3, [128, D], BF16, "Ktm")
        Vcp = self.rot(3, [128, 2 * D], BF16, "Vc")
        SGp = self.rot(2, [128, 2 * D], BF16, "SG")
        OBp = self.rot(3, [128, 2 * D], BF16, "OBt")
        Qxp = self.rot(2, [128, 8, 128], BF16, "Qx")
        Kzp = self.rot(2, [128, D], BF16, "Kz")
        Smp = self.rot(4, [128, 128], BF16, "Sm")
        Ofp = self.rot(2, [128, 4, 512], F32, "Of")
        junk = self.sb([128, 512], BF16, "junk")
        ssp = self.rot(2, [128, 4], F32, "ss")
        r1p = self.rot(2, [128, 4], F32, "r1")
        Yp = self.rot(2, [128, 2 * D], BF16, "Y")
        YTp = self.rot(2, [128, 16, 128], BF16, "YT")
        B = self.psb

        def make_kz(c, Ktm, Ztab):
            Kz = Kzp.next()
            for h in range(4):
                self.op(act, lambda: nc.scalar.activation(out=Kz.t[:, h * 256:(h + 1) * 256], in_=Ktm.t[:, h * 256:(h + 1) * 256], func=AF.Copy, scale=Ztab.t[:, h, c:c + 1]), r=[Ktm, Ztab], w=[Kz])
            return Kz

        def update_state(c, Kz, Vc, CDtab):
            for h in range(4):
                bi = 2 * (h % 2)
                b0, b1 = B[bi], B[bi + 1]

                def f():
                    for dk in range(2):
                        ins = self.mm(self.ps[:, bi + dk, :], Kz.t[:, h * 256 + dk * 128:h * 256 + (dk + 1) * 128], Vc.t[:, h * 512:(h + 1) * 512], True, True)
                    return ins
                self.op(pe, f, r=[Kz, Vc], w=[b0, b1])
                self.op(dv, lambda: nc.vector.scalar_tensor_tensor(out=Rf[h].t[:], in0=Rf[h].t[:], scalar=CDtab.t[:, h, c:c + 1], in1=self.ps[:, bi:bi + 2, :], op0=ALU.mult, op1=ALU.add), r=[Rf[h], b0, b1, CDtab], w=[Rf[h]])
                self.op(act, lambda: nc.scalar.activation(out=Rb[h].t[:], in_=Rf[h].t[:], func=AF.Copy), r=[Rf[h]], w=[Rb[h]])

        for sweep in (0, 1):
            for h in range(4):
                self.op(dv, lambda: nc.vector.memset(Rf[h].t[:], 0.0), w=[Rf[h]])
                self.op(dv, lambda: nc.vector.memset(Rb[h].t[:], 0.0), w=[Rb[h]])
            order = list(range(NCH - 1, -1, -1)) if sweep == 0 else list(range(NCH))
            curg, QG, KG = -1, None, None
            for c in order:
                g, off = c // 4, (c % 4) * 128
                csl = slice(c * 128, (c + 1) * 128)
                if g != curg:
                    curg = g
                    QG = QGp.next()
                    self.dma(out=QG.t[:], in_=self.QTs[:, :, self.tsl(g)].rearrange("c p t -> p c t"), r=[self.k_RQ[g]], w=[QG], sb=QG)
                    if sweep == 1:
                        KG = KGp.next()
                        self.dma(out=KG.t[:], in_=self.KTs[:, :, self.tsl(g)].rearrange("c p t -> p c t"), r=[self.k_RQ[g]], w=[KG], sb=KG)
                Ktm, Vc = Ktp.next(), Vcp.next()
                self.dma(out=Ktm.t[:], in_=self.RK[csl, :], r=[self.k_R[c]], w=[Ktm], sb=Ktm)
                self.dma(out=Vc.t[:], in_=self.RV[csl, :], r=[self.k_R[c]], w=[Vc], sb=Vc)
                if sweep == 1:
                    SG, OBt = SGp.next(), OBp.next()
                    self.dma(out=SG.t[:], in_=self.RG[csl, :], r=[self.k_R[c]], w=[SG], sb=SG)
                    self.dma(out=OBt.t[:], in_=self.OB[csl, :], r=[self.k_OB[c]], w=[OBt], sb=OBt)
                Qx = Qxp.next()
                XI = XIB if sweep == 0 else XIF
                self.op(dv, lambda: nc.vector.tensor_tensor(out=Qx.t[:], in0=QG.t[:, :, off:off + 128], in1=XI.t[:], op=ALU.mult), r=[QG, XI], w=[Qx])
                Kz = make_kz(c, Ktm, ZB if sweep == 0 else ZF)
                if sweep == 0:
                    OBt = OBp.next()
                    for h in range(4):
                        bank = B[4 + h]

                        def f():
                            for dk in range(2):
                                ins = self.mm(bank.t, Qx.t[:, 2 * h + dk, :], Rb[h].t[:, dk, :], dk == 0, dk == 1)
                            return ins
                        self.op(pe, f, r=[Qx, Rb[h]], w=[bank])
                    update_state(c, Kz, Vc, CDB)
                    for h in range(4):
                        bank = B[4 + h]
                        if h % 2:
                            self.op(act, lambda: nc.scalar.activation(out=OBt.t[:, h * 512:(h + 1) * 512], in_=bank.t, func=AF.Copy), r=[bank], w=[OBt])
                        else:
                            self.op(dv, lambda: nc.vector.tensor_copy(out=OBt.t[:, h * 512:(h + 1) * 512], in_=bank.t), r=[bank], w=[OBt])
                    self.dma(out=self.OB[csl, :], in_=OBt.t[:], r=[OBt], w=[self.k_OB[c]], sb=OBt)
                    continue
                Sms = []
                for h in range(4):
                    bS = B[h]

                    def f():
                        for dk in range(2):
                            ins = self.mm(bS.t[:, 0:128], KG.t[:, 2 * h + dk, off:off + 128], QG.t[:, 2 * h + dk, off:off + 128], dk == 0, dk == 1)
                        return ins
                    self.op(pe, f, r=[KG, QG], w=[bS])
                for h in range(4):
                    Sm = Smp.next()
                    self.op(dv, lambda: nc.vector.tensor_tensor(out=Sm.t[:], in0=B[h].t[:, 0:128], in1=DT.t[:, h, :], op=ALU.mult), r=[B[h], DT], w=[Sm])
                    Sms.append(Sm)
                for h in range(4):
                    bO, Sm = B[4 + h], Sms[h]

                    def f2():
                        self.mm(bO.t, Sm.t[:], Vc.t[:, h * 512:(h + 1) * 512], True, False)
                        for dk in range(2):
                            ins = self.mm(bO.t, Qx.t[:, 2 * h + dk, :], Rb[h].t[:, dk, :], False, dk == 1)
                        return ins
                    self.op(pe, f2, r=[Sm, Vc, Qx, Rb[h]], w=[bO])
                update_state(c, Kz, Vc, CDF)
                Of, ss = Ofp.next(), ssp.next()
                for h in range(4):
                    bO = B[4 + h]
                    self.op(dv, lambda: nc.vector.tensor_tensor(out=Of.t[:, h, :], in0=bO.t, in1=OBt.t[:, h * 512:(h + 1) * 512], op=ALU.add), r=[bO, OBt], w=[Of])
                    self.op(act, lambda: nc.scalar.activation(out=junk.t[:], in_=Of.t[:, h, :], func=AF.Square, accum_out=ss.t[:, h:h + 1]), r=[Of], w=[junk, ss])
                r1 = r1p.next()
                self.rsqrt(ss.t[:], [ss], r1.t[:], r1, 512 * 1e-6, 4)
                self.op(dv, lambda: nc.vector.tensor_scalar(out=r1.t[:], in0=r1.t[:], scalar1=math.sqrt(512.0), scalar2=None, op0=ALU.mult), r=[r1], w=[r1])
                Y = Yp.next()
                for h in range(4):
                    self.op(dv, lambda: nc.vector.scalar_tensor_tensor(out=Y.t[:, h * 512:(h + 1) * 512], in0=Of.t[:, h, :], scalar=r1.t[:, h:h + 1], in1=SG.t[:, h * 512:(h + 1) * 512], op0=ALU.mult, op1=ALU.mult), r=[Of, r1, SG], w=[Y])
                YT = YTp.next()
                b0, b1 = B[0], B[1]

                def f3():
                    for q in range(16):
                        ins = nc.tensor.transpose(self.psbf[:, q // 8, (q % 8) * 128:(q % 8 + 1) * 128], Y.t[:, q * 128:(q + 1) * 128], self.identb.t[:])
                    return ins
                self.op(pe, f3, r=[Y, self.identb], w=[b0, b1])
                self.op(act, lambda: nc.scalar.activation(out=YT.t[:, 0:8, :], in_=self.psbf[:, 0, :], func=AF.Copy), r=[b0], w=[YT])
                self.op(dv, lambda: nc.vector.tensor_copy(out=YT.t[:, 8:16, :], in_=self.psbf[:, 1, :]), r=[b1], w=[YT])
                self.dma(out=self.AT[g, :, :, off:off + 128], in_=YT.t[:], r=[YT], w=[self.k_AT[g]], sb=YT)
        self.end_phase()

    def phase_PA(self, l):
        nc = self.nc
        j = l // 2
        attn = (l % 2 == 0)
        nin = 8 if attn else 16
        self.begin_phase()
        W = self.sb([128, nin, D], BF16, "W")
        stage = self.rot(2, [128, 2048], F32, "stg")
        self.load_weight(self.w_o[j] if attn else self.w_rout[j], W, stage)
        Ap = self.rot(2, [128, nin, TT], BF16, "A")
        Xp = self.rot(2, [128, 8, TT], F32, "X")
        Mp = self.rot(2, [128, 8, TT], F32, "M")
        sq = self.sb([128, 8, TT], BF16, "sq")
        rs = self.rot(2, [128, TT], F32, "rs")
        Hp = self.rot(2, [128, 8, TT], BF16, "H")

        def S1(tt):
            ts = self.tsl(tt)
            st = {}

            def p0():
                A = Ap.next()
                self.dma(out=A.t[:], in_=self.AT[tt, :, 0:nin, :], r=[self.k_AT[tt]], w=[A], sb=A)
                X = Xp.next()
                self.load_xT(X, tt)
                st["A"], st["X"], st["M"] = A, X, Mp.next()
            pcs = [p0] + [(lambda dc=dc: self.proj_piece(W, nin, st["A"], st["M"], dc)) for dc in range(8)]
            return pcs, st

        def S2(tt, st):
            ts = self.tsl(tt)
            c2 = {}

            def a():
                c2["b1"] = self.norm_ss(st["M"], 8, sq)

            def b():
                X, M = st["X"], st["M"]
                r1 = rs.next()
                self.rsqrt(c2["b1"].t, [c2["b1"]], r1.t[:], r1, D * 1e-6)
                self.resid_norm(X, M, (l * 4 + 1) * 8, r1, M)
                self.store_xT(X, tt)
                c2["b2"] = self.norm_ss(X, 8, sq)

            def c():
                X = st["X"]
                r2 = rs.next()
                self.rsqrt(c2["b2"].t, [c2["b2"]], r2.t[:], r2, D * 1e-6)
                H = Hp.next()
                self.scale_norm(H, X, (l * 4 + 2) * 8, r2)
                self.dma(out=self.H2s[tt], in_=H.t[:], r=[H], w=[self.k_H2[tt]], sb=H)
            return [lambda: None, lambda: None, a, lambda: None, b, lambda: None, lambda: None, c]

        pcs, st = S1(0)
        for p_ in pcs:
            p_()
        for tt in range(self.NTT):
            if tt + 1 < self.NTT:
                npcs, nst = S1(tt + 1)
            else:
                npcs, nst = [], None
            self.interleave(npcs, S2(tt, st))
            st = nst
        self.end_phase()

    def phase_PB(self, l):
        nc = self.nc
        self.begin_phase()
        W = self.sb([128, 8, DFF], BF16, "W")
        stage = self.rot(3, [128, 2048], F32, "stg")
        self.load_weight(self.w_1[l], W, stage)
        Hp = self.rot(2, [128, 8, TT], BF16, "H")
        Up = self.rot(3, [128, 8, TT], BF16, "U")
        tp = self.rot(4, [128, TT], F32, "tmp")
        for tt in range(self.NTT):
            ts = self.tsl(tt)
            H = Hp.next()
            self.dma(out=H.t[:], in_=self.H2s[tt], r=[self.k_H2[tt]], w=[H], sb=H)
            for fg in range(4):
                U = Up.next()
                for fi in range(8):
                    fc = fg * 8 + fi
                    bank = self.psn()

                    def f():
                        for k in range(8):
                            ins = self.mm(bank.t, W.t[:, k, fc * 128:(fc + 1) * 128], H.t[:, k, :], k == 0, k == 7)
                        return ins
                    self.op(self.pe, f, r=[W, H], w=[bank])
                    t = tp.next()
                    self.op(self.act, lambda: nc.scalar.activation(out=t.t[:], in_=bank.t, func=AF.Relu), r=[bank], w=[t])
                    E = self.ew()
                    self.op(E, lambda: E.h.tensor_tensor(out=U.t[:, fi, :], in0=t.t[:], in1=t.t[:], op=ALU.mult), r=[t], w=[U])
                self.dma(out=self.Us[tt, :, fg * 8:(fg + 1) * 8, :], in_=U.t[:], r=[U], w=[self.k_U[tt]], sb=U)
        self.end_phase()

    def phase_PC(self, l):
        nc = self.nc
        final = (l == self.depth - 1)
        self.begin_phase()
        W2 = self.sb([128, 32, D], BF16, "W2")
        Wg = self.sb([128, 8, D], BF16, "Wg")
        Wp = self.sb([128, 2, D], BF16, "Wp")
        stage = self.rot(2, [128, 512], F32, "stg")
        self.load_weight(self.w_2[l], W2, stage, 512)
        self.load_weight(self.w_pg[l], Wg, stage, 512)
        self.load_weight(self.w_pp[l], Wp, stage, 512)
        Up = self.rot(1, [128, 32, TT], BF16, "U")
        Xp = self.rot(1, [128, 8, TT], F32, "X")
        Mp = self.rot(2, [128, 8, TT], F32, "M")
        sq = self.sb([128, 8, TT], BF16, "sq")
        rs = self.rot(2, [128, TT], F32, "rs")
        pin = self.rot(1, [128, 4, PLE], F32, "pin")
        pTp = self.rot(1, [128, 2, TT], BF16, "pT")
        tg = self.rot(1, [128, TT], F32, "tg")
        tq = self.rot(1, [128, TT], F32, "tq")

        def S1(tt):
            ts = self.tsl(tt)
            st = {}

            def p0():
                U = Up.next()
                for fg in range(4):
                    self.dma(out=U.t[:, fg * 8:(fg + 1) * 8, :], in_=self.Us[tt, :, fg * 8:(fg + 1) * 8, :], r=[self.k_U[tt]], w=[U], sb=U)
                st["U"], st["M"] = U, Mp.next()
            pcs = [p0] + [(lambda dc=dc: self.proj_piece(W2, 32, st["U"], st["M"], dc)) for dc in range(8)]
            return pcs, st

        def S2(tt, st):
            ts = self.tsl(tt)
            M = st["M"]
            c2 = {}

            def a():
                X = Xp.next()
                self.load_xT(X, tt)
                pi = pin.next()
                self.dma(out=pi.t[:], in_=self.p_in[l, ts, :].rearrange("(s p) f -> p s f", p=128), w=[pi], sb=pi)
                pT = pTp.next()
                for kc in range(2):
                    bank = self.psn()

                    def f():
                        for s_ in range(4):
                            ins = nc.tensor.transpose(bank.t[:, s_ * 128:(s_ + 1) * 128], pi.t[:, s_, kc * 128:(kc + 1) * 128], self.identf.t[:])
                        return ins
                    self.op(self.pe, f, r=[pi, self.identf], w=[bank])
                    self.op(self.dve, lambda: nc.vector.tensor_copy(out=pT.t[:, kc, :], in_=bank.t), r=[bank], w=[pT])
                c2["X"], c2["pT"] = X, pT
                c2["b1"] = self.norm_ss(M, 8, sq)

            def b():
                X = c2["X"]
                r1 = rs.next()
                self.rsqrt(c2["b1"].t, [c2["b1"]], r1.t[:], r1, D * 1e-6)
                self.resid_norm(X, M, (l * 4 + 3) * 8, r1, M)
                c2["b2"] = self.norm_ss(X, 8, sq)

            def c():
                X = c2["X"]
                r2 = rs.next()
                self.rsqrt(c2["b2"].t, [c2["b2"]], r2.t[:], r2, D * 1e-6)
                self.scale_norm(sq, X, None, r2)

            def gate(dc):
                X, pT = c2["X"], c2["pT"]
                bg, bp = self.psn(), self.psn()

                def f():
                    for k in range(8):
                        ins = self.mm(bg.t, Wg.t[:, k, dc * 128:(dc + 1) * 128], sq.t[:, k, :], k == 0, k == 7)
                    return ins
                self.op(self.pe, f, r=[Wg, sq], w=[bg])

                def f2():
                    for k in range(2):
                        ins = self.mm(bp.t, Wp.t[:, k, dc * 128:(dc + 1) * 128], pT.t[:, k, :], k == 0, k == 1)
                    return ins
                self.op(self.pe, f2, r=[Wp, pT], w=[bp])
                g = tg.next()
                self.op(self.act, lambda: nc.scalar.activation(out=g.t[:], in_=bg.t, func=AF.Sigmoid), r=[bg], w=[g])
                q = tq.next()
                self.op(self.dve, lambda: nc.vector.tensor_tensor(out=q.t[:], in0=bp.t, in1=g.t[:], op=ALU.mult), r=[bp, g], w=[q])
                self.op(self.dve, lambda: nc.vector.tensor_tensor(out=X.t[:, dc, :], in0=X.t[:, dc, :], in1=q.t[:], op=ALU.add), r=[X, q], w=[X])

            def fin():
                X = c2["X"]
                if not final:
                    self.store_xT(X, tt)
                    return
                Yo = M
                for s_ in range(4):
                    for hf in range(2):
                        bank = self.psn()

                        def f():
                            for q4 in range(4):
                                ins = nc.tensor.transpose(bank.t[:, q4 * 128:(q4 + 1) * 128], X.t[:, hf * 4 + q4, s_ * 128:(s_ + 1) * 128], self.identf.t[:])
                            return ins
                        self.op(self.pe, f, r=[X, self.identf], w=[bank])
                        o_ap = Yo.t[:, 2 * s_ + hf, :]
                        if hf:
                            self.op(self.act, lambda: nc.scalar.activation(out=o_ap, in_=bank.t, func=AF.Copy), r=[bank], w=[Yo])
                        else:
                            self.op(self.dve, lambda: nc.vector.tensor_copy(out=o_ap, in_=bank.t), r=[bank], w=[Yo])
                self.dma(out=self.y_out[ts, :].rearrange("(s p) (h f) -> p s h f", p=128, h=2), in_=Yo.t[:].rearrange("p (s h) f -> p s h f", h=2), r=[Yo], w=[], sb=Yo)

            def g2(d0):
                def run():
                    for dc in range(d0, d0 + 2):
                        gate(dc)
                    if d0 == 6:
                        fin()
                return run
            return [a, lambda: None, b, lambda: None, c, g2(0), g2(2), g2(4), g2(6)]

        pcs, st = S1(0)
        for p_ in pcs:
            p_()
        for tt in range(self.NTT):
            if tt + 1 < self.NTT:
                npcs, nst = S1(tt + 1)
            else:
                npcs, nst = [], None
            self.interleave(npcs, S2(tt, st))
            st = nst
        self.end_phase()


def _const_tables(NT, seqs):
    assert sum(seqs) == NT
    pos = np.concatenate([np.arange(s) for s in seqs]).astype(np.float32)
    sid = np.concatenate([np.full(s, i) for i, s in enumerate(seqs)])
    inv = (1.0 / (np.float32(10000.0) ** (np.arange(0, 64, 2, dtype=np.float32) / np.float32(64)))).astype(np.float32)
    angA = pos[None, :] * inv[np.arange(128) % 32][:, None]
    ropeA = np.stack([np.cos(angA), np.sin(angA)]).astype(np.float32)
    angle = (1.0 / (np.float32(10000.0) ** np.linspace(0.0, 1.0, 128, dtype=np.float32))).astype(np.float32)
    angR = pos[None, :] * angle[:, None]
    ropeR = np.stack([np.cos(angR), np.sin(angR)]).astype(np.float32)
    NTT, NCH = NT // 512, NT // 128
    NKP = NCH // 2
    qs = sid[np.arange(NTT) * 512]
    ks = sid[np.arange(NKP) * 256]
    mb = np.where(qs[:, None] == ks[None, :], 0.0, NEG).astype(np.float32).reshape(1, -1)
    maskb = np.repeat(mb, 128, axis=0)
    k = np.arange(128)[:, None].astype(np.float32)
    q = np.arange(128)[None, :].astype(np.float32)
    BIG = np.float32(1.0e6)
    TFm = np.where(q >= k, q - k, BIG).astype(np.float32)
    TBm = np.where(k > q, k - q, BIG).astype(np.float32)
    posq = np.stack([np.repeat(q + 1.0, 128, axis=0), np.repeat(128.0 - q, 128, axis=0)]).astype(np.float32)
    pcol = np.arange(128, dtype=np.float32)
    col = np.stack([127.0 - pcol, pcol, np.full(128, 128.0, np.float32), np.zeros(128, np.float32)], axis=1).astype(np.float32)
    csid = sid[np.arange(NCH) * 128]
    last = np.ones(NCH, bool)
    last[:-1] = csid[1:] != csid[:-1]
    first = np.ones(NCH, bool)
    first[1:] = csid[1:] != csid[:-1]
    mfb = np.stack([np.repeat(np.where(last, 0.0, 1.0)[None, :], 128, axis=0), np.repeat(np.where(first, 0.0, 1.0)[None, :], 128, axis=0)]).astype(np.float32)
    return dict(c_ropeA=ropeA, c_ropeR=ropeR, c_maskb=maskb, c_tftb=np.stack([TFm, TBm]), c_pos=posq, c_col=col, c_mfb=mfb,
                c_ident=np.eye(128, dtype=np.float32))


def _perm_attn():
    idx = []
    for jj in range(4):
        for half in range(2):
            for r in range(4):
                m = 4 * jj + r
                idx.extend(range(m * 64 + half * 32, m * 64 + half * 32 + 32))
    return np.array(idx)


def _perm_ret():
    idx = []
    for h in range(4):
        idx.extend(range(h * 256, (h + 1) * 256, 2))
        idx.extend(range(h * 256 + 1, (h + 1) * 256, 2))
    return np.array(idx)


def _shared_inputs(inp):
    f = lambda a: np.ascontiguousarray(np.asarray(a, dtype=np.float32))
    pa, prr = _perm_attn(), _perm_ret()
    wqkv = f(inp["attn_w_qkv"])
    wqkv = np.concatenate([wqkv[:, :, 0:D][:, :, pa], wqkv[:, :, D:2 * D][:, :, pa], wqkv[:, :, 2 * D:]], axis=2)
    wrin = f(inp["ret_w_in"])
    wrin = np.concatenate([wrin[:, :, 0:D][:, :, prr], wrin[:, :, D:2 * D][:, :, prr], wrin[:, :, 2 * D:]], axis=2)
    norms = [f(inp[k]) for k in ("norm_pre_mix", "norm_post_mix", "norm_pre_mlp", "norm_post_mlp")]
    gains = np.zeros((128, 128), np.float32)
    for l in range(4):
        for k in range(4):
            gains[:, (l * 4 + k) * 8:(l * 4 + k + 1) * 8] = norms[k][l].reshape(8, 128).T
    lam = np.stack([f(inp[k]) for k in ("attn_lambda_q1", "attn_lambda_k1", "attn_lambda_q2", "attn_lambda_k2")], axis=1)
    lamv = np.repeat(lam.reshape(1, -1), 128, axis=0)
    decay = np.repeat(f(inp["ret_decay"]).reshape(1, -1), 128, axis=0)
    return dict(gains=gains, subln=np.ascontiguousarray(f(inp["attn_subln"]).T), lamv=np.ascontiguousarray(lamv),
                decay=np.ascontiguousarray(decay), w_qkv=np.ascontiguousarray(wqkv), w_o=f(inp["attn_w_o"]),
                w_rin=np.ascontiguousarray(wrin), w_rout=f(inp["ret_w_out"]), w_1=f(inp["mlp_w_in"]), w_2=f(inp["mlp_w_out"]),
                w_pp=f(inp["ple_w_proj"]), w_pg=f(inp["ple_w_gate"]))


_NC_CACHE = {}


def run_cores(inp, core_specs, NT, depth=4, debug=False):
    key = (NT, depth, debug)
    if key not in _NC_CACHE:
        _NC_CACHE[key] = KB(NT, depth, debug).build()
    nc = _NC_CACHE[key]
    shared = _shared_inputs(inp)
    in_maps = []
    for x, p, seqs in core_specs:
        m = dict(shared)
        m.update(_const_tables(NT, seqs))
        m["x"] = np.ascontiguousarray(x, dtype=np.float32)
        m["p"] = np.ascontiguousarray(p, dtype=np.float32)
        in_maps.append(m)
    res = run_bass_kernel_spmd(nc, in_maps, core_ids=list(range(len(in_maps))))
    if debug:
        return res.results
    return [r["y"] for r in res.results]


def kernel(**inputs):
    xp = np.asarray(inputs["x_prompt"], dtype=np.float32)
    xs = np.asarray(inputs["x_sample"], dtype=np.float32)
    pp = np.asarray(inputs["p_prompt"], dtype=np.float32)
    psm = np.asarray(inputs["p_sample"], dtype=np.float32)
    NT = 8192
    specs = []
    for c in range(4):
        specs.append((xp[4 * c:4 * c + 4].reshape(NT, D), pp[:, 4 * c:4 * c + 4].reshape(4, NT, PLE), [2048] * 4))
    for c in range(4):
        specs.append((xs[c], psm[:, c], [8192]))
    ys = run_cores(inputs, specs, NT, 4)
    y_prompt = np.stack([ys[c].reshape(4, 2048, D) for c in range(4)]).reshape(16, 2048, D).astype(np.float32)
    y_sample = np.stack([ys[4 + c] for c in range(4)]).astype(np.float32)
    return (y_prompt, y_sample)
```

```python
import math
from contextlib import ExitStack
import numpy as np
import concourse.bass as bass
import concourse.mybir as mybir
from concourse.bass_utils import run_bass_kernel_spmd

F32 = mybir.dt.float32
BF16 = mybir.dt.bfloat16
AF = mybir.ActivationFunctionType
ALU = mybir.AluOpType
AX = mybir.AxisListType

D = 1024
DFF = 4096
PLE = 256
TT = 512
NEG = -30000.0


class Sem:
    def __init__(self, nc, name):
        self.h = nc.alloc_semaphore(name)
        self.cnt = 0


class Eng:
    def __init__(self, nc, name, h, same_raw):
        self.name = name
        self.h = h
        self.sem = Sem(nc, "e_" + name)
        self.waited = {}
        self.same_raw = same_raw


class Buf:
    def __init__(self, t=None):
        self.t = t
        self.w = {}
        self.r = {}
        self.dsem = None


class Rot:
    def __init__(self, items):
        self.items = items
        self.i = 0

    def next(self):
        b = self.items[self.i % len(self.items)]
        self.i += 1
        return b


class KB:
    def __init__(self, NT, depth=4, debug=False):
        self.debug = debug
        self.nc = nc = bass.Bass("TRN2", target_bir_lowering=False)
        self.NT = NT
        self.NTT = NT // TT
        self.NCH = NT // 128
        self.depth = depth
        self.pe = Eng(nc, "pe", nc.tensor, False)
        self.act = Eng(nc, "act", nc.scalar, True)
        self.dve = Eng(nc, "dve", nc.vector, True)
        self.pool = Eng(nc, "pool", nc.gpsimd, True)
        self.sp = Eng(nc, "sp", nc.sync, False)
        self.engs = [self.pe, self.act, self.dve, self.pool, self.sp]
        self.all_sems = [e.sem for e in self.engs]
        self.free_dsems = []
        self.phase_dsems = []
        self.uid = 0
        self.st = None
        self.alt = 0

    def _need(self, r, w):
        need = {}
        for b in r:
            for S, v in b.w.items():
                if v > need.get(S, 0):
                    need[S] = v
        for b in w:
            for S, v in b.w.items():
                if v > need.get(S, 0):
                    need[S] = v
            for S, v in b.r.items():
                if v > need.get(S, 0):
                    need[S] = v
        return need

    def _waits(self, E, need, r):
        for S, v in need.items():
            if S is E.sem:
                continue
            if v > E.waited.get(S, 0):
                E.h.wait_ge(S.h, v)
                E.waited[S] = v
        if E.same_raw:
            raw = 0
            for b in r:
                v = b.w.get(E.sem, 0)
                if v > raw:
                    raw = v
            if raw > E.sem.cnt - 2 and raw > E.waited.get(E.sem, 0):
                E.h.wait_ge(E.sem.h, raw)
                E.waited[E.sem] = raw

    def op(self, E, fn, r=(), w=()):
        self._waits(E, self._need(r, w), r)
        ins = fn()
        E.sem.cnt += 1
        ins.then_inc(E.sem.h, 1)
        v = E.sem.cnt
        for b in r:
            b.r[E.sem] = v
        for b in w:
            b.w[E.sem] = v
        return ins

    def dma(self, out, in_, r=(), w=(), sb=None):
        Q = self.sp
        if sb.dsem is None:
            if self.free_dsems:
                sb.dsem = self.free_dsems.pop()
            else:
                self.uid += 1
                sb.dsem = Sem(self.nc, "d%d" % self.uid)
                self.all_sems.append(sb.dsem)
            self.phase_dsems.append(sb.dsem)
        self._waits(Q, self._need(r, w), ())
        ins = Q.h.dma_start(out=out, in_=in_)
        S = sb.dsem
        S.cnt += 16
        ins.then_inc(S.h, 16)
        for b in r:
            b.r[S] = S.cnt
        for b in w:
            b.w[S] = S.cnt

    def barrier(self):
        for E in self.engs:
            for S in self.all_sems:
                if S is E.sem:
                    continue
                if S.cnt > E.waited.get(S, 0):
                    E.h.wait_ge(S.h, S.cnt)
                    E.waited[S] = S.cnt

    def sb(self, shape, dt, name="t"):
        self.uid += 1
        t = self.st.enter_context(self.nc.sbuf_tensor("%s_%d" % (name, self.uid), list(shape), dt))
        return Buf(t)

    def rot(self, n, shape, dt, name="r"):
        return Rot([self.sb(shape, dt, name) for _ in range(n)])

    def begin_phase(self):
        self.st = ExitStack()
        self.st.__enter__()
        self.rt = self.rot(2, [128, TT], F32, "rt")

    def rsqrt(self, src_ap, src_bufs, out_ap, out_buf, addc, n=TT):
        nc = self.nc
        t = self.rt.next()
        self.op(self.act, lambda: nc.scalar.activation(out=t.t[:, :n], in_=src_ap, func=AF.Ln, bias=self.cbias.t[:, self.cbias_col[addc]:self.cbias_col[addc] + 1]), r=src_bufs + [self.cbias], w=[t])
        self.op(self.act, lambda: nc.scalar.activation(out=out_ap, in_=t.t[:, :n], func=AF.Exp, scale=-0.5), r=[t], w=[out_buf])

    def end_phase(self):
        self.barrier()
        self.free_dsems.extend(self.phase_dsems)
        self.phase_dsems = []
        self.st.__exit__(None, None, None)
        self.st = None

    def psn(self):
        b = self.psb[self.psi % 8]
        self.psi += 1
        return b

    def psn2(self):
        if self.psi % 2:
            self.psi += 1
        a = self.psb[self.psi % 8]
        b = self.psb[(self.psi + 1) % 8]
        self.psi += 2
        return a, b

    def ew(self):
        self.alt += 1
        return self.dve if self.alt % 2 else self.pool

    def mm(self, out, lhsT, rhs, start=True, stop=True):
        return self.nc.tensor.matmul(out, lhsT=lhsT, rhs=rhs, start=start, stop=stop)

    def declare(self):
        nc, NT, NCH = self.nc, self.NT, self.NCH
        di = lambda n, s: nc.dram_tensor(n, list(s), F32, kind="ExternalInput").ap()
        self.x_in = di("x", [NT, D])
        self.p_in = di("p", [self.depth, NT, PLE])
        self.gains = di("gains", [128, 16 * 8])
        self.subln = di("subln", [128, 2])
        self.lamv = di("lamv", [128, 2 * 4 * 64])
        self.decay = di("decay", [128, 16])
        self.w_qkv = di("w_qkv", [2, D, 3 * D])
        self.w_o = di("w_o", [2, D, D])
        self.w_rin = di("w_rin", [2, D, 6 * D])
        self.w_rout = di("w_rout", [2, 2 * D, D])
        self.w_1 = di("w_1", [4, D, DFF])
        self.w_2 = di("w_2", [4, DFF, D])
        self.w_pp = di("w_pp", [4, PLE, D])
        self.w_pg = di("w_pg", [4, D, D])
        self.c_ident = di("c_ident", [128, 128])
        self.c_ropeA = di("c_ropeA", [2, 128, NT])
        self.c_ropeR = di("c_ropeR", [2, 128, NT])
        self.c_maskb = di("c_maskb", [128, self.NTT * (NCH // 2)])
        self.c_tftb = di("c_tftb", [2, 128, 128])
        self.c_pos = di("c_pos", [2, 128, 128])
        self.c_col = di("c_col", [128, 4])
        self.c_mfb = di("c_mfb", [2, 128, NCH])
        self.y_out = nc.dram_tensor("y", [NT, D], F32, kind="ExternalOutput").ap()
        ds = lambda n, s, dt: nc.dram_tensor(n, list(s), dt, kind="ExternalOutput" if self.debug else "Internal").ap()
        self.xT = ds("s_xT", [self.NTT, 128, 8, TT], F32)
        self.QTs = ds("s_QT", [8, 128, NT], BF16)
        self.KTs = ds("s_KT", [8, 128, NT], BF16)
        self.VS = ds("s_V", [NT, D], BF16)
        self.AT = ds("s_AT", [self.NTT, 128, 16, TT], BF16)
        self.H2s = ds("s_H2", [self.NTT, 128, 8, TT], BF16)
        self.Us = ds("s_U", [self.NTT, 128, 32, TT], BF16)
        self.RK = ds("s_RK", [NT, D], BF16)
        self.RV = ds("s_RV", [NT, 2 * D], BF16)
        self.RG = ds("s_RG", [NT, 2 * D], BF16)
        self.OB = ds("s_OB", [NT, 2 * D], BF16)
        mk = lambda n: [Buf() for _ in range(n)]
        self.k_xT = mk(self.NTT)
        self.k_QT = mk(8)
        self.k_KT = mk(8)
        self.k_V = mk(1)
        self.k_AT = mk(self.NTT)
        self.k_H2 = mk(self.NTT)
        self.k_U = mk(self.NTT)
        self.k_R = mk(NCH)
        self.k_RQ = mk(self.NTT)
        self.k_OB = mk(NCH)

    def build(self):
        nc = self.nc
        self.declare()
        with ExitStack() as gst:
            self.st = gst
            ps = nc.alloc_psum_tensor("ps", [128, 8, 512], F32)
            self.ps = ps
            self.psbf = ps[:, :, :].bitcast(BF16)
            self.psb = [Buf(ps[:, b, :]) for b in range(8)]
            self.psi = 0
            self.ones = self.sb([128, 128], BF16, "ones")
            self.cbias = self.sb([128, 4], F32, "cbias")
            self.cbias_col = {D * 1e-6: 0, 128 * 1e-5: 1, 512 * 1e-6: 2}
            for v, cidx in self.cbias_col.items():
                self.op(self.dve, lambda: nc.vector.memset(self.cbias.t[:, cidx:cidx + 1], float(v)), w=[self.cbias])
            self.onesf = self.sb([128, 128], F32, "onesf")
            self.identf = self.sb([128, 128], F32, "identf")
            self.identb = self.sb([128, 128], BF16, "identb")
            self.gs = self.sb([128, 128], F32, "gs")
            self.sgs = self.sb([128, 2], F32, "sgs")
            self.lamt = self.sb([128, 8], F32, "lamt")
            self.lg = self.sb([128, 16], F32, "lg")
            self.maskb = self.sb([128, self.NTT * (self.NCH // 2)], F32, "maskb")
            lv = self.sb([128, 512], F32, "lv")
            pr = self.sb([128, 64], F32, "pr")
            sraw = self.sb([128, 2], F32, "sraw")
            dv, act = self.dve, self.act
            self.dma(out=self.identf.t[:], in_=self.c_ident, w=[self.identf], sb=self.identf)
            self.dma(out=self.gs.t[:], in_=self.gains, w=[self.gs], sb=self.gs)
            self.dma(out=sraw.t[:], in_=self.subln, w=[sraw], sb=sraw)
            self.dma(out=lv.t[:], in_=self.lamv, w=[lv], sb=lv)
            self.dma(out=self.lg.t[:], in_=self.decay, w=[self.lg], sb=self.lg)
            self.dma(out=self.maskb.t[:], in_=self.c_maskb, w=[self.maskb], sb=self.maskb)
            self.op(dv, lambda: nc.vector.memset(self.ones.t[:], 1.0), w=[self.ones])
            self.op(dv, lambda: nc.vector.memset(self.onesf.t[:], 1.0), w=[self.onesf])
            self.op(dv, lambda: nc.vector.tensor_copy(out=self.identb.t[:], in_=self.identf.t[:]), r=[self.identf], w=[self.identb])
            self.op(dv, lambda: nc.vector.tensor_scalar(out=self.gs.t[:], in0=self.gs.t[:], scalar1=32.0, scalar2=None, op0=ALU.mult), r=[self.gs], w=[self.gs])
            self.op(act, lambda: nc.scalar.activation(out=self.lg.t[:], in_=self.lg.t[:], func=AF.Exp), r=[self.lg], w=[self.lg])
            self.op(dv, lambda: nc.vector.tensor_scalar(out=self.lg.t[:], in0=self.lg.t[:], scalar1=-1.0, scalar2=None, op0=ALU.mult), r=[self.lg], w=[self.lg])
            for j in range(2):
                lam_init = 0.8 - 0.6 * math.exp(-0.3 * (2 * j))
                for e in range(2):
                    a0 = (j * 4 + 2 * e) * 64
                    self.op(dv, lambda: nc.vector.tensor_tensor(out=pr.t[:], in0=lv.t[:, a0:a0 + 64], in1=lv.t[:, a0 + 64:a0 + 128], op=ALU.mult), r=[lv], w=[pr])
                    self.op(dv, lambda: nc.vector.reduce_sum(out=self.lamt.t[:, j * 4 + 2 + e:j * 4 + 3 + e], in_=pr.t[:], axis=AX.X), r=[pr], w=[self.lamt])
                self.op(act, lambda: nc.scalar.activation(out=self.lamt.t[:, j * 4 + 2:j * 4 + 4], in_=self.lamt.t[:, j * 4 + 2:j * 4 + 4], func=AF.Exp), r=[self.lamt], w=[self.lamt])
                self.op(dv, lambda: nc.vector.tensor_tensor(out=self.lamt.t[:, j * 4:j * 4 + 1], in0=self.lamt.t[:, j * 4 + 2:j * 4 + 3], in1=self.lamt.t[:, j * 4 + 3:j * 4 + 4], op=ALU.subtract), r=[self.lamt], w=[self.lamt])
                self.op(dv, lambda: nc.vector.tensor_scalar(out=self.lamt.t[:, j * 4:j * 4 + 1], in0=self.lamt.t[:, j * 4:j * 4 + 1], scalar1=lam_init, scalar2=None, op0=ALU.add), r=[self.lamt], w=[self.lamt])
                self.op(dv, lambda: nc.vector.tensor_scalar(out=self.lamt.t[:, j * 4 + 1:j * 4 + 2], in0=self.lamt.t[:, j * 4:j * 4 + 1], scalar1=-1.0, scalar2=None, op0=ALU.mult), r=[self.lamt], w=[self.lamt])
                self.op(dv, lambda: nc.vector.tensor_scalar(out=self.sgs.t[:, j:j + 1], in0=sraw.t[:, j:j + 1], scalar1=math.sqrt(128.0) * (1.0 - lam_init), scalar2=None, op0=ALU.mult), r=[sraw], w=[self.sgs])
            self.barrier()
            for l in range(self.depth):
                if l % 2 == 0:
                    self.phase_PD_attn(l)
                    self.phase_attn(l)
                else:
                    self.phase_PD_ret(l)
                    self.phase_ret(l)
                self.phase_PA(l)
                self.phase_PB(l)
                self.phase_PC(l)
            self.barrier()
            self.st = None
        return nc

    def load_weight(self, Wd, Wb, stage, sw=2048):
        nc = self.nc
        K, Fd = Wd.shape
        i = 0
        for c in range(K // 128):
            for f0 in range(0, Fd, sw):
                fs = min(sw, Fd - f0)
                s = stage.next()
                self.dma(out=s.t[:, :fs], in_=Wd[c * 128:(c + 1) * 128, f0:f0 + fs], w=[s], sb=s)
                k = i % 3
                i += 1
                if k == 0:
                    self.op(self.dve, lambda: nc.vector.tensor_copy(out=Wb.t[:, c, f0:f0 + fs], in_=s.t[:, :fs]), r=[s], w=[Wb])
                elif k == 1:
                    self.op(self.pool, lambda: nc.gpsimd.tensor_copy(out=Wb.t[:, c, f0:f0 + fs], in_=s.t[:, :fs]), r=[s], w=[Wb])
                else:
                    self.op(self.act, lambda: nc.scalar.activation(out=Wb.t[:, c, f0:f0 + fs], in_=s.t[:, :fs], func=AF.Copy), r=[s], w=[Wb])

    def tsl(self, tt):
        return slice(tt * TT, (tt + 1) * TT)

    def load_xT(self, X, tt):
        self.dma(out=X.t[:], in_=self.xT[tt], r=[self.k_xT[tt]], w=[X], sb=X)

    def store_xT(self, X, tt):
        self.dma(out=self.xT[tt], in_=X.t[:], r=[X], w=[self.k_xT[tt]], sb=X)

    def norm_rstd(self, X, nch, Dn, eps, sq, rstd):
        bank = self.norm_ss(X, nch, sq)
        self.rsqrt(bank.t, [bank], rstd.t[:], rstd, Dn * eps)

    def norm_ss(self, X, nch, sq):
        nc = self.nc
        self.op(self.act, lambda: nc.scalar.activation(out=sq.t[:, :nch, :], in_=X.t[:, :nch, :], func=AF.Square), r=[X], w=[sq])
        bank = self.psn()

        def f():
            for c in range(nch):
                ins = self.mm(bank.t, self.ones.t[:], sq.t[:, c, :], c == 0, c == nch - 1)
            return ins
        self.op(self.pe, f, r=[sq, self.ones], w=[bank])
        return bank

    def scale_norm(self, H, X, gcol, rstd):
        nc = self.nc
        for c in range(8):
            E = self.dve
            sc = 32.0 if gcol is None else self.gs.t[:, gcol + c:gcol + c + 1]
            rr = [X, rstd] + ([] if gcol is None else [self.gs])
            self.op(E, lambda: E.h.scalar_tensor_tensor(out=H.t[:, c, :], in0=X.t[:, c, :], scalar=sc, in1=rstd.t[:], op0=ALU.mult, op1=ALU.mult), r=rr, w=[H])

    def resid_norm(self, X, M, gcol, rstd, Mr):
        nc = self.nc
        self.op(self.dve, lambda: nc.vector.tensor_tensor(out=Mr.t[:], in0=M.t[:], in1=rstd.t[:].unsqueeze(1).to_broadcast([128, 8, TT]), op=ALU.mult), r=[M, rstd], w=[Mr])
        for c in range(8):
            E = self.dve
            self.op(E, lambda: E.h.scalar_tensor_tensor(out=X.t[:, c, :], in0=Mr.t[:, c, :], scalar=self.gs.t[:, gcol + c:gcol + c + 1], in1=X.t[:, c, :], op0=ALU.mult, op1=ALU.add), r=[Mr, X, self.gs], w=[X])

    def proj_fm(self, W, nk, A, M, col0=0):
        for dc in range(8):
            self.proj_piece(W, nk, A, M, dc, col0)

    def proj_piece(self, W, nk, A, M, dc, col0=0):
        nc = self.nc
        bank = self.psn()

        def f():
            for k in range(nk):
                ins = self.mm(bank.t, W.t[:, k, col0 + dc * 128:col0 + (dc + 1) * 128], A.t[:, k, :], k == 0, k == nk - 1)
            return ins
        self.op(self.pe, f, r=[W, A], w=[bank])
        self.op(self.act, lambda: nc.scalar.activation(out=M.t[:, dc, :], in_=bank.t, func=AF.Copy), r=[bank], w=[M])

    def interleave(self, pieces, steps):
        n = max(len(pieces), len(steps))
        for k in range(n):
            if k < len(pieces):
                pieces[k]()
            if k < len(steps):
                steps[k]()

    def x_from_input(self, X, tt, xin):
        nc = self.nc
        self.dma(out=xin.t[:], in_=self.x_in[self.tsl(tt), :].rearrange("(s p) f -> p s f", p=128), w=[xin], sb=xin)
        for c in range(8):
            bank = self.psn()

            def f():
                for s in range(4):
                    ins = nc.tensor.transpose(bank.t[:, s * 128:(s + 1) * 128], xin.t[:, s, c * 128:(c + 1) * 128], self.identf.t[:])
                return ins
            self.op(self.pe, f, r=[xin, self.identf], w=[bank])
            if c % 2:
                self.op(self.act, lambda: nc.scalar.activation(out=X.t[:, c, :], in_=bank.t, func=AF.Copy), r=[bank], w=[X])
            else:
                self.op(self.dve, lambda: nc.vector.tensor_copy(out=X.t[:, c, :], in_=bank.t), r=[bank], w=[X])

    def rope_pair(self, bankA, bankB, cs, sn, Ap, Bp, tmp, scale):
        nc = self.nc
        Af, Bf, t1, t2, t3, t4 = [tmp.next() for _ in range(6)]
        self.op(self.act, lambda: nc.scalar.activation(out=Af.t[:], in_=bankA.t, func=AF.Copy, scale=scale), r=[bankA], w=[Af])
        self.op(self.act, lambda: nc.scalar.activation(out=Bf.t[:], in_=bankB.t, func=AF.Copy, scale=scale), r=[bankB], w=[Bf])
        self.op(self.dve, lambda: nc.vector.tensor_tensor(out=t1.t[:], in0=Af.t[:], in1=cs.t[:], op=ALU.mult), r=[Af, cs], w=[t1])
        self.op(self.dve, lambda: nc.vector.tensor_tensor(out=t2.t[:], in0=Bf.t[:], in1=sn.t[:], op=ALU.mult), r=[Bf, sn], w=[t2])
        self.op(self.pool, lambda: nc.gpsimd.tensor_tensor(out=t3.t[:], in0=Bf.t[:], in1=cs.t[:], op=ALU.mult), r=[Bf, cs], w=[t3])
        self.op(self.pool, lambda: nc.gpsimd.tensor_tensor(out=t4.t[:], in0=Af.t[:], in1=sn.t[:], op=ALU.mult), r=[Af, sn], w=[t4])
        self.op(self.dve, lambda: nc.vector.tensor_tensor(out=Ap, in0=t1.t[:], in1=t2.t[:], op=ALU.subtract), r=[t1, t2], w=[self._apbuf])
        self.op(self.pool, lambda: nc.gpsimd.tensor_tensor(out=Bp, in0=t3.t[:], in1=t4.t[:], op=ALU.add), r=[t3, t4], w=[self._bpbuf])

    def phase_PD_attn(self, l):
        nc = self.nc
        j = l // 2
        self.begin_phase()
        W = self.sb([128, 8, 3 * D], BF16, "W")
        stage = self.rot(2, [128, 2048], F32, "stg")
        self.load_weight(self.w_qkv[j], W, stage)
        Xp = self.rot(2, [128, 8, TT], F32, "X")
        xin = self.rot(1, [128, 4, D], F32, "xin") if l == 0 else None
        sq = self.sb([128, 8, TT], BF16, "sq")
        rs = self.rot(2, [128, TT], F32, "rs")
        Hp = self.rot(2, [128, 8, TT], BF16, "H")
        csp = self.rot(2, [128, TT], F32, "cs")
        snp = self.rot(2, [128, TT], F32, "sn")
        tmp = self.rot(12, [128, TT], F32, "tmp")
        ABp = self.rot(4, [128, 2, TT], BF16, "AB")
        Vp = self.rot(2, [128, 4, D], BF16, "Vt")
        def S1(tt):
            ts = self.tsl(tt)
            X = Xp.next()
            if l == 0:
                self.x_from_input(X, tt, xin.next())
                self.store_xT(X, tt)
            else:
                self.load_xT(X, tt)
            r = rs.next()
            self.norm_rstd(X, 8, D, 1e-6, sq, r)
            H = Hp.next()
            self.scale_norm(H, X, (l * 4 + 0) * 8, r)
            return (H,)

        cst = {}

        def S2(tt, H, part):
            ts = self.tsl(tt)
            if part == 0:
                cs, sn = csp.next(), snp.next()
                self.dma(out=cs.t[:], in_=self.c_ropeA[0, :, ts], w=[cs], sb=cs)
                self.dma(out=sn.t[:], in_=self.c_ropeA[1, :, ts], w=[sn], sb=sn)
                cst["cs"], cst["sn"] = cs, sn
            cs, sn = cst["cs"], cst["sn"]
            for which in ((0,) if part == 0 else (1,)):
                dst = self.QTs if which == 0 else self.KTs
                ktok = self.k_QT if which == 0 else self.k_KT
                for jj in range(4):
                    bA, bB = self.psn(), self.psn()
                    for bank, ch in ((bA, 2 * jj), (bB, 2 * jj + 1)):
                        c0 = which * D + ch * 128

                        def f():
                            for k in range(8):
                                ins = self.mm(bank.t, W.t[:, k, c0:c0 + 128], H.t[:, k, :], k == 0, k == 7)
                            return ins
                        self.op(self.pe, f, r=[W, H], w=[bank])
                    AB = ABp.next()
                    self._apbuf = AB
                    self._bpbuf = AB
                    self.rope_pair(bA, bB, cs, sn, AB.t[:, 0, :], AB.t[:, 1, :], tmp, 1.0)
                    for rr in range(4):
                        m = 4 * jj + rr
                        hd, cc = m // 2, m % 2
                        for ab in range(2):
                            self.dma(out=dst[hd, cc * 64 + ab * 32:cc * 64 + ab * 32 + 32, ts], in_=AB.t[32 * rr:32 * rr + 32, ab, :], r=[AB], w=[ktok[hd]], sb=AB)
            if part == 0:
                return
            Vt = Vp.next()
            for s in range(4):
                for hf in range(2):
                    bank = self.psn()
                    c0 = 2 * D + hf * 512

                    def f():
                        for k in range(8):
                            ins = self.mm(bank.t, H.t[:, k, s * 128:(s + 1) * 128], W.t[:, k, c0:c0 + 512], k == 0, k == 7)
                        return ins
                    self.op(self.pe, f, r=[W, H], w=[bank])
                    if hf:
                        self.op(self.act, lambda: nc.scalar.activation(out=Vt.t[:, s, hf * 512:(hf + 1) * 512], in_=bank.t, func=AF.Copy), r=[bank], w=[Vt])
                    else:
                        self.op(self.dve, lambda: nc.vector.tensor_copy(out=Vt.t[:, s, hf * 512:(hf + 1) * 512], in_=bank.t), r=[bank], w=[Vt])
            self.dma(out=self.VS[ts, :].rearrange("(s p) f -> p s f", p=128), in_=Vt.t[:], r=[Vt], w=[self.k_V[0]], sb=Vt)

        cur = S1(0)
        for tt in range(self.NTT):
            S2(tt, cur[0], 0)
            nxt = S1(tt + 1) if tt + 1 < self.NTT else None
            S2(tt, cur[0], 1)
            cur = nxt
        self.end_phase()

    def phase_attn(self, l):
        nc = self.nc
        j = l // 2
        NT, NTT, NCH = self.NT, self.NTT, self.NCH
        NKP = NCH // 2
        dv = self.dve
        self.begin_phase()
        KTp = self.rot(2, [128, 2, NT], BF16, "KT")
        for kb_ in KTp.items:
            self.op(self.pool, lambda: nc.gpsimd.memset(kb_.t[:], 0.0), w=[kb_])
        QTp = self.rot(2, [128, NT], BF16, "QT")
        Vp = self.rot(2, [128, NCH, 128], BF16, "Vh")
        Pp = self.rot(6, [128, 2, TT], BF16, "P")
        Ocp = self.rot(3, [128, TT], F32, "Oc")
        zsp = self.rot(2, [128, TT], F32, "zs")
        rzp = self.rot(3, [128, TT], F32, "rz")
        t0p = self.rot(2, [128, TT], F32, "t0")
        t1p = self.rot(2, [128, TT], F32, "t1")
        ddp = self.rot(2, [128, TT], F32, "dd")
        sqp = self.rot(2, [128, TT], BF16, "sqd")
        rrp = self.rot(2, [128, TT], F32, "rr")
        yp = self.rot(2, [128, TT], BF16, "yt")
        ZEp = [self.sb([128, 2, TT], F32, "ZE") for _ in range(2)]
        ZOp = [[self.sb([128, TT], F32, "ZO") for _ in range(2)] for _ in range(2)]
        slots = [(self.psb[2 * k], self.psb[2 * k + 1], self.ps[:, 2 * k:2 * k + 2, :]) for k in range(3)]
        O, Z = self.psb[6], self.psb[7]
        heads = {}
        slot_of = {}
        sctr = [0]

        def next_slot():
            sl = slots[sctr[0] % 3]
            sctr[0] += 1
            return sl

        def load_head(h):
            KT, QT, V = KTp.next(), QTp.next(), Vp.next()
            self.dma(out=KT.t[0:64, 0, :], in_=self.KTs[h, 0:64, :], r=[self.k_KT[h]], w=[KT], sb=KT)
            self.dma(out=KT.t[64:128, 1, :], in_=self.KTs[h, 64:128, :], r=[self.k_KT[h]], w=[KT], sb=KT)
            self.dma(out=QT.t[:], in_=self.QTs[h], r=[self.k_QT[h]], w=[QT], sb=QT)
            self.dma(out=V.t[:], in_=self.VS[:, h * 128:(h + 1) * 128].rearrange("(kb p) d -> p kb d", p=128), r=[self.k_V[0]], w=[V], sb=V)
            heads[h] = (KT, QT, V)

        items = [(h, qt, c, kp) for h in range(8) for qt in range(NTT) for c in range(2) for kp in range(NKP)]
        state = {"t0": None}
        pending = []
        big = NKP >= 16
        KZ = 12 if big else 5
        D_FZ, D_RZ, D_B, D_C = (2, 4, 5, 9) if big else (1, 1, 1, 2)

        def emit_qk(i):
            h, qt, c, kp = items[i]
            if qt == 0 and c == 0 and kp == 0 and h == 0:
                load_head(0)
            if qt == 0 and c == 0 and kp == 3 and h + 1 < 8:
                load_head(h + 1)
            KT, QT, V = heads[h]
            b0, b1, sap = slot_of[i] = next_slot()

            def f():
                for jx in range(2):
                    kb = 2 * kp + jx
                    ins = self.mm(sap[:, jx, :], KT.t[:, c, kb * 128:(kb + 1) * 128], QT.t[:, qt * TT:(qt + 1) * TT], True, True)
                return ins
            self.op(self.pe, f, r=[KT, QT], w=[b0, b1])

        def emit_exp(i):
            h, qt, c, kp = items[i]
            b0, b1, sap = slot_of.pop(i)
            P = Pp.next()
            col = qt * NKP + kp
            self.op(self.act, lambda: nc.scalar.activation(out=P.t[:], in_=sap, func=AF.Exp, bias=self.maskb.t[:, col:col + 1], scale=0.125), r=[b0, b1, self.maskb], w=[P])
            return P

        def emit_av(i, P):
            h, qt, c, kp = items[i]
            KT, QT, V = heads[h]
            g = (h * NTT + qt) * 2 + c
            ZE, ZO = ZEp[g % 2], ZOp[g % 2]
            peZ = (kp >= KZ)
            if kp == 0:
                self.op(dv, lambda: nc.vector.tensor_copy(out=ZE.t[:], in_=P.t[:]), r=[P], w=[ZE])
            elif not peZ:
                self.op(dv, lambda: nc.vector.tensor_tensor(out=ZE.t[:], in0=ZE.t[:], in1=P.t[:], op=ALU.add), r=[P, ZE], w=[ZE])
            else:
                ZOk = ZO[kp % 2]
                if kp < KZ + 2:
                    self.op(dv, lambda: nc.vector.tensor_copy(out=ZOk.t[:], in_=P.t[:, 0, :]), r=[P], w=[ZOk])
                else:
                    self.op(dv, lambda: nc.vector.tensor_tensor(out=ZOk.t[:], in0=ZOk.t[:], in1=P.t[:, 0, :], op=ALU.add), r=[P, ZOk], w=[ZOk])

            def f():
                for jx in range(2):
                    kb = 2 * kp + jx
                    first = (kp == 0 and jx == 0)
                    last = (kp == NKP - 1 and jx == 1)
                    ins = self.mm(O.t, V.t[:, kb, :], P.t[:, jx, :], first, last)
                if peZ:
                    ins = self.mm(Z.t, self.ones.t[:], P.t[:, 1, :], kp == KZ, False)
                return ins
            self.op(self.pe, f, r=[V, P, self.ones], w=[O, Z] if peZ else [O])
            if kp != NKP - 1:
                return
            Oc = Ocp.next()
            self.op(dv, lambda: nc.vector.tensor_copy(out=Oc.t[:], in_=O.t), r=[O], w=[Oc])
            zs = zsp.next()
            self.op(dv, lambda: nc.vector.tensor_tensor(out=zs.t[:], in0=ZE.t[:, 0, :], in1=ZE.t[:, 1, :], op=ALU.add), r=[ZE], w=[zs])
            for kk in (KZ, KZ + 1):
                if NKP > kk:
                    ZOk = ZO[kk % 2]
                    self.op(dv, lambda: nc.vector.tensor_tensor(out=zs.t[:], in0=zs.t[:], in1=ZOk.t[:], op=ALU.add), r=[zs, ZOk], w=[zs])
            rz = rzp.next()

            def stageFZ():
                self.op(self.pe, lambda: self.mm(Z.t, self.onesf.t[:], zs.t[:], NKP <= KZ, True), r=[zs, self.onesf], w=[Z])

            def stageRZ():
                self.op(dv, lambda: nc.vector.reciprocal(out=rz.t[:], in_=Z.t), r=[Z], w=[rz])
            pending.append((i + D_FZ, stageFZ))
            pending.append((i + D_RZ, stageRZ))

            def stageB():
                if c == 0:
                    t0 = t0p.next()
                    self.op(dv, lambda: nc.vector.tensor_tensor(out=t0.t[:], in0=Oc.t[:], in1=rz.t[:], op=ALU.mult), r=[Oc, rz], w=[t0])
                    state["t0"] = t0
                    return
                t0 = state["t0"]
                t1 = t1p.next()
                self.op(dv, lambda: nc.vector.tensor_tensor(out=t1.t[:], in0=Oc.t[:], in1=rz.t[:], op=ALU.mult), r=[Oc, rz], w=[t1])
                dd = ddp.next()
                self.op(dv, lambda: nc.vector.scalar_tensor_tensor(out=dd.t[:], in0=t1.t[:], scalar=self.lamt.t[:, j * 4 + 1:j * 4 + 2], in1=t0.t[:], op0=ALU.mult, op1=ALU.add), r=[t1, t0, self.lamt], w=[dd])
                sqd = sqp.next()
                self.op(dv, lambda: nc.vector.tensor_tensor(out=sqd.t[:], in0=dd.t[:], in1=dd.t[:], op=ALU.mult), r=[dd], w=[sqd])
                self.op(self.pe, lambda: self.mm(Z.t, self.ones.t[:], sqd.t[:], True, True), r=[sqd, self.ones], w=[Z])

                def stageC():
                    rr = rrp.next()
                    self.rsqrt(Z.t, [Z], rr.t[:], rr, 128 * 1e-5)
                    yt = yp.next()
                    self.op(dv, lambda: nc.vector.scalar_tensor_tensor(out=yt.t[:], in0=dd.t[:], scalar=self.sgs.t[:, j:j + 1], in1=rr.t[:], op0=ALU.mult, op1=ALU.mult), r=[dd, rr, self.sgs], w=[yt])
                    self.dma(out=self.AT[qt, :, h, :], in_=yt.t[:], r=[yt], w=[self.k_AT[qt]], sb=yt)
                pending.append((i + D_C, stageC))
            pending.append((i + D_B, stageB))

        def run_pending(upto):
            pending.sort(key=lambda e: e[0])
            while pending and (upto is None or pending[0][0] <= upto):
                pending.pop(0)[1]()
                pending.sort(key=lambda e: e[0])

        n = len(items)
        emit_qk(0)
        emit_qk(1)
        for i in range(n):
            if i + 2 < n:
                emit_qk(i + 2)
            P = emit_exp(i)
            emit_av(i, P)
            run_pending(i)
        run_pending(None)
        self.end_phase()

    def phase_PD_ret(self, l):
        nc = self.nc
        j = l // 2
        self.begin_phase()
        W = self.sb([128, 8, 6 * D], BF16, "W")
        stage = self.rot(2, [128, 1024], F32, "stg")
        self.load_weight(self.w_rin[j], W, stage, 1024)
        Xp = self.rot(1, [128, 8, TT], F32, "X")
        sq = self.sb([128, 8, TT], BF16, "sq")
        rs = self.rot(2, [128, TT], F32, "rs")
        Hp = self.rot(1, [128, 8, TT], BF16, "H")
        csp = self.rot(1, [128, TT], F32, "cs")
        snp = self.rot(1, [128, TT], F32, "sn")
        tmp = self.rot(6, [128, TT], F32, "tmp")
        QKp = self.rot(2, [128, 8, TT], BF16, "QK")
        Ktp = self.rot(1, [128, 4, D], BF16, "Ktm")
        Vp = self.rot(1, [128, 4, 2 * D], BF16, "Vt")
        for tt in range(self.NTT):
            ts = self.tsl(tt)
            X = Xp.next()
            self.load_xT(X, tt)
            r = rs.next()
            self.norm_rstd(X, 8, D, 1e-6, sq, r)
            H = Hp.next()
            self.scale_norm(H, X, (l * 4 + 0) * 8, r)
            cs, sn = csp.next(), snp.next()
            self.dma(out=cs.t[:], in_=self.c_ropeR[0, :, ts], w=[cs], sb=cs)
            self.dma(out=sn.t[:], in_=self.c_ropeR[1, :, ts], w=[sn], sb=sn)
            KR = None
            for which in range(2):
                dst = self.QTs if which == 0 else self.KTs
                QK = QKp.next()
                for hh in range(4):
                    bA, bB = self.psn(), self.psn()
                    for bank, ch in ((bA, 2 * hh), (bB, 2 * hh + 1)):
                        c0 = which * D + ch * 128

                        def f():
                            for k in range(8):
                                ins = self.mm(bank.t, W.t[:, k, c0:c0 + 128], H.t[:, k, :], k == 0, k == 7)
                            return ins
                        self.op(self.pe, f, r=[W, H], w=[bank])
                    self._apbuf = QK
                    self._bpbuf = QK
                    self.rope_pair(bA, bB, cs, sn, QK.t[:, 2 * hh, :], QK.t[:, 2 * hh + 1, :], tmp, 1.0 if which == 0 else 1.0 / 16.0)
                self.dma(out=dst[:, :, ts].rearrange("c p t -> p c t"), in_=QK.t[:], r=[QK], w=[self.k_RQ[tt]], sb=QK)
                if which == 1:
                    KR = QK
            Ktm = Ktp.next()
            for s in range(4):
                b0, b1 = self.psn2()
                bi = self.psb.index(b0)

                def f():
                    for ch in range(8):
                        ins = nc.tensor.transpose(self.psbf[:, bi, ch * 128:(ch + 1) * 128], KR.t[:, ch, s * 128:(s + 1) * 128], self.identb.t[:])
                    return ins
                self.op(self.pe, f, r=[KR, self.identb], w=[b0])
                self.op(self.dve, lambda: nc.vector.tensor_copy(out=Ktm.t[:, s, :], in_=self.psbf[:, bi, :]), r=[b0], w=[Ktm])
            self.dma(out=self.RK[ts, :].rearrange("(s p) f -> p s f", p=128), in_=Ktm.t[:], r=[Ktm], w=[self.k_R[tt * 4 + q] for q in range(4)], sb=Ktm)
            for which, dst in ((2, self.RV), (3, self.RG)):
                Vt = Vp.next()
                for s in range(4):
                    for hf in range(4):
                        bank = self.psn()
                        c0 = (2 * D if which == 2 else 4 * D) + hf * 512

                        def f():
                            for k in range(8):
                                ins = self.mm(bank.t, H.t[:, k, s * 128:(s + 1) * 128], W.t[:, k, c0:c0 + 512], k == 0, k == 7)
                            return ins
                        self.op(self.pe, f, r=[W, H], w=[bank])
                        if which == 3:
                            self.op(self.act, lambda: nc.scalar.activation(out=Vt.t[:, s, hf * 512:(hf + 1) * 512], in_=bank.t, func=AF.Silu), r=[bank], w=[Vt])
                        elif hf % 2:
                            self.op(self.act, lambda: nc.scalar.activation(out=Vt.t[:, s, hf * 512:(hf + 1) * 512], in_=bank.t, func=AF.Copy), r=[bank], w=[Vt])
                        else:
                            self.op(self.dve, lambda: nc.vector.tensor_copy(out=Vt.t[:, s, hf * 512:(hf + 1) * 512], in_=bank.t), r=[bank], w=[Vt])
                self.dma(out=dst[ts, :].rearrange("(s p) f -> p s f", p=128), in_=Vt.t[:], r=[Vt], w=[self.k_R[tt * 4 + q] for q in range(4)], sb=Vt)
        self.end_phase()

    def phase_ret(self, l):
        nc = self.nc
        j = l // 2
        NCH = self.NCH
        dv, act, pool, pe = self.dve, self.act, self.pool, self.pe
        self.begin_phase()
        TF = self.sb([128, 128], F32, "TF")
        TB = self.sb([128, 128], F32, "TB")
        P1 = self.sb([128, 128], F32, "P1")
        P2 = self.sb([128, 128], F32, "P2")
        CC = self.sb([128, 4], F32, "CC")
        MF = self.sb([128, NCH], F32, "MF")
        MB = self.sb([128, NCH], F32, "MB")
        for bfr, src in ((TF, self.c_tftb[0]), (TB, self.c_tftb[1]), (P1, self.c_pos[0]), (P2, self.c_pos[1]), (CC, self.c_col), (MF, self.c_mfb[0]), (MB, self.c_mfb[1])):
            self.dma(out=bfr.t[:], in_=src, w=[bfr], sb=bfr)
        DT = self.sb([128, 4, 128], F32, "DT")
        e1 = self.sb([128, 128], F32, "e1")
        e2 = self.sb([128, 128], F32, "e2")
        XIF = self.sb([128, 8, 128], BF16, "XIF")
        XIB = self.sb([128, 8, 128], BF16, "XIB")
        ZF = self.sb([128, 4, NCH], F32, "ZF")
        ZB = self.sb([128, 4, NCH], F32, "ZB")
        CDF = self.sb([128, 4, NCH], F32, "CDF")
        CDB = self.sb([128, 4, NCH], F32, "CDB")
        sc = self.sb([128, 16], F32, "sc")
        for h in range(4):
            lf = self.lg.t[:, j * 8 + h:j * 8 + h + 1]
            lb = self.lg.t[:, j * 8 + 4 + h:j * 8 + 4 + h + 1]
            self.op(act, lambda: nc.scalar.activation(out=e1.t[:], in_=TF.t[:], func=AF.Exp, scale=lf), r=[TF, self.lg], w=[e1])
            self.op(act, lambda: nc.scalar.activation(out=e2.t[:], in_=TB.t[:], func=AF.Exp, scale=lb), r=[TB, self.lg], w=[e2])
            self.op(dv, lambda: nc.vector.tensor_tensor(out=DT.t[:, h, :], in0=e1.t[:], in1=e2.t[:], op=ALU.add), r=[e1, e2], w=[DT])
            for q in range(2):
                self.op(act, lambda: nc.scalar.activation(out=XIF.t[:, 2 * h + q, :], in_=P1.t[:], func=AF.Exp, scale=lf), r=[P1, self.lg], w=[XIF])
                self.op(act, lambda: nc.scalar.activation(out=XIB.t[:, 2 * h + q, :], in_=P2.t[:], func=AF.Exp, scale=lb), r=[P2, self.lg], w=[XIB])
            self.op(act, lambda: nc.scalar.activation(out=sc.t[:, 4 * h + 0:4 * h + 1], in_=CC.t[:, 0:1], func=AF.Exp, scale=lf), r=[CC, self.lg], w=[sc])
            self.op(act, lambda: nc.scalar.activation(out=sc.t[:, 4 * h + 1:4 * h + 2], in_=CC.t[:, 2:3], func=AF.Exp, scale=lf), r=[CC, self.lg], w=[sc])
            self.op(act, lambda: nc.scalar.activation(out=sc.t[:, 4 * h + 2:4 * h + 3], in_=CC.t[:, 1:2], func=AF.Exp, scale=lb), r=[CC, self.lg], w=[sc])
            self.op(act, lambda: nc.scalar.activation(out=sc.t[:, 4 * h + 3:4 * h + 4], in_=CC.t[:, 2:3], func=AF.Exp, scale=lb), r=[CC, self.lg], w=[sc])
            for k, (dst, msk) in enumerate(((ZF, MF), (CDF, MF), (ZB, MB), (CDB, MB))):
                self.op(dv, lambda: nc.vector.tensor_scalar(out=dst.t[:, h, :], in0=msk.t[:], scalar1=sc.t[:, 4 * h + k:4 * h + k + 1], scalar2=None, op0=ALU.mult), r=[msk, sc], w=[dst])

        Rf = [self.sb([128, 2, 512], F32, "Rf") for _ in range(4)]
        Rb = [self.sb([128, 2, 512], BF16, "Rb") for _ in range(4)]
        QGp = self.rot(2, [128, 8, TT], BF16, "QG")
        KGp = self.rot(2, [128, 8, TT], BF16, "KG")
        Ktp = self.rot(3, [128, D], BF16, "Ktm")
        Vcp = self.rot(3, [128, 2 * D], BF16, "Vc")
        SGp = self.rot(3, [128, 2 * D], BF16, "SG")
        OBp = self.rot(3, [128, 2 * D], BF16, "OBt")
        Qxp = self.rot(2, [128, 8, 128], BF16, "Qx")
        Kzp = self.rot(2, [128, D], BF16, "Kz")
        Smp = self.rot(4, [128, 128], BF16, "Sm")
        Ofp = self.rot(3, [128, 4, 512], F32, "Of")
        junk = self.sb([128, 512], BF16, "junk")
        ssp = self.rot(2, [128, 4], F32, "ss")
        r1p = self.rot(2, [128, 4], F32, "r1")
        Yp = self.rot(2, [128, 2 * D], BF16, "Y")
        YTp = self.rot(2, [128, 16, 128], BF16, "YT")
        B = self.psb

        def make_kz(c, Ktm, Ztab):
            Kz = Kzp.next()
            for h in range(4):
                self.op(act, lambda: nc.scalar.activation(out=Kz.t[:, h * 256:(h + 1) * 256], in_=Ktm.t[:, h * 256:(h + 1) * 256], func=AF.Copy, scale=Ztab.t[:, h, c:c + 1]), r=[Ktm, Ztab], w=[Kz])
            return Kz

        def update_state(c, Kz, Vc, CDtab):
            for h in range(4):
                bi = 2 * (h % 2)
                b0, b1 = B[bi], B[bi + 1]

                def f():
                    for dk in range(2):
                        ins = self.mm(self.ps[:, bi + dk, :], Kz.t[:, h * 256 + dk * 128:h * 256 + (dk + 1) * 128], Vc.t[:, h * 512:(h + 1) * 512], True, True)
                    return ins
                self.op(pe, f, r=[Kz, Vc], w=[b0, b1])
                self.op(dv, lambda: nc.vector.scalar_tensor_tensor(out=Rf[h].t[:], in0=Rf[h].t[:], scalar=CDtab.t[:, h, c:c + 1], in1=self.ps[:, bi:bi + 2, :], op0=ALU.mult, op1=ALU.add), r=[Rf[h], b0, b1, CDtab], w=[Rf[h]])
                self.op(act, lambda: nc.scalar.activation(out=Rb[h].t[:], in_=Rf[h].t[:], func=AF.Copy), r=[Rf[h]], w=[Rb[h]])

        for sweep in (0, 1):
            for h in range(4):
                self.op(dv, lambda: nc.vector.memset(Rf[h].t[:], 0.0), w=[Rf[h]])
                self.op(dv, lambda: nc.vector.memset(Rb[h].t[:], 0.0), w=[Rb[h]])
            order = list(range(NCH - 1, -1, -1)) if sweep == 0 else list(range(NCH))
            curg, QG, KG = -1, None, None
            prev_post = [None]
            for c in order:
                g, off = c // 4, (c % 4) * 128
                csl = slice(c * 128, (c + 1) * 128)
                if g != curg:
                    curg = g
                    QG = QGp.next()
                    self.dma(out=QG.t[:], in_=self.QTs[:, :, self.tsl(g)].rearrange("c p t -> p c t"), r=[self.k_RQ[g]], w=[QG], sb=QG)
                    if sweep == 1:
                        KG = KGp.next()
                        self.dma(out=KG.t[:], in_=self.KTs[:, :, self.tsl(g)].rearrange("c p t -> p c t"), r=[self.k_RQ[g]], w=[KG], sb=KG)
                Ktm, Vc = Ktp.next(), Vcp.next()
                self.dma(out=Ktm.t[:], in_=self.RK[csl, :], r=[self.k_R[c]], w=[Ktm], sb=Ktm)
                self.dma(out=Vc.t[:], in_=self.RV[csl, :], r=[self.k_R[c]], w=[Vc], sb=Vc)
                if sweep == 1:
                    SG, OBt = SGp.next(), OBp.next()
                    self.dma(out=SG.t[:], in_=self.RG[csl, :], r=[self.k_R[c]], w=[SG], sb=SG)
                    self.dma(out=OBt.t[:], in_=self.OB[csl, :], r=[self.k_OB[c]], w=[OBt], sb=OBt)
                Qx = Qxp.next()
                XI = XIB if sweep == 0 else XIF
                self.op(dv, lambda: nc.vector.tensor_tensor(out=Qx.t[:], in0=QG.t[:, :, off:off + 128], in1=XI.t[:], op=ALU.mult), r=[QG, XI], w=[Qx])
                Kz = make_kz(c, Ktm, ZB if sweep == 0 else ZF)
                if sweep == 0:
                    OBt = OBp.next()
                    for h in range(4):
                        bank = B[4 + h]

                        def f():
                            for dk in range(2):
                                ins = self.mm(bank.t, Qx.t[:, 2 * h + dk, :], Rb[h].t[:, dk, :], dk == 0, dk == 1)
                            return ins
                        self.op(pe, f, r=[Qx, Rb[h]], w=[bank])
                    update_state(c, Kz, Vc, CDB)
                    for h in range(4):
                        bank = B[4 + h]
                        if h % 2:
                            self.op(act, lambda: nc.scalar.activation(out=OBt.t[:, h * 512:(h + 1) * 512], in_=bank.t, func=AF.Copy), r=[bank], w=[OBt])
                        else:
                            self.op(dv, lambda: nc.vector.tensor_copy(out=OBt.t[:, h * 512:(h + 1) * 512], in_=bank.t), r=[bank], w=[OBt])
                    self.dma(out=self.OB[csl, :], in_=OBt.t[:], r=[OBt], w=[self.k_OB[c]], sb=OBt)
                    continue
                Sms = []
                for h in range(4):
                    bS = B[h]

                    def f():
                        for dk in range(2):
                            ins = self.mm(bS.t[:, 0:128], KG.t[:, 2 * h + dk, off:off + 128], QG.t[:, 2 * h + dk, off:off + 128], dk == 0, dk == 1)
                        return ins
                    self.op(pe, f, r=[KG, QG], w=[bS])
                for h in range(4):
                    Sm = Smp.next()
                    self.op(dv, lambda: nc.vector.tensor_tensor(out=Sm.t[:], in0=B[h].t[:, 0:128], in1=DT.t[:, h, :], op=ALU.mult), r=[B[h], DT], w=[Sm])
                    Sms.append(Sm)
                for h in range(4):
                    bO, Sm = B[4 + h], Sms[h]

                    def f2():
                        self.mm(bO.t, Sm.t[:], Vc.t[:, h * 512:(h + 1) * 512], True, False)
                        for dk in range(2):
                            ins = self.mm(bO.t, Qx.t[:, 2 * h + dk, :], Rb[h].t[:, dk, :], False, dk == 1)
                        return ins
                    self.op(pe, f2, r=[Sm, Vc, Qx, Rb[h]], w=[bO])
                Of, ss = Ofp.next(), ssp.next()
                for h in range(4):
                    bO = B[4 + h]
                    self.op(dv, lambda: nc.vector.tensor_tensor(out=Of.t[:, h, :], in0=bO.t, in1=OBt.t[:, h * 512:(h + 1) * 512], op=ALU.add), r=[bO, OBt], w=[Of])
                if prev_post[0] is not None:
                    prev_post[0]()
                update_state(c, Kz, Vc, CDF)

                def post(Of=Of, ss=ss, SG=SG, g=g, off=off):
                    for h in range(4):
                        self.op(act, lambda: nc.scalar.activation(out=junk.t[:], in_=Of.t[:, h, :], func=AF.Square, accum_out=ss.t[:, h:h + 1]), r=[Of], w=[junk, ss])
                    r1 = r1p.next()
                    self.rsqrt(ss.t[:], [ss], r1.t[:], r1, 512 * 1e-6, 4)
                    self.op(dv, lambda: nc.vector.tensor_scalar(out=r1.t[:], in0=r1.t[:], scalar1=math.sqrt(512.0), scalar2=None, op0=ALU.mult), r=[r1], w=[r1])
                    Y = Yp.next()
                    for h in range(4):
                        self.op(dv, lambda: nc.vector.scalar_tensor_tensor(out=Y.t[:, h * 512:(h + 1) * 512], in0=Of.t[:, h, :], scalar=r1.t[:, h:h + 1], in1=SG.t[:, h * 512:(h + 1) * 512], op0=ALU.mult, op1=ALU.mult), r=[Of, r1, SG], w=[Y])
                    YT = YTp.next()
                    b0, b1 = B[0], B[1]

                    def f3():
                        for q in range(16):
                            ins = nc.tensor.transpose(self.psbf[:, q // 8, (q % 8) * 128:(q % 8 + 1) * 128], Y.t[:, q * 128:(q + 1) * 128], self.identb.t[:])
                        return ins
                    self.op(pe, f3, r=[Y, self.identb], w=[b0, b1])
                    self.op(act, lambda: nc.scalar.activation(out=YT.t[:, 0:8, :], in_=self.psbf[:, 0, :], func=AF.Copy), r=[b0], w=[YT])
                    self.op(dv, lambda: nc.vector.tensor_copy(out=YT.t[:, 8:16, :], in_=self.psbf[:, 1, :]), r=[b1], w=[YT])
                    self.dma(out=self.AT[g, :, :, off:off + 128], in_=YT.t[:], r=[YT], w=[self.k_AT[g]], sb=YT)
                prev_post[0] = post
            if sweep == 1 and prev_post[0] is not None:
                prev_post[0]()
                prev_post[0] = None
        self.end_phase()

    def phase_PA(self, l):
        nc = self.nc
        j = l // 2
        attn = (l % 2 == 0)
        nin = 8 if attn else 16
        self.begin_phase()
        W = self.sb([128, nin, D], BF16, "W")
        stage = self.rot(2, [128, 2048], F32, "stg")
        self.load_weight(self.w_o[j] if attn else self.w_rout[j], W, stage)
        Ap = self.rot(2, [128, nin, TT], BF16, "A")
        Xp = self.rot(2, [128, 8, TT], F32, "X")
        Mp = self.rot(2, [128, 8, TT], F32, "M")
        sq = self.sb([128, 8, TT], BF16, "sq")
        rs = self.rot(2, [128, TT], F32, "rs")
        Hp = self.rot(2, [128, 8, TT], BF16, "H")

        def S1(tt):
            ts = self.tsl(tt)
            st = {}

            def p0():
                A = Ap.next()
                self.dma(out=A.t[:], in_=self.AT[tt, :, 0:nin, :], r=[self.k_AT[tt]], w=[A], sb=A)
                X = Xp.next()
                self.load_xT(X, tt)
                st["A"], st["X"], st["M"] = A, X, Mp.next()
            pcs = [p0] + [(lambda dc=dc: self.proj_piece(W, nin, st["A"], st["M"], dc)) for dc in range(8)]
            return pcs, st

        def S2(tt, st):
            ts = self.tsl(tt)
            c2 = {}

            def a():
                c2["b1"] = self.norm_ss(st["M"], 8, sq)

            def b():
                X, M = st["X"], st["M"]
                r1 = rs.next()
                self.rsqrt(c2["b1"].t, [c2["b1"]], r1.t[:], r1, D * 1e-6)
                self.resid_norm(X, M, (l * 4 + 1) * 8, r1, M)
                self.store_xT(X, tt)
                c2["b2"] = self.norm_ss(X, 8, sq)

            def c():
                X = st["X"]
                r2 = rs.next()
                self.rsqrt(c2["b2"].t, [c2["b2"]], r2.t[:], r2, D * 1e-6)
                H = Hp.next()
                self.scale_norm(H, X, (l * 4 + 2) * 8, r2)
                self.dma(out=self.H2s[tt], in_=H.t[:], r=[H], w=[self.k_H2[tt]], sb=H)
            return [lambda: None, lambda: None, a, lambda: None, b, lambda: None, lambda: None, c]

        pcs, st = S1(0)
        for p_ in pcs:
            p_()
        for tt in range(self.NTT):
            if tt + 1 < self.NTT:
                npcs, nst = S1(tt + 1)
            else:
                npcs, nst = [], None
            self.interleave(npcs, S2(tt, st))
            st = nst
        self.end_phase()

    def phase_PB(self, l):
        nc = self.nc
        self.begin_phase()
        W = self.sb([128, 8, DFF], BF16, "W")
        stage = self.rot(3, [128, 2048], F32, "stg")
        self.load_weight(self.w_1[l], W, stage)
        Hp = self.rot(2, [128, 8, TT], BF16, "H")
        Up = self.rot(3, [128, 8, TT], BF16, "U")
        tp = self.rot(4, [128, TT], F32, "tmp")
        for tt in range(self.NTT):
            ts = self.tsl(tt)
            H = Hp.next()
            self.dma(out=H.t[:], in_=self.H2s[tt], r=[self.k_H2[tt]], w=[H], sb=H)
            for fg in range(4):
                U = Up.next()
                for fi in range(8):
                    fc = fg * 8 + fi
                    bank = self.psn()

                    def f():
                        for k in range(8):
                            ins = self.mm(bank.t, W.t[:, k, fc * 128:(fc + 1) * 128], H.t[:, k, :], k == 0, k == 7)
                        return ins
                    self.op(self.pe, f, r=[W, H], w=[bank])
                    t = tp.next()
                    self.op(self.act, lambda: nc.scalar.activation(out=t.t[:], in_=bank.t, func=AF.Relu), r=[bank], w=[t])
                    E = self.ew()
                    self.op(E, lambda: E.h.tensor_tensor(out=U.t[:, fi, :], in0=t.t[:], in1=t.t[:], op=ALU.mult), r=[t], w=[U])
                self.dma(out=self.Us[tt, :, fg * 8:(fg + 1) * 8, :], in_=U.t[:], r=[U], w=[self.k_U[tt]], sb=U)
        self.end_phase()

    def phase_PC(self, l):
        nc = self.nc
        final = (l == self.depth - 1)
        self.begin_phase()
        W2 = self.sb([128, 32, D], BF16, "W2")
        Wg = self.sb([128, 8, D], BF16, "Wg")
        Wp = self.sb([128, 2, D], BF16, "Wp")
        stage = self.rot(2, [128, 512], F32, "stg")
        self.load_weight(self.w_2[l], W2, stage, 512)
        self.load_weight(self.w_pg[l], Wg, stage, 512)
        self.load_weight(self.w_pp[l], Wp, stage, 512)
        Up = self.rot(1, [128, 32, TT], BF16, "U")
        Xp = self.rot(1, [128, 8, TT], F32, "X")
        Mp = self.rot(2, [128, 8, TT], F32, "M")
        sq = self.sb([128, 8, TT], BF16, "sq")
        rs = self.rot(2, [128, TT], F32, "rs")
        pin = self.rot(1, [128, 4, PLE], F32, "pin")
        pTp = self.rot(1, [128, 2, TT], BF16, "pT")
        tg = self.rot(1, [128, TT], F32, "tg")
        tq = self.rot(1, [128, TT], F32, "tq")

        def S1(tt):
            ts = self.tsl(tt)
            st = {}

            def p0():
                U = Up.next()
                for fg in range(4):
                    self.dma(out=U.t[:, fg * 8:(fg + 1) * 8, :], in_=self.Us[tt, :, fg * 8:(fg + 1) * 8, :], r=[self.k_U[tt]], w=[U], sb=U)
                st["U"], st["M"] = U, Mp.next()
            pcs = [p0] + [(lambda dc=dc: self.proj_piece(W2, 32, st["U"], st["M"], dc)) for dc in range(8)]
            return pcs, st

        def S2(tt, st):
            ts = self.tsl(tt)
            M = st["M"]
            c2 = {}

            def a():
                X = Xp.next()
                self.load_xT(X, tt)
                pi = pin.next()
                self.dma(out=pi.t[:], in_=self.p_in[l, ts, :].rearrange("(s p) f -> p s f", p=128), w=[pi], sb=pi)
                pT = pTp.next()
                for kc in range(2):
                    bank = self.psn()

                    def f():
                        for s_ in range(4):
                            ins = nc.tensor.transpose(bank.t[:, s_ * 128:(s_ + 1) * 128], pi.t[:, s_, kc * 128:(kc + 1) * 128], self.identf.t[:])
                        return ins
                    self.op(self.pe, f, r=[pi, self.identf], w=[bank])
                    self.op(self.dve, lambda: nc.vector.tensor_copy(out=pT.t[:, kc, :], in_=bank.t), r=[bank], w=[pT])
                c2["X"], c2["pT"] = X, pT
                c2["b1"] = self.norm_ss(M, 8, sq)

            def b():
                X = c2["X"]
                r1 = rs.next()
                self.rsqrt(c2["b1"].t, [c2["b1"]], r1.t[:], r1, D * 1e-6)
                self.resid_norm(X, M, (l * 4 + 3) * 8, r1, M)
                c2["b2"] = self.norm_ss(X, 8, sq)

            def c():
                X = c2["X"]
                r2 = rs.next()
                self.rsqrt(c2["b2"].t, [c2["b2"]], r2.t[:], r2, D * 1e-6)
                self.scale_norm(sq, X, None, r2)

            def gate(dc):
                X, pT = c2["X"], c2["pT"]
                bg, bp = self.psn(), self.psn()

                def f():
                    for k in range(8):
                        ins = self.mm(bg.t, Wg.t[:, k, dc * 128:(dc + 1) * 128], sq.t[:, k, :], k == 0, k == 7)
                    return ins
                self.op(self.pe, f, r=[Wg, sq], w=[bg])

                def f2():
                    for k in range(2):
                        ins = self.mm(bp.t, Wp.t[:, k, dc * 128:(dc + 1) * 128], pT.t[:, k, :], k == 0, k == 1)
                    return ins
                self.op(self.pe, f2, r=[Wp, pT], w=[bp])
                g = tg.next()
                self.op(self.act, lambda: nc.scalar.activation(out=g.t[:], in_=bg.t, func=AF.Sigmoid), r=[bg], w=[g])
                q = tq.next()
                self.op(self.dve, lambda: nc.vector.tensor_tensor(out=q.t[:], in0=bp.t, in1=g.t[:], op=ALU.mult), r=[bp, g], w=[q])
                self.op(self.dve, lambda: nc.vector.tensor_tensor(out=X.t[:, dc, :], in0=X.t[:, dc, :], in1=q.t[:], op=ALU.add), r=[X, q], w=[X])

            def fin():
                X = c2["X"]
                if not final:
                    self.store_xT(X, tt)
                    return
                Yo = M
                for s_ in range(4):
                    for hf in range(2):
                        bank = self.psn()

                        def f():
                            for q4 in range(4):
                                ins = nc.tensor.transpose(bank.t[:, q4 * 128:(q4 + 1) * 128], X.t[:, hf * 4 + q4, s_ * 128:(s_ + 1) * 128], self.identf.t[:])
                            return ins
                        self.op(self.pe, f, r=[X, self.identf], w=[bank])
                        o_ap = Yo.t[:, 2 * s_ + hf, :]
                        if hf:
                            self.op(self.act, lambda: nc.scalar.activation(out=o_ap, in_=bank.t, func=AF.Copy), r=[bank], w=[Yo])
                        else:
                            self.op(self.dve, lambda: nc.vector.tensor_copy(out=o_ap, in_=bank.t), r=[bank], w=[Yo])
                self.dma(out=self.y_out[ts, :].rearrange("(s p) (h f) -> p s h f", p=128, h=2), in_=Yo.t[:].rearrange("p (s h) f -> p s h f", h=2), r=[Yo], w=[], sb=Yo)

            def g2(d0):
                def run():
                    for dc in range(d0, d0 + 2):
                        gate(dc)
                    if d0 == 6:
                        fin()
                return run
            return [a, lambda: None, b, lambda: None, c, g2(0), g2(2), g2(4), g2(6)]

        pcs, st = S1(0)
        for p_ in pcs:
            p_()
        for tt in range(self.NTT):
            if tt + 1 < self.NTT:
                npcs, nst = S1(tt + 1)
            else:
                npcs, nst = [], None
            self.interleave(npcs, S2(tt, st))
            st = nst
        self.end_phase()


def _const_tables(NT, seqs):
    assert sum(seqs) == NT
    pos = np.concatenate([np.arange(s) for s in seqs]).astype(np.float32)
    sid = np.concatenate([np.full(s, i) for i, s in enumerate(seqs)])
    inv = (1.0 / (np.float32(10000.0) ** (np.arange(0, 64, 2, dtype=np.float32) / np.float32(64)))).astype(np.float32)
    angA = pos[None, :] * inv[np.arange(128) % 32][:, None]
    ropeA = np.stack([np.cos(angA), np.sin(angA)]).astype(np.float32)
    angle = (1.0 / (np.float32(10000.0) ** np.linspace(0.0, 1.0, 128, dtype=np.float32))).astype(np.float32)
    angR = pos[None, :] * angle[:, None]
    ropeR = np.stack([np.cos(angR), np.sin(angR)]).astype(np.float32)
    NTT, NCH = NT // 512, NT // 128
    NKP = NCH // 2
    qs = sid[np.arange(NTT) * 512]
    ks = sid[np.arange(NKP) * 256]
    mb = np.where(qs[:, None] == ks[None, :], 0.0, NEG).astype(np.float32).reshape(1, -1)
    maskb = np.repeat(mb, 128, axis=0)
    k = np.arange(128)[:, None].astype(np.float32)
    q = np.arange(128)[None, :].astype(np.float32)
    BIG = np.float32(1.0e6)
    TFm = np.where(q >= k, q - k, BIG).astype(np.float32)
    TBm = np.where(k > q, k - q, BIG).astype(np.float32)
    posq = np.stack([np.repeat(q + 1.0, 128, axis=0), np.repeat(128.0 - q, 128, axis=0)]).astype(np.float32)
    pcol = np.arange(128, dtype=np.float32)
    col = np.stack([127.0 - pcol, pcol, np.full(128, 128.0, np.float32), np.zeros(128, np.float32)], axis=1).astype(np.float32)
    csid = sid[np.arange(NCH) * 128]
    last = np.ones(NCH, bool)
    last[:-1] = csid[1:] != csid[:-1]
    first = np.ones(NCH, bool)
    first[1:] = csid[1:] != csid[:-1]
    mfb = np.stack([np.repeat(np.where(last, 0.0, 1.0)[None, :], 128, axis=0), np.repeat(np.where(first, 0.0, 1.0)[None, :], 128, axis=0)]).astype(np.float32)
    return dict(c_ropeA=ropeA, c_ropeR=ropeR, c_maskb=maskb, c_tftb=np.stack([TFm, TBm]), c_pos=posq, c_col=col, c_mfb=mfb,
                c_ident=np.eye(128, dtype=np.float32))


def _perm_attn():
    idx = []
    for jj in range(4):
        for half in range(2):
            for r in range(4):
                m = 4 * jj + r
                idx.extend(range(m * 64 + half * 32, m * 64 + half * 32 + 32))
    return np.array(idx)


def _perm_ret():
    idx = []
    for h in range(4):
        idx.extend(range(h * 256, (h + 1) * 256, 2))
        idx.extend(range(h * 256 + 1, (h + 1) * 256, 2))
    return np.array(idx)


def _shared_inputs(inp):
    f = lambda a: np.ascontiguousarray(np.asarray(a, dtype=np.float32))
    pa, prr = _perm_attn(), _perm_ret()
    wqkv = f(inp["attn_w_qkv"])
    wqkv = np.concatenate([wqkv[:, :, 0:D][:, :, pa], wqkv[:, :, D:2 * D][:, :, pa], wqkv[:, :, 2 * D:]], axis=2)
    wrin = f(inp["ret_w_in"])
    wrin = np.concatenate([wrin[:, :, 0:D][:, :, prr], wrin[:, :, D:2 * D][:, :, prr], wrin[:, :, 2 * D:]], axis=2)
    norms = [f(inp[k]) for k in ("norm_pre_mix", "norm_post_mix", "norm_pre_mlp", "norm_post_mlp")]
    gains = np.zeros((128, 128), np.float32)
    for l in range(4):
        for k in range(4):
            gains[:, (l * 4 + k) * 8:(l * 4 + k + 1) * 8] = norms[k][l].reshape(8, 128).T
    lam = np.stack([f(inp[k]) for k in ("attn_lambda_q1", "attn_lambda_k1", "attn_lambda_q2", "attn_lambda_k2")], axis=1)
    lamv = np.repeat(lam.reshape(1, -1), 128, axis=0)
    decay = np.repeat(f(inp["ret_decay"]).reshape(1, -1), 128, axis=0)
    return dict(gains=gains, subln=np.ascontiguousarray(f(inp["attn_subln"]).T), lamv=np.ascontiguousarray(lamv),
                decay=np.ascontiguousarray(decay), w_qkv=np.ascontiguousarray(wqkv), w_o=f(inp["attn_w_o"]),
                w_rin=np.ascontiguousarray(wrin), w_rout=f(inp["ret_w_out"]), w_1=f(inp["mlp_w_in"]), w_2=f(inp["mlp_w_out"]),
                w_pp=f(inp["ple_w_proj"]), w_pg=f(inp["ple_w_gate"]))


_NC_CACHE = {}


def run_cores(inp, core_specs, NT, depth=4, debug=False):
    key = (NT, depth, debug)
    if key not in _NC_CACHE:
        _NC_CACHE[key] = KB(NT, depth, debug).build()
    nc = _NC_CACHE[key]
    shared = _shared_inputs(inp)
    in_maps = []
    for x, p, seqs in core_specs:
        m = dict(shared)
        m.update(_const_tables(NT, seqs))
        m["x"] = np.ascontiguousarray(x, dtype=np.float32)
        m["p"] = np.ascontiguousarray(p, dtype=np.float32)
        in_maps.append(m)
    res = run_bass_kernel_spmd(nc, in_maps, core_ids=list(range(len(in_maps))))
    if debug:
        return res.results
    return [r["y"] for r in res.results]


def kernel(**inputs):
    xp = np.asarray(inputs["x_prompt"], dtype=np.float32)
    xs = np.asarray(inputs["x_sample"], dtype=np.float32)
    pp = np.asarray(inputs["p_prompt"], dtype=np.float32)
    psm = np.asarray(inputs["p_sample"], dtype=np.float32)
    NT = 8192
    specs = []
    for c in range(4):
        specs.append((xp[4 * c:4 * c + 4].reshape(NT, D), pp[:, 4 * c:4 * c + 4].reshape(4, NT, PLE), [2048] * 4))
    for c in range(4):
        specs.append((xs[c], psm[:, c], [8192]))
    ys = run_cores(inputs, specs, NT, 4)
    y_prompt = np.stack([ys[c].reshape(4, 2048, D) for c in range(4)]).reshape(16, 2048, D).astype(np.float32)
    y_sample = np.stack([ys[4 + c] for c in range(4)]).astype(np.float32)
    return (y_prompt, y_sample)
```

```python
import math
from contextlib import ExitStack
import numpy as np
import concourse.bass as bass
import concourse.mybir as mybir
from concourse.bass_utils import run_bass_kernel_spmd

F32 = mybir.dt.float32
BF16 = mybir.dt.bfloat16
AF = mybir.ActivationFunctionType
ALU = mybir.AluOpType
AX = mybir.AxisListType

D = 1024
DFF = 4096
PLE = 256
TT = 512
NEG = -30000.0


class Sem:
    def __init__(self, nc, name):
        self.h = nc.alloc_semaphore(name)
        self.cnt = 0


class Eng:
    def __init__(self, nc, name, h, same_raw):
        self.name = name
        self.h = h
        self.sem = Sem(nc, "e_" + name)
        self.waited = {}
        self.same_raw = same_raw


class Buf:
    def __init__(self, t=None):
        self.t = t
        self.w = {}
        self.r = {}
        self.dsem = None


class Rot:
    def __init__(self, items):
        self.items = items
        self.i = 0

    def next(self):
        b = self.items[self.i % len(self.items)]
        self.i += 1
        return b


class KB:
    def __init__(self, NT, depth=4, debug=False):
        self.debug = debug
        self.nc = nc = bass.Bass("TRN2", target_bir_lowering=False)
        self.NT = NT
        self.NTT = NT // TT
        self.NCH = NT // 128
        self.depth = depth
        self.pe = Eng(nc, "pe", nc.tensor, False)
        self.act = Eng(nc, "act", nc.scalar, True)
        self.dve = Eng(nc, "dve", nc.vector, True)
        self.pool = Eng(nc, "pool", nc.gpsimd, True)
        self.sp = Eng(nc, "sp", nc.sync, False)
        self.engs = [self.pe, self.act, self.dve, self.pool, self.sp]
        self.all_sems = [e.sem for e in self.engs]
        self.free_dsems = []
        self.phase_dsems = []
        self.uid = 0
        self.st = None
        self.alt = 0

    def _need(self, r, w):
        need = {}
        for b in r:
            for S, v in b.w.items():
                if v > need.get(S, 0):
                    need[S] = v
        for b in w:
            for S, v in b.w.items():
                if v > need.get(S, 0):
                    need[S] = v
            for S, v in b.r.items():
                if v > need.get(S, 0):
                    need[S] = v
        return need

    def _waits(self, E, need, r):
        for S, v in need.items():
            if S is E.sem:
                continue
            if v > E.waited.get(S, 0):
                E.h.wait_ge(S.h, v)
                E.waited[S] = v
        if E.same_raw:
            raw = 0
            for b in r:
                v = b.w.get(E.sem, 0)
                if v > raw:
                    raw = v
            if raw > E.sem.cnt - 2 and raw > E.waited.get(E.sem, 0):
                E.h.wait_ge(E.sem.h, raw)
                E.waited[E.sem] = raw

    def op(self, E, fn, r=(), w=()):
        self._waits(E, self._need(r, w), r)
        ins = fn()
        E.sem.cnt += 1
        ins.then_inc(E.sem.h, 1)
        v = E.sem.cnt
        for b in r:
            b.r[E.sem] = v
        for b in w:
            b.w[E.sem] = v
        return ins

    def dma(self, out, in_, r=(), w=(), sb=None):
        Q = self.sp
        if sb.dsem is None:
            if self.free_dsems:
                sb.dsem = self.free_dsems.pop()
            else:
                self.uid += 1
                sb.dsem = Sem(self.nc, "d%d" % self.uid)
                self.all_sems.append(sb.dsem)
            self.phase_dsems.append(sb.dsem)
        self._waits(Q, self._need(r, w), ())
        ins = Q.h.dma_start(out=out, in_=in_)
        S = sb.dsem
        S.cnt += 16
        ins.then_inc(S.h, 16)
        for b in r:
            b.r[S] = S.cnt
        for b in w:
            b.w[S] = S.cnt

    def barrier(self):
        for E in self.engs:
            for S in self.all_sems:
                if S is E.sem:
                    continue
                if S.cnt > E.waited.get(S, 0):
                    E.h.wait_ge(S.h, S.cnt)
                    E.waited[S] = S.cnt

    def sb(self, shape, dt, name="t"):
        self.uid += 1
        t = self.st.enter_context(self.nc.sbuf_tensor("%s_%d" % (name, self.uid), list(shape), dt))
        return Buf(t)

    def rot(self, n, shape, dt, name="r"):
        return Rot([self.sb(shape, dt, name) for _ in range(n)])

    def begin_phase(self):
        self.st = ExitStack()
        self.st.__enter__()
        self.rt = self.rot(2, [128, TT], F32, "rt")

    def rsqrt(self, src_ap, src_bufs, out_ap, out_buf, addc, n=TT):
        nc = self.nc
        t = self.rt.next()
        self.op(self.act, lambda: nc.scalar.activation(out=t.t[:, :n], in_=src_ap, func=AF.Ln, bias=self.cbias.t[:, self.cbias_col[addc]:self.cbias_col[addc] + 1]), r=src_bufs + [self.cbias], w=[t])
        self.op(self.act, lambda: nc.scalar.activation(out=out_ap, in_=t.t[:, :n], func=AF.Exp, scale=-0.5), r=[t], w=[out_buf])

    def end_phase(self):
        self.barrier()
        self.free_dsems.extend(self.phase_dsems)
        self.phase_dsems = []
        self.st.__exit__(None, None, None)
        self.st = None

    def psn(self):
        b = self.psb[self.psi % 8]
        self.psi += 1
        return b

    def psn2(self):
        if self.psi % 2:
            self.psi += 1
        a = self.psb[self.psi % 8]
        b = self.psb[(self.psi + 1) % 8]
        self.psi += 2
        return a, b

    def ew(self):
        self.alt += 1
        return self.dve if self.alt % 2 else self.pool

    def mm(self, out, lhsT, rhs, start=True, stop=True):
        return self.nc.tensor.matmul(out, lhsT=lhsT, rhs=rhs, start=start, stop=stop)

    def declare(self):
        nc, NT, NCH = self.nc, self.NT, self.NCH
        di = lambda n, s: nc.dram_tensor(n, list(s), F32, kind="ExternalInput").ap()
        self.x_in = di("x", [NT, D])
        self.p_in = di("p", [self.depth, NT, PLE])
        self.gains = di("gains", [128, 16 * 8])
        self.subln = di("subln", [128, 2])
        self.lamv = di("lamv", [128, 2 * 4 * 64])
        self.decay = di("decay", [128, 16])
        self.w_qkv = di("w_qkv", [2, D, 3 * D])
        self.w_o = di("w_o", [2, D, D])
        self.w_rin = di("w_rin", [2, D, 6 * D])
        self.w_rout = di("w_rout", [2, 2 * D, D])
        self.w_1 = di("w_1", [4, D, DFF])
        self.w_2 = di("w_2", [4, DFF, D])
        self.w_pp = di("w_pp", [4, PLE, D])
        self.w_pg = di("w_pg", [4, D, D])
        self.c_ident = di("c_ident", [128, 128])
        self.c_ropeA = di("c_ropeA", [2, 128, NT])
        self.c_ropeR = di("c_ropeR", [2, 128, NT])
        self.c_maskb = di("c_maskb", [128, self.NTT * (NCH // 2)])
        self.c_tftb = di("c_tftb", [2, 128, 128])
        self.c_pos = di("c_pos", [2, 128, 128])
        self.c_col = di("c_col", [128, 4])
        self.c_mfb = di("c_mfb", [2, 128, NCH])
        self.y_out = nc.dram_tensor("y", [NT, D], F32, kind="ExternalOutput").ap()
        ds = lambda n, s, dt: nc.dram_tensor(n, list(s), dt, kind="ExternalOutput" if self.debug else "Internal").ap()
        self.xT = ds("s_xT", [self.NTT, 128, 8, TT], F32)
        self.QTs = ds("s_QT", [8, 128, NT], BF16)
        self.KTs = ds("s_KT", [8, 128, NT], BF16)
        self.VS = ds("s_V", [NT, D], BF16)
        self.AT = ds("s_AT", [self.NTT, 128, 16, TT], BF16)
        self.H2s = ds("s_H2", [self.NTT, 128, 8, TT], BF16)
        self.Us = ds("s_U", [self.NTT, 128, 32, TT], BF16)
        self.RK = ds("s_RK", [NT, D], BF16)
        self.RV = ds("s_RV", [NT, 2 * D], BF16)
        self.RG = ds("s_RG", [NT, 2 * D], BF16)
        self.OB = ds("s_OB", [NT, 2 * D], BF16)
        mk = lambda n: [Buf() for _ in range(n)]
        self.k_xT = mk(self.NTT)
        self.k_QT = mk(8)
        self.k_KT = mk(8)
        self.k_V = mk(1)
        self.k_AT = mk(self.NTT)
        self.k_H2 = mk(self.NTT)
        self.k_U = mk(self.NTT)
        self.k_R = mk(NCH)
        self.k_RQ = mk(self.NTT)
        self.k_OB = mk(NCH)

    def build(self):
        nc = self.nc
        self.declare()
        with ExitStack() as gst:
            self.st = gst
            ps = nc.alloc_psum_tensor("ps", [128, 8, 512], F32)
            self.ps = ps
            self.psbf = ps[:, :, :].bitcast(BF16)
            self.psb = [Buf(ps[:, b, :]) for b in range(8)]
            self.psi = 0
            self.ones = self.sb([128, 128], BF16, "ones")
            self.cbias = self.sb([128, 4], F32, "cbias")
            self.cbias_col = {D * 1e-6: 0, 128 * 1e-5: 1, 512 * 1e-6: 2}
            for v, cidx in self.cbias_col.items():
                self.op(self.dve, lambda: nc.vector.memset(self.cbias.t[:, cidx:cidx + 1], float(v)), w=[self.cbias])
            self.onesf = self.sb([128, 128], F32, "onesf")
            self.identf = self.sb([128, 128], F32, "identf")
            self.identb = self.sb([128, 128], BF16, "identb")
            self.gs = self.sb([128, 128], F32, "gs")
            self.sgs = self.sb([128, 2], F32, "sgs")
            self.lamt = self.sb([128, 8], F32, "lamt")
            self.lg = self.sb([128, 16], F32, "lg")
            self.maskb = self.sb([128, self.NTT * (self.NCH // 2)], F32, "maskb")
            lv = self.sb([128, 512], F32, "lv")
            pr = self.sb([128, 64], F32, "pr")
            sraw = self.sb([128, 2], F32, "sraw")
            dv, act = self.dve, self.act
            self.dma(out=self.identf.t[:], in_=self.c_ident, w=[self.identf], sb=self.identf)
            self.dma(out=self.gs.t[:], in_=self.gains, w=[self.gs], sb=self.gs)
            self.dma(out=sraw.t[:], in_=self.subln, w=[sraw], sb=sraw)
            self.dma(out=lv.t[:], in_=self.lamv, w=[lv], sb=lv)
            self.dma(out=self.lg.t[:], in_=self.decay, w=[self.lg], sb=self.lg)
            self.dma(out=self.maskb.t[:], in_=self.c_maskb, w=[self.maskb], sb=self.maskb)
            self.op(dv, lambda: nc.vector.memset(self.ones.t[:], 1.0), w=[self.ones])
            self.op(dv, lambda: nc.vector.memset(self.onesf.t[:], 1.0), w=[self.onesf])
            self.op(dv, lambda: nc.vector.tensor_copy(out=self.identb.t[:], in_=self.identf.t[:]), r=[self.identf], w=[self.identb])
            self.op(dv, lambda: nc.vector.tensor_scalar(out=self.gs.t[:], in0=self.gs.t[:], scalar1=32.0, scalar2=None, op0=ALU.mult), r=[self.gs], w=[self.gs])
            self.op(act, lambda: nc.scalar.activation(out=self.lg.t[:], in_=self.lg.t[:], func=AF.Exp), r=[self.lg], w=[self.lg])
            self.op(dv, lambda: nc.vector.tensor_scalar(out=self.lg.t[:], in0=self.lg.t[:], scalar1=-1.0, scalar2=None, op0=ALU.mult), r=[self.lg], w=[self.lg])
            for j in range(2):
                lam_init = 0.8 - 0.6 * math.exp(-0.3 * (2 * j))
                for e in range(2):
                    a0 = (j * 4 + 2 * e) * 64
                    self.op(dv, lambda: nc.vector.tensor_tensor(out=pr.t[:], in0=lv.t[:, a0:a0 + 64], in1=lv.t[:, a0 + 64:a0 + 128], op=ALU.mult), r=[lv], w=[pr])
                    self.op(dv, lambda: nc.vector.reduce_sum(out=self.lamt.t[:, j * 4 + 2 + e:j * 4 + 3 + e], in_=pr.t[:], axis=AX.X), r=[pr], w=[self.lamt])
                self.op(act, lambda: nc.scalar.activation(out=self.lamt.t[:, j * 4 + 2:j * 4 + 4], in_=self.lamt.t[:, j * 4 + 2:j * 4 + 4], func=AF.Exp), r=[self.lamt], w=[self.lamt])
                self.op(dv, lambda: nc.vector.tensor_tensor(out=self.lamt.t[:, j * 4:j * 4 + 1], in0=self.lamt.t[:, j * 4 + 2:j * 4 + 3], in1=self.lamt.t[:, j * 4 + 3:j * 4 + 4], op=ALU.subtract), r=[self.lamt], w=[self.lamt])
                self.op(dv, lambda: nc.vector.tensor_scalar(out=self.lamt.t[:, j * 4:j * 4 + 1], in0=self.lamt.t[:, j * 4:j * 4 + 1], scalar1=lam_init, scalar2=None, op0=ALU.add), r=[self.lamt], w=[self.lamt])
                self.op(dv, lambda: nc.vector.tensor_scalar(out=self.lamt.t[:, j * 4 + 1:j * 4 + 2], in0=self.lamt.t[:, j * 4:j * 4 + 1], scalar1=-1.0, scalar2=None, op0=ALU.mult), r=[self.lamt], w=[self.lamt])
                self.op(dv, lambda: nc.vector.tensor_scalar(out=self.sgs.t[:, j:j + 1], in0=sraw.t[:, j:j + 1], scalar1=math.sqrt(128.0) * (1.0 - lam_init), scalar2=None, op0=ALU.mult), r=[sraw], w=[self.sgs])
            self.barrier()
            for l in range(self.depth):
                if l % 2 == 0:
                    self.phase_PD_attn(l)
                    self.phase_attn(l)
                else:
                    self.phase_PD_ret(l)
                    self.phase_ret(l)
                self.phase_PA(l)
                self.phase_PB(l)
                self.phase_PC(l)
            self.barrier()
            self.st = None
        return nc

    def load_weight(self, Wd, Wb, stage, sw=2048):
        nc = self.nc
        K, Fd = Wd.shape
        i = 0
        for c in range(K // 128):
            for f0 in range(0, Fd, sw):
                fs = min(sw, Fd - f0)
                s = stage.next()
                self.dma(out=s.t[:, :fs], in_=Wd[c * 128:(c + 1) * 128, f0:f0 + fs], w=[s], sb=s)
                k = i % 3
                i += 1
                if k == 0:
                    self.op(self.dve, lambda: nc.vector.tensor_copy(out=Wb.t[:, c, f0:f0 + fs], in_=s.t[:, :fs]), r=[s], w=[Wb])
                elif k == 1:
                    self.op(self.pool, lambda: nc.gpsimd.tensor_copy(out=Wb.t[:, c, f0:f0 + fs], in_=s.t[:, :fs]), r=[s], w=[Wb])
                else:
                    self.op(self.act, lambda: nc.scalar.activation(out=Wb.t[:, c, f0:f0 + fs], in_=s.t[:, :fs], func=AF.Copy), r=[s], w=[Wb])

    def tsl(self, tt):
        return slice(tt * TT, (tt + 1) * TT)

    def load_xT(self, X, tt):
        self.dma(out=X.t[:], in_=self.xT[tt], r=[self.k_xT[tt]], w=[X], sb=X)

    def store_xT(self, X, tt):
        self.dma(out=self.xT[tt], in_=X.t[:], r=[X], w=[self.k_xT[tt]], sb=X)

    def norm_rstd(self, X, nch, Dn, eps, sq, rstd):
        bank = self.norm_ss(X, nch, sq)
        self.rsqrt(bank.t, [bank], rstd.t[:], rstd, Dn * eps)

    def norm_ss(self, X, nch, sq):
        nc = self.nc
        self.op(self.act, lambda: nc.scalar.activation(out=sq.t[:, :nch, :], in_=X.t[:, :nch, :], func=AF.Square), r=[X], w=[sq])
        bank = self.psn()

        def f():
            for c in range(nch):
                ins = self.mm(bank.t, self.ones.t[:], sq.t[:, c, :], c == 0, c == nch - 1)
            return ins
        self.op(self.pe, f, r=[sq, self.ones], w=[bank])
        return bank

    def scale_norm(self, H, X, gcol, rstd):
        nc = self.nc
        for c in range(8):
            E = self.dve
            sc = 32.0 if gcol is None else self.gs.t[:, gcol + c:gcol + c + 1]
            rr = [X, rstd] + ([] if gcol is None else [self.gs])
            self.op(E, lambda: E.h.scalar_tensor_tensor(out=H.t[:, c, :], in0=X.t[:, c, :], scalar=sc, in1=rstd.t[:], op0=ALU.mult, op1=ALU.mult), r=rr, w=[H])

    def resid_norm(self, X, M, gcol, rstd, Mr):
        nc = self.nc
        self.op(self.dve, lambda: nc.vector.tensor_tensor(out=Mr.t[:], in0=M.t[:], in1=rstd.t[:].unsqueeze(1).to_broadcast([128, 8, TT]), op=ALU.mult), r=[M, rstd], w=[Mr])
        for c in range(8):
            E = self.dve
            self.op(E, lambda: E.h.scalar_tensor_tensor(out=X.t[:, c, :], in0=Mr.t[:, c, :], scalar=self.gs.t[:, gcol + c:gcol + c + 1], in1=X.t[:, c, :], op0=ALU.mult, op1=ALU.add), r=[Mr, X, self.gs], w=[X])

    def proj_fm(self, W, nk, A, M, col0=0):
        for dc in range(8):
            self.proj_piece(W, nk, A, M, dc, col0)

    def proj_piece(self, W, nk, A, M, dc, col0=0):
        nc = self.nc
        bank = self.psn()

        def f():
            for k in range(nk):
                ins = self.mm(bank.t, W.t[:, k, col0 + dc * 128:col0 + (dc + 1) * 128], A.t[:, k, :], k == 0, k == nk - 1)
            return ins
        self.op(self.pe, f, r=[W, A], w=[bank])
        self.op(self.act, lambda: nc.scalar.activation(out=M.t[:, dc, :], in_=bank.t, func=AF.Copy), r=[bank], w=[M])

    def interleave(self, pieces, steps):
        n = max(len(pieces), len(steps))
        for k in range(n):
            if k < len(pieces):
                pieces[k]()
            if k < len(steps):
                steps[k]()

    def x_from_input(self, X, tt, xin):
        nc = self.nc
        self.dma(out=xin.t[:], in_=self.x_in[self.tsl(tt), :].rearrange("(s p) f -> p s f", p=128), w=[xin], sb=xin)
        for c in range(8):
            bank = self.psn()

            def f():
                for s in range(4):
                    ins = nc.tensor.transpose(bank.t[:, s * 128:(s + 1) * 128], xin.t[:, s, c * 128:(c + 1) * 128], self.identf.t[:])
                return ins
            self.op(self.pe, f, r=[xin, self.identf], w=[bank])
            if c % 2:
                self.op(self.act, lambda: nc.scalar.activation(out=X.t[:, c, :], in_=bank.t, func=AF.Copy), r=[bank], w=[X])
            else:
                self.op(self.dve, lambda: nc.vector.tensor_copy(out=X.t[:, c, :], in_=bank.t), r=[bank], w=[X])

    def rope_pair(self, bankA, bankB, cs, sn, Ap, Bp, tmp, scale):
        nc = self.nc
        Af, Bf, t1, t2, t3, t4 = [tmp.next() for _ in range(6)]
        self.op(self.act, lambda: nc.scalar.activation(out=Af.t[:], in_=bankA.t, func=AF.Copy, scale=scale), r=[bankA], w=[Af])
        self.op(self.act, lambda: nc.scalar.activation(out=Bf.t[:], in_=bankB.t, func=AF.Copy, scale=scale), r=[bankB], w=[Bf])
        self.op(self.dve, lambda: nc.vector.tensor_tensor(out=t1.t[:], in0=Af.t[:], in1=cs.t[:], op=ALU.mult), r=[Af, cs], w=[t1])
        self.op(self.dve, lambda: nc.vector.tensor_tensor(out=t2.t[:], in0=Bf.t[:], in1=sn.t[:], op=ALU.mult), r=[Bf, sn], w=[t2])
        self.op(self.pool, lambda: nc.gpsimd.tensor_tensor(out=t3.t[:], in0=Bf.t[:], in1=cs.t[:], op=ALU.mult), r=[Bf, cs], w=[t3])
        self.op(self.pool, lambda: nc.gpsimd.tensor_tensor(out=t4.t[:], in0=Af.t[:], in1=sn.t[:], op=ALU.mult), r=[Af, sn], w=[t4])
        self.op(self.dve, lambda: nc.vector.tensor_tensor(out=Ap, in0=t1.t[:], in1=t2.t[:], op=ALU.subtract), r=[t1, t2], w=[self._apbuf])
        self.op(self.pool, lambda: nc.gpsimd.tensor_tensor(out=Bp, in0=t3.t[:], in1=t4.t[:], op=ALU.add), r=[t3, t4], w=[self._bpbuf])

    def phase_PD_attn(self, l):
        nc = self.nc
        j = l // 2
        self.begin_phase()
        W = self.sb([128, 8, 3 * D], BF16, "W")
        stage = self.rot(2, [128, 2048], F32, "stg")
        self.load_weight(self.w_qkv[j], W, stage)
        Xp = self.rot(2, [128, 8, TT], F32, "X")
        xin = self.rot(1, [128, 4, D], F32, "xin") if l == 0 else None
        sq = self.sb([128, 8, TT], BF16, "sq")
        rs = self.rot(2, [128, TT], F32, "rs")
        Hp = self.rot(2, [128, 8, TT], BF16, "H")
        csp = self.rot(2, [128, TT], F32, "cs")
        snp = self.rot(2, [128, TT], F32, "sn")
        tmp = self.rot(12, [128, TT], F32, "tmp")
        ABp = self.rot(4, [128, 2, TT], BF16, "AB")
        Vp = self.rot(2, [128, 4, D], BF16, "Vt")
        def S1(tt):
            ts = self.tsl(tt)
            X = Xp.next()
            if l == 0:
                self.x_from_input(X, tt, xin.next())
                self.store_xT(X, tt)
            else:
                self.load_xT(X, tt)
            r = rs.next()
            self.norm_rstd(X, 8, D, 1e-6, sq, r)
            H = Hp.next()
            self.scale_norm(H, X, (l * 4 + 0) * 8, r)
            return (H,)

        cst = {}

        def S2(tt, H, part):
            ts = self.tsl(tt)
            if part == 0:
                cs, sn = csp.next(), snp.next()
                self.dma(out=cs.t[:], in_=self.c_ropeA[0, :, ts], w=[cs], sb=cs)
                self.dma(out=sn.t[:], in_=self.c_ropeA[1, :, ts], w=[sn], sb=sn)
                cst["cs"], cst["sn"] = cs, sn
            cs, sn = cst["cs"], cst["sn"]
            for which in ((0,) if part == 0 else (1,)):
                dst = self.QTs if which == 0 else self.KTs
                ktok = self.k_QT if which == 0 else self.k_KT
                for jj in range(4):
                    bA, bB = self.psn(), self.psn()
                    for bank, ch in ((bA, 2 * jj), (bB, 2 * jj + 1)):
                        c0 = which * D + ch * 128

                        def f():
                            for k in range(8):
                                ins = self.mm(bank.t, W.t[:, k, c0:c0 + 128], H.t[:, k, :], k == 0, k == 7)
                            return ins
                        self.op(self.pe, f, r=[W, H], w=[bank])
                    AB = ABp.next()
                    self._apbuf = AB
                    self._bpbuf = AB
                    self.rope_pair(bA, bB, cs, sn, AB.t[:, 0, :], AB.t[:, 1, :], tmp, 1.0)
                    for rr in range(4):
                        m = 4 * jj + rr
                        hd, cc = m // 2, m % 2
                        for ab in range(2):
                            self.dma(out=dst[hd, cc * 64 + ab * 32:cc * 64 + ab * 32 + 32, ts], in_=AB.t[32 * rr:32 * rr + 32, ab, :], r=[AB], w=[ktok[hd]], sb=AB)
            if part == 0:
                return
            Vt = Vp.next()
            for s in range(4):
                for hf in range(2):
                    bank = self.psn()
                    c0 = 2 * D + hf * 512

                    def f():
                        for k in range(8):
                            ins = self.mm(bank.t, H.t[:, k, s * 128:(s + 1) * 128], W.t[:, k, c0:c0 + 512], k == 0, k == 7)
                        return ins
                    self.op(self.pe, f, r=[W, H], w=[bank])
                    if hf:
                        self.op(self.act, lambda: nc.scalar.activation(out=Vt.t[:, s, hf * 512:(hf + 1) * 512], in_=bank.t, func=AF.Copy), r=[bank], w=[Vt])
                    else:
                        self.op(self.dve, lambda: nc.vector.tensor_copy(out=Vt.t[:, s, hf * 512:(hf + 1) * 512], in_=bank.t), r=[bank], w=[Vt])
            self.dma(out=self.VS[ts, :].rearrange("(s p) f -> p s f", p=128), in_=Vt.t[:], r=[Vt], w=[self.k_V[0]], sb=Vt)

        cur = S1(0)
        for tt in range(self.NTT):
            S2(tt, cur[0], 0)
            nxt = S1(tt + 1) if tt + 1 < self.NTT else None
            S2(tt, cur[0], 1)
            cur = nxt
        self.end_phase()

    def phase_attn(self, l):
        nc = self.nc
        j = l // 2
        NT, NTT, NCH = self.NT, self.NTT, self.NCH
        NKP = NCH // 2
        dv = self.dve
        self.begin_phase()
        KTp = self.rot(2, [128, 2, NT], BF16, "KT")
        for kb_ in KTp.items:
            self.op(self.pool, lambda: nc.gpsimd.memset(kb_.t[:], 0.0), w=[kb_])
        QTp = self.rot(2, [128, NT], BF16, "QT")
        Vp = self.rot(2, [128, NCH, 128], BF16, "Vh")
        Pp = self.rot(6, [128, 2, TT], BF16, "P")
        Ocp = self.rot(3, [128, TT], F32, "Oc")
        zsp = self.rot(2, [128, TT], F32, "zs")
        rzp = self.rot(3, [128, TT], F32, "rz")
        t0p = self.rot(2, [128, TT], F32, "t0")
        t1p = self.rot(2, [128, TT], F32, "t1")
        ddp = self.rot(2, [128, TT], F32, "dd")
        sqp = self.rot(2, [128, TT], BF16, "sqd")
        rrp = self.rot(2, [128, TT], F32, "rr")
        yp = self.rot(2, [128, TT], BF16, "yt")
        ZEp = [self.sb([128, 2, TT], F32, "ZE") for _ in range(2)]
        ZOp = [[self.sb([128, TT], F32, "ZO") for _ in range(2)] for _ in range(2)]
        slots = [(self.psb[2 * k], self.psb[2 * k + 1], self.ps[:, 2 * k:2 * k + 2, :]) for k in range(3)]
        O, Z = self.psb[6], self.psb[7]
        heads = {}
        slot_of = {}
        sctr = [0]

        def next_slot():
            sl = slots[sctr[0] % 3]
            sctr[0] += 1
            return sl

        def load_head(h):
            KT, QT, V = KTp.next(), QTp.next(), Vp.next()
            self.dma(out=KT.t[0:64, 0, :], in_=self.KTs[h, 0:64, :], r=[self.k_KT[h]], w=[KT], sb=KT)
            self.dma(out=KT.t[64:128, 1, :], in_=self.KTs[h, 64:128, :], r=[self.k_KT[h]], w=[KT], sb=KT)
            self.dma(out=QT.t[:], in_=self.QTs[h], r=[self.k_QT[h]], w=[QT], sb=QT)
            self.dma(out=V.t[:], in_=self.VS[:, h * 128:(h + 1) * 128].rearrange("(kb p) d -> p kb d", p=128), r=[self.k_V[0]], w=[V], sb=V)
            heads[h] = (KT, QT, V)

        items = [(h, qt, c, kp) for h in range(8) for qt in range(NTT) for c in range(2) for kp in range(NKP)]
        state = {"t0": None}
        pending = []
        big = NKP >= 16
        KZ = 8 if big else 5
        D_FZ, D_RZ, D_B, D_C = (2, 3, 4, 6) if big else (1, 1, 1, 2)

        def emit_qk(i):
            h, qt, c, kp = items[i]
            if qt == 0 and c == 0 and kp == 0 and h == 0:
                load_head(0)
            if qt == 0 and c == 0 and kp == 3 and h + 1 < 8:
                load_head(h + 1)
            KT, QT, V = heads[h]
            b0, b1, sap = slot_of[i] = next_slot()

            def f():
                for jx in range(2):
                    kb = 2 * kp + jx
                    ins = self.mm(sap[:, jx, :], KT.t[:, c, kb * 128:(kb + 1) * 128], QT.t[:, qt * TT:(qt + 1) * TT], True, True)
                return ins
            self.op(self.pe, f, r=[KT, QT], w=[b0, b1])

        def emit_exp(i):
            h, qt, c, kp = items[i]
            b0, b1, sap = slot_of.pop(i)
            P = Pp.next()
            col = qt * NKP + kp
            self.op(self.act, lambda: nc.scalar.activation(out=P.t[:], in_=sap, func=AF.Exp, bias=self.maskb.t[:, col:col + 1], scale=0.125), r=[b0, b1, self.maskb], w=[P])
            return P

        def emit_av(i, P):
            h, qt, c, kp = items[i]
            KT, QT, V = heads[h]
            g = (h * NTT + qt) * 2 + c
            ZE, ZO = ZEp[g % 2], ZOp[g % 2]
            peZ = (kp >= KZ)
            if kp == 0:
                self.op(dv, lambda: nc.vector.tensor_copy(out=ZE.t[:], in_=P.t[:]), r=[P], w=[ZE])
            elif not peZ:
                self.op(dv, lambda: nc.vector.tensor_tensor(out=ZE.t[:], in0=ZE.t[:], in1=P.t[:], op=ALU.add), r=[P, ZE], w=[ZE])
            else:
                ZOk = ZO[kp % 2]
                if kp < KZ + 2:
                    self.op(dv, lambda: nc.vector.tensor_copy(out=ZOk.t[:], in_=P.t[:, 0, :]), r=[P], w=[ZOk])
                else:
                    self.op(dv, lambda: nc.vector.tensor_tensor(out=ZOk.t[:], in0=ZOk.t[:], in1=P.t[:, 0, :], op=ALU.add), r=[P, ZOk], w=[ZOk])

            def f():
                for jx in range(2):
                    kb = 2 * kp + jx
                    first = (kp == 0 and jx == 0)
                    last = (kp == NKP - 1 and jx == 1)
                    ins = self.mm(O.t, V.t[:, kb, :], P.t[:, jx, :], first, last)
                if peZ:
                    ins = self.mm(Z.t, self.ones.t[:], P.t[:, 1, :], kp == KZ, False)
                return ins
            self.op(self.pe, f, r=[V, P, self.ones], w=[O, Z] if peZ else [O])
            if kp != NKP - 1:
                return
            Oc = Ocp.next()
            self.op(dv, lambda: nc.vector.tensor_copy(out=Oc.t[:], in_=O.t), r=[O], w=[Oc])
            zs = zsp.next()
            self.op(dv, lambda: nc.vector.tensor_tensor(out=zs.t[:], in0=ZE.t[:, 0, :], in1=ZE.t[:, 1, :], op=ALU.add), r=[ZE], w=[zs])
            for kk in (KZ, KZ + 1):
                if NKP > kk:
                    ZOk = ZO[kk % 2]
                    self.op(dv, lambda: nc.vector.tensor_tensor(out=zs.t[:], in0=zs.t[:], in1=ZOk.t[:], op=ALU.add), r=[zs, ZOk], w=[zs])
            rz = rzp.next()

            def stageFZ():
                self.op(self.pe, lambda: self.mm(Z.t, self.onesf.t[:], zs.t[:], NKP <= KZ, True), r=[zs, self.onesf], w=[Z])

            def stageRZ():
                self.op(dv, lambda: nc.vector.reciprocal(out=rz.t[:], in_=Z.t), r=[Z], w=[rz])
            pending.append((i + D_FZ, stageFZ))
            pending.append((i + D_RZ, stageRZ))

            def stageB():
                if c == 0:
                    t0 = t0p.next()
                    self.op(dv, lambda: nc.vector.tensor_tensor(out=t0.t[:], in0=Oc.t[:], in1=rz.t[:], op=ALU.mult), r=[Oc, rz], w=[t0])
                    state["t0"] = t0
                    return
                t0 = state["t0"]
                t1 = t1p.next()
                self.op(dv, lambda: nc.vector.tensor_tensor(out=t1.t[:], in0=Oc.t[:], in1=rz.t[:], op=ALU.mult), r=[Oc, rz], w=[t1])
                dd = ddp.next()
                self.op(dv, lambda: nc.vector.scalar_tensor_tensor(out=dd.t[:], in0=t1.t[:], scalar=self.lamt.t[:, j * 4 + 1:j * 4 + 2], in1=t0.t[:], op0=ALU.mult, op1=ALU.add), r=[t1, t0, self.lamt], w=[dd])
                sqd = sqp.next()
                self.op(dv, lambda: nc.vector.tensor_tensor(out=sqd.t[:], in0=dd.t[:], in1=dd.t[:], op=ALU.mult), r=[dd], w=[sqd])
                self.op(self.pe, lambda: self.mm(Z.t, self.ones.t[:], sqd.t[:], True, True), r=[sqd, self.ones], w=[Z])

                def stageC():
                    rr = rrp.next()
                    self.rsqrt(Z.t, [Z], rr.t[:], rr, 128 * 1e-5)
                    yt = yp.next()
                    self.op(dv, lambda: nc.vector.scalar_tensor_tensor(out=yt.t[:], in0=dd.t[:], scalar=self.sgs.t[:, j:j + 1], in1=rr.t[:], op0=ALU.mult, op1=ALU.mult), r=[dd, rr, self.sgs], w=[yt])
                    self.dma(out=self.AT[qt, :, h, :], in_=yt.t[:], r=[yt], w=[self.k_AT[qt]], sb=yt)
                pending.append((i + D_C, stageC))
            pending.append((i + D_B, stageB))

        def run_pending(upto):
            pending.sort(key=lambda e: e[0])
            while pending and (upto is None or pending[0][0] <= upto):
                pending.pop(0)[1]()
                pending.sort(key=lambda e: e[0])

        n = len(items)
        emit_qk(0)
        emit_qk(1)
        for i in range(n):
            if i + 2 < n:
                emit_qk(i + 2)
            P = emit_exp(i)
            emit_av(i, P)
            run_pending(i)
        run_pending(None)
        self.end_phase()

    def phase_PD_ret(self, l):
        nc = self.nc
        j = l // 2
        self.begin_phase()
        W = self.sb([128, 8, 6 * D], BF16, "W")
        stage = self.rot(2, [128, 1024], F32, "stg")
        self.load_weight(self.w_rin[j], W, stage, 1024)
        Xp = self.rot(1, [128, 8, TT], F32, "X")
        sq = self.sb([128, 8, TT], BF16, "sq")
        rs = self.rot(2, [128, TT], F32, "rs")
        Hp = self.rot(1, [128, 8, TT], BF16, "H")
        csp = self.rot(1, [128, TT], F32, "cs")
        snp = self.rot(1, [128, TT], F32, "sn")
        tmp = self.rot(6, [128, TT], F32, "tmp")
        QKp = self.rot(2, [128, 8, TT], BF16, "QK")
        Ktp = self.rot(1, [128, 4, D], BF16, "Ktm")
        Vp = self.rot(1, [128, 4, 2 * D], BF16, "Vt")
        for tt in range(self.NTT):
            ts = self.tsl(tt)
            X = Xp.next()
            self.load_xT(X, tt)
            r = rs.next()
            self.norm_rstd(X, 8, D, 1e-6, sq, r)
            H = Hp.next()
            self.scale_norm(H, X, (l * 4 + 0) * 8, r)
            cs, sn = csp.next(), snp.next()
            self.dma(out=cs.t[:], in_=self.c_ropeR[0, :, ts], w=[cs], sb=cs)
            self.dma(out=sn.t[:], in_=self.c_ropeR[1, :, ts], w=[sn], sb=sn)
            KR = None
            for which in range(2):
                dst = self.QTs if which == 0 else self.KTs
                QK = QKp.next()
                for hh in range(4):
                    bA, bB = self.psn(), self.psn()
                    for bank, ch in ((bA, 2 * hh), (bB, 2 * hh + 1)):
                        c0 = which * D + ch * 128

                        def f():
                            for k in range(8):
                                ins = self.mm(bank.t, W.t[:, k, c0:c0 + 128], H.t[:, k, :], k == 0, k == 7)
                            return ins
                        self.op(self.pe, f, r=[W, H], w=[bank])
                    self._apbuf = QK
                    self._bpbuf = QK
                    self.rope_pair(bA, bB, cs, sn, QK.t[:, 2 * hh, :], QK.t[:, 2 * hh + 1, :], tmp, 1.0 if which == 0 else 1.0 / 16.0)
                self.dma(out=dst[:, :, ts].rearrange("c p t -> p c t"), in_=QK.t[:], r=[QK], w=[self.k_RQ[tt]], sb=QK)
                if which == 1:
                    KR = QK
            Ktm = Ktp.next()
            for s in range(4):
                b0, b1 = self.psn2()
                bi = self.psb.index(b0)

                def f():
                    for ch in range(8):
                        ins = nc.tensor.transpose(self.psbf[:, bi, ch * 128:(ch + 1) * 128], KR.t[:, ch, s * 128:(s + 1) * 128], self.identb.t[:])
                    return ins
                self.op(self.pe, f, r=[KR, self.identb], w=[b0])
                self.op(self.dve, lambda: nc.vector.tensor_copy(out=Ktm.t[:, s, :], in_=self.psbf[:, bi, :]), r=[b0], w=[Ktm])
            self.dma(out=self.RK[ts, :].rearrange("(s p) f -> p s f", p=128), in_=Ktm.t[:], r=[Ktm], w=[self.k_R[tt * 4 + q] for q in range(4)], sb=Ktm)
            for which, dst in ((2, self.RV), (3, self.RG)):
                Vt = Vp.next()
                for s in range(4):
                    for hf in range(4):
                        bank = self.psn()
                        c0 = (2 * D if which == 2 else 4 * D) + hf * 512

                        def f():
                            for k in range(8):
                                ins = self.mm(bank.t, H.t[:, k, s * 128:(s + 1) * 128], W.t[:, k, c0:c0 + 512], k == 0, k == 7)
                            return ins
                        self.op(self.pe, f, r=[W, H], w=[bank])
                        if which == 3:
                            self.op(self.act, lambda: nc.scalar.activation(out=Vt.t[:, s, hf * 512:(hf + 1) * 512], in_=bank.t, func=AF.Silu), r=[bank], w=[Vt])
                        elif hf % 2:
                            self.op(self.act, lambda: nc.scalar.activation(out=Vt.t[:, s, hf * 512:(hf + 1) * 512], in_=bank.t, func=AF.Copy), r=[bank], w=[Vt])
                        else:
                            self.op(self.dve, lambda: nc.vector.tensor_copy(out=Vt.t[:, s, hf * 512:(hf + 1) * 512], in_=bank.t), r=[bank], w=[Vt])
                self.dma(out=dst[ts, :].rearrange("(s p) f -> p s f", p=128), in_=Vt.t[:], r=[Vt], w=[self.k_R[tt * 4 + q] for q in range(4)], sb=Vt)
        self.end_phase()

    def phase_ret(self, l):
        nc = self.nc
        j = l // 2
        NCH = self.NCH
        dv, act, pool, pe = self.dve, self.act, self.pool, self.pe
        self.begin_phase()
        TF = self.sb([128, 128], F32, "TF")
        TB = self.sb([128, 128], F32, "TB")
        P1 = self.sb([128, 128], F32, "P1")
        P2 = self.sb([128, 128], F32, "P2")
        CC = self.sb([128, 4], F32, "CC")
        MF = self.sb([128, NCH], F32, "MF")
        MB = self.sb([128, NCH], F32, "MB")
        for bfr, src in ((TF, self.c_tftb[0]), (TB, self.c_tftb[1]), (P1, self.c_pos[0]), (P2, self.c_pos[1]), (CC, self.c_col), (MF, self.c_mfb[0]), (MB, self.c_mfb[1])):
            self.dma(out=bfr.t[:], in_=src, w=[bfr], sb=bfr)
        DT = self.sb([128, 4, 128], F32, "DT")
        e1 = self.sb([128, 128], F32, "e1")
        e2 = self.sb([128, 128], F32, "e2")
        XIF = self.sb([128, 8, 128], BF16, "XIF")
        XIB = self.sb([128, 8, 128], BF16, "XIB")
        ZF = self.sb([128, 4, NCH], F32, "ZF")
        ZB = self.sb([128, 4, NCH], F32, "ZB")
        CDF = self.sb([128, 4, NCH], F32, "CDF")
        CDB = self.sb([128, 4, NCH], F32, "CDB")
        sc = self.sb([128, 16], F32, "sc")
        for h in range(4):
            lf = self.lg.t[:, j * 8 + h:j * 8 + h + 1]
            lb = self.lg.t[:, j * 8 + 4 + h:j * 8 + 4 + h + 1]
            self.op(act, lambda: nc.scalar.activation(out=e1.t[:], in_=TF.t[:], func=AF.Exp, scale=lf), r=[TF, self.lg], w=[e1])
            self.op(act, lambda: nc.scalar.activation(out=e2.t[:], in_=TB.t[:], func=AF.Exp, scale=lb), r=[TB, self.lg], w=[e2])
            self.op(dv, lambda: nc.vector.tensor_tensor(out=DT.t[:, h, :], in0=e1.t[:], in1=e2.t[:], op=ALU.add), r=[e1, e2], w=[DT])
            for q in range(2):
                self.op(act, lambda: nc.scalar.activation(out=XIF.t[:, 2 * h + q, :], in_=P1.t[:], func=AF.Exp, scale=lf), r=[P1, self.lg], w=[XIF])
                self.op(act, lambda: nc.scalar.activation(out=XIB.t[:, 2 * h + q, :], in_=P2.t[:], func=AF.Exp, scale=lb), r=[P2, self.lg], w=[XIB])
            self.op(act, lambda: nc.scalar.activation(out=sc.t[:, 4 * h + 0:4 * h + 1], in_=CC.t[:, 0:1], func=AF.Exp, scale=lf), r=[CC, self.lg], w=[sc])
            self.op(act, lambda: nc.scalar.activation(out=sc.t[:, 4 * h + 1:4 * h + 2], in_=CC.t[:, 2:3], func=AF.Exp, scale=lf), r=[CC, self.lg], w=[sc])
            self.op(act, lambda: nc.scalar.activation(out=sc.t[:, 4 * h + 2:4 * h + 3], in_=CC.t[:, 1:2], func=AF.Exp, scale=lb), r=[CC, self.lg], w=[sc])
            self.op(act, lambda: nc.scalar.activation(out=sc.t[:, 4 * h + 3:4 * h + 4], in_=CC.t[:, 2:3], func=AF.Exp, scale=lb), r=[CC, self.lg], w=[sc])
            for k, (dst, msk) in enumerate(((ZF, MF), (CDF, MF), (ZB, MB), (CDB, MB))):
                self.op(dv, lambda: nc.vector.tensor_scalar(out=dst.t[:, h, :], in0=msk.t[:], scalar1=sc.t[:, 4 * h + k:4 * h + k + 1], scalar2=None, op0=ALU.mult), r=[msk, sc], w=[dst])

        Rf = [self.sb([128, 2, 512], F32, "Rf") for _ in range(4)]
        Rb = [self.sb([128, 2, 512], BF16, "Rb") for _ in range(4)]
        QGp = self.rot(2, [128, 8, TT], BF16, "QG")
        KGp = self.rot(2, [128, 8, TT], BF16, "KG")
        Ktp = self.rot(3, [128, D], BF16, "Ktm")
        Vcp = self.rot(3, [128, 2 * D], BF16, "Vc")
        SGp = self.rot(3, [128, 2 * D], BF16, "SG")
        OBp = self.rot(3, [128, 2 * D], BF16, "OBt")
        Qxp = self.rot(2, [128, 8, 128], BF16, "Qx")
        Kzp = self.rot(2, [128, D], BF16, "Kz")
        Smp = self.rot(4, [128, 128], BF16, "Sm")
        Ofp = self.rot(3, [128, 4, 512], F32, "Of")
        junk = self.sb([128, 512], BF16, "junk")
        ssp = self.rot(2, [128, 4], F32, "ss")
        r1p = self.rot(2, [128, 4], F32, "r1")
        Yp = self.rot(2, [128, 2 * D], BF16, "Y")
        YTp = self.rot(2, [128, 16, 128], BF16, "YT")
        B = self.psb

        def make_kz(c, Ktm, Ztab):
            Kz = Kzp.next()
            for h in range(4):
                self.op(act, lambda: nc.scalar.activation(out=Kz.t[:, h * 256:(h + 1) * 256], in_=Ktm.t[:, h * 256:(h + 1) * 256], func=AF.Copy, scale=Ztab.t[:, h, c:c + 1]), r=[Ktm, Ztab], w=[Kz])
            return Kz

        def update_state(c, Kz, Vc, CDtab):
            for h in range(4):
                bi = 2 * (h % 2)
                b0, b1 = B[bi], B[bi + 1]

                def f():
                    for dk in range(2):
                        ins = self.mm(self.ps[:, bi + dk, :], Kz.t[:, h * 256 + dk * 128:h * 256 + (dk + 1) * 128], Vc.t[:, h * 512:(h + 1) * 512], True, True)
                    return ins
                self.op(pe, f, r=[Kz, Vc], w=[b0, b1])
                self.op(dv, lambda: nc.vector.scalar_tensor_tensor(out=Rf[h].t[:], in0=Rf[h].t[:], scalar=CDtab.t[:, h, c:c + 1], in1=self.ps[:, bi:bi + 2, :], op0=ALU.mult, op1=ALU.add), r=[Rf[h], b0, b1, CDtab], w=[Rf[h]])
                self.op(act, lambda: nc.scalar.activation(out=Rb[h].t[:], in_=Rf[h].t[:], func=AF.Copy), r=[Rf[h]], w=[Rb[h]])

        for sweep in (0, 1):
            for h in range(4):
                self.op(dv, lambda: nc.vector.memset(Rf[h].t[:], 0.0), w=[Rf[h]])
                self.op(dv, lambda: nc.vector.memset(Rb[h].t[:], 0.0), w=[Rb[h]])
            order = list(range(NCH - 1, -1, -1)) if sweep == 0 else list(range(NCH))
            curg, QG, KG = -1, None, None
            prev_post = [None]
            for c in order:
                g, off = c // 4, (c % 4) * 128
                csl = slice(c * 128, (c + 1) * 128)
                if g != curg:
                    curg = g
                    QG = QGp.next()
                    self.dma(out=QG.t[:], in_=self.QTs[:, :, self.tsl(g)].rearrange("c p t -> p c t"), r=[self.k_RQ[g]], w=[QG], sb=QG)
                    if sweep == 1:
                        KG = KGp.next()
                        self.dma(out=KG.t[:], in_=self.KTs[:, :, self.tsl(g)].rearrange("c p t -> p c t"), r=[self.k_RQ[g]], w=[KG], sb=KG)
                Ktm, Vc = Ktp.next(), Vcp.next()
                self.dma(out=Ktm.t[:], in_=self.RK[csl, :], r=[self.k_R[c]], w=[Ktm], sb=Ktm)
                self.dma(out=Vc.t[:], in_=self.RV[csl, :], r=[self.k_R[c]], w=[Vc], sb=Vc)
                if sweep == 1:
                    SG, OBt = SGp.next(), OBp.next()
                    self.dma(out=SG.t[:], in_=self.RG[csl, :], r=[self.k_R[c]], w=[SG], sb=SG)
                    self.dma(out=OBt.t[:], in_=self.OB[csl, :], r=[self.k_OB[c]], w=[OBt], sb=OBt)
                Qx = Qxp.next()
                XI = XIB if sweep == 0 else XIF
                self.op(dv, lambda: nc.vector.tensor_tensor(out=Qx.t[:], in0=QG.t[:, :, off:off + 128], in1=XI.t[:], op=ALU.mult), r=[QG, XI], w=[Qx])
                Kz = make_kz(c, Ktm, ZB if sweep == 0 else ZF)
                if sweep == 0:
                    OBt = OBp.next()
                    for h in range(4):
                        bank = B[4 + h]

                        def f():
                            for dk in range(2):
                                ins = self.mm(bank.t, Qx.t[:, 2 * h + dk, :], Rb[h].t[:, dk, :], dk == 0, dk == 1)
                            return ins
                        self.op(pe, f, r=[Qx, Rb[h]], w=[bank])
                    update_state(c, Kz, Vc, CDB)
                    for h in range(4):
                        bank = B[4 + h]
                        if h % 2:
                            self.op(act, lambda: nc.scalar.activation(out=OBt.t[:, h * 512:(h + 1) * 512], in_=bank.t, func=AF.Copy), r=[bank], w=[OBt])
                        else:
                            self.op(dv, lambda: nc.vector.tensor_copy(out=OBt.t[:, h * 512:(h + 1) * 512], in_=bank.t), r=[bank], w=[OBt])
                    self.dma(out=self.OB[csl, :], in_=OBt.t[:], r=[OBt], w=[self.k_OB[c]], sb=OBt)
                    continue
                Sms = []
                for h in range(4):
                    bS = B[h]

                    def f():
                        for dk in range(2):
                            ins = self.mm(bS.t[:, 0:128], KG.t[:, 2 * h + dk, off:off + 128], QG.t[:, 2 * h + dk, off:off + 128], dk == 0, dk == 1)
                        return ins
                    self.op(pe, f, r=[KG, QG], w=[bS])
                for h in range(4):
                    Sm = Smp.next()
                    self.op(dv, lambda: nc.vector.tensor_tensor(out=Sm.t[:], in0=B[h].t[:, 0:128], in1=DT.t[:, h, :], op=ALU.mult), r=[B[h], DT], w=[Sm])
                    Sms.append(Sm)
                for h in range(4):
                    bO, Sm = B[4 + h], Sms[h]

                    def f2():
                        self.mm(bO.t, Sm.t[:], Vc.t[:, h * 512:(h + 1) * 512], True, False)
                        for dk in range(2):
                            ins = self.mm(bO.t, Qx.t[:, 2 * h + dk, :], Rb[h].t[:, dk, :], False, dk == 1)
                        return ins
                    self.op(pe, f2, r=[Sm, Vc, Qx, Rb[h]], w=[bO])
                Of, ss = Ofp.next(), ssp.next()
                for h in range(4):
                    bO = B[4 + h]
                    self.op(dv, lambda: nc.vector.tensor_tensor(out=Of.t[:, h, :], in0=bO.t, in1=OBt.t[:, h * 512:(h + 1) * 512], op=ALU.add), r=[bO, OBt], w=[Of])
                if prev_post[0] is not None:
                    prev_post[0]()
                update_state(c, Kz, Vc, CDF)

                def post(Of=Of, ss=ss, SG=SG, g=g, off=off):
                    for h in range(4):
                        self.op(act, lambda: nc.scalar.activation(out=junk.t[:], in_=Of.t[:, h, :], func=AF.Square, accum_out=ss.t[:, h:h + 1]), r=[Of], w=[junk, ss])
                    r1 = r1p.next()
                    self.rsqrt(ss.t[:], [ss], r1.t[:], r1, 512 * 1e-6, 4)
                    self.op(dv, lambda: nc.vector.tensor_scalar(out=r1.t[:], in0=r1.t[:], scalar1=math.sqrt(512.0), scalar2=None, op0=ALU.mult), r=[r1], w=[r1])
                    Y = Yp.next()
                    for h in range(4):
                        self.op(dv, lambda: nc.vector.scalar_tensor_tensor(out=Y.t[:, h * 512:(h + 1) * 512], in0=Of.t[:, h, :], scalar=r1.t[:, h:h + 1], in1=SG.t[:, h * 512:(h + 1) * 512], op0=ALU.mult, op1=ALU.mult), r=[Of, r1, SG], w=[Y])
                    YT = YTp.next()
                    b0, b1 = B[0], B[1]

                    def f3():
                        for q in range(16):
                            ins = nc.tensor.transpose(self.psbf[:, q // 8, (q % 8) * 128:(q % 8 + 1) * 128], Y.t[:, q * 128:(q + 1) * 128], self.identb.t[:])
                        return ins
                    self.op(pe, f3, r=[Y, self.identb], w=[b0, b1])
                    self.op(act, lambda: nc.scalar.activation(out=YT.t[:, 0:8, :], in_=self.psbf[:, 0, :], func=AF.Copy), r=[b0], w=[YT])
                    self.op(dv, lambda: nc.vector.tensor_copy(out=YT.t[:, 8:16, :], in_=self.psbf[:, 1, :]), r=[b1], w=[YT])
                    self.dma(out=self.AT[g, :, :, off:off + 128], in_=YT.t[:], r=[YT], w=[self.k_AT[g]], sb=YT)
                prev_post[0] = post
            if sweep == 1 and prev_post[0] is not None:
                prev_post[0]()
                prev_post[0] = None
        self.end_phase()

    def phase_PA(self, l):
        nc = self.nc
        j = l // 2
        attn = (l % 2 == 0)
        nin = 8 if attn else 16
        self.begin_phase()
        W = self.sb([128, nin, D], BF16, "W")
        stage = self.rot(2, [128, 2048], F32, "stg")
        self.load_weight(self.w_o[j] if attn else self.w_rout[j], W, stage)
        Ap = self.rot(2, [128, nin, TT], BF16, "A")
        Xp = self.rot(2, [128, 8, TT], F32, "X")
        Mp = self.rot(2, [128, 8, TT], F32, "M")
        sq = self.sb([128, 8, TT], BF16, "sq")
        rs = self.rot(2, [128, TT], F32, "rs")
        Hp = self.rot(2, [128, 8, TT], BF16, "H")

        def S1(tt):
            ts = self.tsl(tt)
            st = {}

            def p0():
                A = Ap.next()
                self.dma(out=A.t[:], in_=self.AT[tt, :, 0:nin, :], r=[self.k_AT[tt]], w=[A], sb=A)
                X = Xp.next()
                self.load_xT(X, tt)
                st["A"], st["X"], st["M"] = A, X, Mp.next()
            pcs = [p0] + [(lambda dc=dc: self.proj_piece(W, nin, st["A"], st["M"], dc)) for dc in range(8)]
            return pcs, st

        def S2(tt, st):
            ts = self.tsl(tt)
            c2 = {}

            def a():
                c2["b1"] = self.norm_ss(st["M"], 8, sq)

            def b():
                X, M = st["X"], st["M"]
                r1 = rs.next()
                self.rsqrt(c2["b1"].t, [c2["b1"]], r1.t[:], r1, D * 1e-6)
                self.resid_norm(X, M, (l * 4 + 1) * 8, r1, M)
                self.store_xT(X, tt)
                c2["b2"] = self.norm_ss(X, 8, sq)

            def c():
                X = st["X"]
                r2 = rs.next()
                self.rsqrt(c2["b2"].t, [c2["b2"]], r2.t[:], r2, D * 1e-6)
                H = Hp.next()
                self.scale_norm(H, X, (l * 4 + 2) * 8, r2)
                self.dma(out=self.H2s[tt], in_=H.t[:], r=[H], w=[self.k_H2[tt]], sb=H)
            return [lambda: None, lambda: None, a, lambda: None, b, lambda: None, lambda: None, c]

        pcs, st = S1(0)
        for p_ in pcs:
            p_()
        for tt in range(self.NTT):
            if tt + 1 < self.NTT:
                npcs, nst = S1(tt + 1)
            else:
                npcs, nst = [], None
            self.interleave(npcs, S2(tt, st))
            st = nst
        self.end_phase()

    def phase_PB(self, l):
        nc = self.nc
        self.begin_phase()
        W = self.sb([128, 8, DFF], BF16, "W")
        stage = self.rot(3, [128, 2048], F32, "stg")
        self.load_weight(self.w_1[l], W, stage)
        Hp = self.rot(2, [128, 8, TT], BF16, "H")
        Up = self.rot(3, [128, 8, TT], BF16, "U")
        tp = self.rot(4, [128, TT], F32, "tmp")
        for tt in range(self.NTT):
            ts = self.tsl(tt)
            H = Hp.next()
            self.dma(out=H.t[:], in_=self.H2s[tt], r=[self.k_H2[tt]], w=[H], sb=H)
            for fg in range(4):
                U = Up.next()
                for fi in range(8):
                    fc = fg * 8 + fi
                    bank = self.psn()

                    def f():
                        for k in range(8):
                            ins = self.mm(bank.t, W.t[:, k, fc * 128:(fc + 1) * 128], H.t[:, k, :], k == 0, k == 7)
                        return ins
                    self.op(self.pe, f, r=[W, H], w=[bank])
                    t = tp.next()
                    self.op(self.act, lambda: nc.scalar.activation(out=t.t[:], in_=bank.t, func=AF.Relu), r=[bank], w=[t])
                    E = self.ew()
                    self.op(E, lambda: E.h.tensor_tensor(out=U.t[:, fi, :], in0=t.t[:], in1=t.t[:], op=ALU.mult), r=[t], w=[U])
                self.dma(out=self.Us[tt, :, fg * 8:(fg + 1) * 8, :], in_=U.t[:], r=[U], w=[self.k_U[tt]], sb=U)
        self.end_phase()

    def phase_PC(self, l):
        nc = self.nc
        final = (l == self.depth - 1)
        self.begin_phase()
        W2 = self.sb([128, 32, D], BF16, "W2")
        Wg = self.sb([128, 8, D], BF16, "Wg")
        Wp = self.sb([128, 2, D], BF16, "Wp")
        stage = self.rot(2, [128, 512], F32, "stg")
        self.load_weight(self.w_2[l], W2, stage, 512)
        self.load_weight(self.w_pg[l], Wg, stage, 512)
        self.load_weight(self.w_pp[l], Wp, stage, 512)
        Up = self.rot(1, [128, 32, TT], BF16, "U")
        Xp = self.rot(1, [128, 8, TT], F32, "X")
        Mp = self.rot(2, [128, 8, TT], F32, "M")
        sq = self.sb([128, 8, TT], BF16, "sq")
        rs = self.rot(2, [128, TT], F32, "rs")
        pin = self.rot(1, [128, 4, PLE], F32, "pin")
        pTp = self.rot(1, [128, 2, TT], BF16, "pT")
        tg = self.rot(1, [128, TT], F32, "tg")
        tq = self.rot(1, [128, TT], F32, "tq")

        def S1(tt):
            ts = self.tsl(tt)
            st = {}

            def p0():
                U = Up.next()
                for fg in range(4):
                    self.dma(out=U.t[:, fg * 8:(fg + 1) * 8, :], in_=self.Us[tt, :, fg * 8:(fg + 1) * 8, :], r=[self.k_U[tt]], w=[U], sb=U)
                st["U"], st["M"] = U, Mp.next()
            pcs = [p0] + [(lambda dc=dc: self.proj_piece(W2, 32, st["U"], st["M"], dc)) for dc in range(8)]
            return pcs, st

        def S2(tt, st):
            ts = self.tsl(tt)
            M = st["M"]
            c2 = {}

            def a():
                X = Xp.next()
                self.load_xT(X, tt)
                pi = pin.next()
                self.dma(out=pi.t[:], in_=self.p_in[l, ts, :].rearrange("(s p) f -> p s f", p=128), w=[pi], sb=pi)
                pT = pTp.next()
                for kc in range(2):
                    bank = self.psn()

                    def f():
                        for s_ in range(4):
                            ins = nc.tensor.transpose(bank.t[:, s_ * 128:(s_ + 1) * 128], pi.t[:, s_, kc * 128:(kc + 1) * 128], self.identf.t[:])
                        return ins
                    self.op(self.pe, f, r=[pi, self.identf], w=[bank])
                    self.op(self.dve, lambda: nc.vector.tensor_copy(out=pT.t[:, kc, :], in_=bank.t), r=[bank], w=[pT])
                c2["X"], c2["pT"] = X, pT
                c2["b1"] = self.norm_ss(M, 8, sq)

            def b():
                X = c2["X"]
                r1 = rs.next()
                self.rsqrt(c2["b1"].t, [c2["b1"]], r1.t[:], r1, D * 1e-6)
                self.resid_norm(X, M, (l * 4 + 3) * 8, r1, M)
                c2["b2"] = self.norm_ss(X, 8, sq)

            def c():
                X = c2["X"]
                r2 = rs.next()
                self.rsqrt(c2["b2"].t, [c2["b2"]], r2.t[:], r2, D * 1e-6)
                self.scale_norm(sq, X, None, r2)

            def gate(dc):
                X, pT = c2["X"], c2["pT"]
                bg, bp = self.psn(), self.psn()

                def f():
                    for k in range(8):
                        ins = self.mm(bg.t, Wg.t[:, k, dc * 128:(dc + 1) * 128], sq.t[:, k, :], k == 0, k == 7)
                    return ins
                self.op(self.pe, f, r=[Wg, sq], w=[bg])

                def f2():
                    for k in range(2):
                        ins = self.mm(bp.t, Wp.t[:, k, dc * 128:(dc + 1) * 128], pT.t[:, k, :], k == 0, k == 1)
                    return ins
                self.op(self.pe, f2, r=[Wp, pT], w=[bp])
                g = tg.next()
                self.op(self.act, lambda: nc.scalar.activation(out=g.t[:], in_=bg.t, func=AF.Sigmoid), r=[bg], w=[g])
                q = tq.next()
                self.op(self.dve, lambda: nc.vector.tensor_tensor(out=q.t[:], in0=bp.t, in1=g.t[:], op=ALU.mult), r=[bp, g], w=[q])
                self.op(self.dve, lambda: nc.vector.tensor_tensor(out=X.t[:, dc, :], in0=X.t[:, dc, :], in1=q.t[:], op=ALU.add), r=[X, q], w=[X])

            def fin():
                X = c2["X"]
                if not final:
                    self.store_xT(X, tt)
                    return
                Yo = M
                for s_ in range(4):
                    for hf in range(2):
                        bank = self.psn()

                        def f():
                            for q4 in range(4):
                                ins = nc.tensor.transpose(bank.t[:, q4 * 128:(q4 + 1) * 128], X.t[:, hf * 4 + q4, s_ * 128:(s_ + 1) * 128], self.identf.t[:])
                            return ins
                        self.op(self.pe, f, r=[X, self.identf], w=[bank])
                        o_ap = Yo.t[:, 2 * s_ + hf, :]
                        if hf:
                            self.op(self.act, lambda: nc.scalar.activation(out=o_ap, in_=bank.t, func=AF.Copy), r=[bank], w=[Yo])
                        else:
                            self.op(self.dve, lambda: nc.vector.tensor_copy(out=o_ap, in_=bank.t), r=[bank], w=[Yo])
                self.dma(out=self.y_out[ts, :].rearrange("(s p) (h f) -> p s h f", p=128, h=2), in_=Yo.t[:].rearrange("p (s h) f -> p s h f", h=2), r=[Yo], w=[], sb=Yo)

            def g2(d0):
                def run():
                    for dc in range(d0, d0 + 2):
                        gate(dc)
                    if d0 == 6:
                        fin()
                return run
            return [a, lambda: None, b, lambda: None, c, g2(0), g2(2), g2(4), g2(6)]

        pcs, st = S1(0)
        for p_ in pcs:
            p_()
        for tt in range(self.NTT):
            if tt + 1 < self.NTT:
                npcs, nst = S1(tt + 1)
            else:
                npcs, nst = [], None
            self.interleave(npcs, S2(tt, st))
            st = nst
        self.end_phase()


def _const_tables(NT, seqs):
    assert sum(seqs) == NT
    pos = np.concatenate([np.arange(s) for s in seqs]).astype(np.float32)
    sid = np.concatenate([np.full(s, i) for i, s in enumerate(seqs)])
    inv = (1.0 / (np.float32(10000.0) ** (np.arange(0, 64, 2, dtype=np.float32) / np.float32(64)))).astype(np.float32)
    angA = pos[None, :] * inv[np.arange(128) % 32][:, None]
    ropeA = np.stack([np.cos(angA), np.sin(angA)]).astype(np.float32)
    angle = (1.0 / (np.float32(10000.0) ** np.linspace(0.0, 1.0, 128, dtype=np.float32))).astype(np.float32)
    angR = pos[None, :] * angle[:, None]
    ropeR = np.stack([np.cos(angR), np.sin(angR)]).astype(np.float32)
    NTT, NCH = NT // 512, NT // 128
    NKP = NCH // 2
    qs = sid[np.arange(NTT) * 512]
    ks = sid[np.arange(NKP) * 256]
    mb = np.where(qs[:, None] == ks[None, :], 0.0, NEG).astype(np.float32).reshape(1, -1)
    maskb = np.repeat(mb, 128, axis=0)
    k = np.arange(128)[:, None].astype(np.float32)
    q = np.arange(128)[None, :].astype(np.float32)
    BIG = np.float32(1.0e6)
    TFm = np.where(q >= k, q - k, BIG).astype(np.float32)
    TBm = np.where(k > q, k - q, BIG).astype(np.float32)
    posq = np.stack([np.repeat(q + 1.0, 128, axis=0), np.repeat(128.0 - q, 128, axis=0)]).astype(np.float32)
    pcol = np.arange(128, dtype=np.float32)
    col = np.stack([127.0 - pcol, pcol, np.full(128, 128.0, np.float32), np.zeros(128, np.float32)], axis=1).astype(np.float32)
    csid = sid[np.arange(NCH) * 128]
    last = np.ones(NCH, bool)
    last[:-1] = csid[1:] != csid[:-1]
    first = np.ones(NCH, bool)
    first[1:] = csid[1:] != csid[:-1]
    mfb = np.stack([np.repeat(np.where(last, 0.0, 1.0)[None, :], 128, axis=0), np.repeat(np.where(first, 0.0, 1.0)[None, :], 128, axis=0)]).astype(np.float32)
    return dict(c_ropeA=ropeA, c_ropeR=ropeR, c_maskb=maskb, c_tftb=np.stack([TFm, TBm]), c_pos=posq, c_col=col, c_mfb=mfb,
                c_ident=np.eye(128, dtype=np.float32))


def _perm_attn():
    idx = []
    for jj in range(4):
        for half in range(2):
            for r in range(4):
                m = 4 * jj + r
                idx.extend(range(m * 64 + half * 32, m * 64 + half * 32 + 32))
    return np.array(idx)


def _perm_ret():
    idx = []
    for h in range(4):
        idx.extend(range(h * 256, (h + 1) * 256, 2))
        idx.extend(range(h * 256 + 1, (h + 1) * 256, 2))
    return np.array(idx)


def _shared_inputs(inp):
    f = lambda a: np.ascontiguousarray(np.asarray(a, dtype=np.float32))
    pa, prr = _perm_attn(), _perm_ret()
    wqkv = f(inp["attn_w_qkv"])
    wqkv = np.concatenate([wqkv[:, :, 0:D][:, :, pa], wqkv[:, :, D:2 * D][:, :, pa], wqkv[:, :, 2 * D:]], axis=2)
    wrin = f(inp["ret_w_in"])
    wrin = np.concatenate([wrin[:, :, 0:D][:, :, prr], wrin[:, :, D:2 * D][:, :, prr], wrin[:, :, 2 * D:]], axis=2)
    norms = [f(inp[k]) for k in ("norm_pre_mix", "norm_post_mix", "norm_pre_mlp", "norm_post_mlp")]
    gains = np.zeros((128, 128), np.float32)
    for l in range(4):
        for k in range(4):
            gains[:, (l * 4 + k) * 8:(l * 4 + k + 1) * 8] = norms[k][l].reshape(8, 128).T
    lam = np.stack([f(inp[k]) for k in ("attn_lambda_q1", "attn_lambda_k1", "attn_lambda_q2", "attn_lambda_k2")], axis=1)
    lamv = np.repeat(lam.reshape(1, -1), 128, axis=0)
    decay = np.repeat(f(inp["ret_decay"]).reshape(1, -1), 128, axis=0)
    return dict(gains=gains, subln=np.ascontiguousarray(f(inp["attn_subln"]).T), lamv=np.ascontiguousarray(lamv),
                decay=np.ascontiguousarray(decay), w_qkv=np.ascontiguousarray(wqkv), w_o=f(inp["attn_w_o"]),
                w_rin=np.ascontiguousarray(wrin), w_rout=f(inp["ret_w_out"]), w_1=f(inp["mlp_w_in"]), w_2=f(inp["mlp_w_out"]),
                w_pp=f(inp["ple_w_proj"]), w_pg=f(inp["ple_w_gate"]))


_NC_CACHE = {}


def run_cores(inp, core_specs, NT, depth=4, debug=False):
    key = (NT, depth, debug)
    if key not in _NC_CACHE:
        _NC_CACHE[key] = KB(NT, depth, debug).build()
    nc = _NC_CACHE[key]
    shared = _shared_inputs(inp)
    in_maps = []
    for x, p, seqs in core_specs:
        m = dict(shared)
        m.update(_const_tables(NT, seqs))
        m["x"] = np.ascontiguousarray(x, dtype=np.float32)
        m["p"] = np.ascontiguousarray(p, dtype=np.float32)
        in_maps.append(m)
    res = run_bass_kernel_spmd(nc, in_maps, core_ids=list(range(len(in_maps))))
    if debug:
        return res.results
    return [r["y"] for r in res.results]


def kernel(**inputs):
    xp = np.asarray(inputs["x_prompt"], dtype=np.float32)
    xs = np.asarray(inputs["x_sample"], dtype=np.float32)
    pp = np.asarray(inputs["p_prompt"], dtype=np.float32)
    psm = np.asarray(inputs["p_sample"], dtype=np.float32)
    NT = 8192
    specs = []
    for c in range(4):
        specs.append((xp[4 * c:4 * c + 4].reshape(NT, D), pp[:, 4 * c:4 * c + 4].reshape(4, NT, PLE), [2048] * 4))
    for c in range(4):
        specs.append((xs[c], psm[:, c], [8192]))
    ys = run_cores(inputs, specs, NT, 4)
    y_prompt = np.stack([ys[c].reshape(4, 2048, D) for c in range(4)]).reshape(16, 2048, D).astype(np.float32)
    y_sample = np.stack([ys[4 + c] for c in range(4)]).astype(np.float32)
    return (y_prompt, y_sample)
```

```python
import math
from contextlib import ExitStack
import numpy as np
import concourse.bass as bass
import concourse.mybir as mybir
from concourse.bass_utils import run_bass_kernel_spmd

F32 = mybir.dt.float32
BF16 = mybir.dt.bfloat16
AF = mybir.ActivationFunctionType
ALU = mybir.AluOpType
AX = mybir.AxisListType

D = 1024
DFF = 4096
PLE = 256
TT = 512
NEG = -30000.0


class Sem:
    def __init__(self, nc, name):
        self.h = nc.alloc_semaphore(name)
        self.cnt = 0


class Eng:
    def __init__(self, nc, name, h, same_raw):
        self.name = name
        self.h = h
        self.sem = Sem(nc, "e_" + name)
        self.waited = {}
        self.same_raw = same_raw


class Buf:
    def __init__(self, t=None):
        self.t = t
        self.w = {}
        self.r = {}
        self.dsem = None


class Rot:
    def __init__(self, items):
        self.items = items
        self.i = 0

    def next(self):
        b = self.items[self.i % len(self.items)]
        self.i += 1
        return b


class KB:
    def __init__(self, NT, depth=4, debug=False):
        self.debug = debug
        self.nc = nc = bass.Bass("TRN2", target_bir_lowering=False)
        self.NT = NT
        self.NTT = NT // TT
        self.NCH = NT // 128
        self.depth = depth
        self.pe = Eng(nc, "pe", nc.tensor, False)
        self.act = Eng(nc, "act", nc.scalar, True)
        self.dve = Eng(nc, "dve", nc.vector, True)
        self.pool = Eng(nc, "pool", nc.gpsimd, True)
        self.sp = Eng(nc, "sp", nc.sync, False)
        self.engs = [self.pe, self.act, self.dve, self.pool, self.sp]
        self.all_sems = [e.sem for e in self.engs]
        self.free_dsems = []
        self.phase_dsems = []
        self.uid = 0
        self.st = None
        self.alt = 0

    def _need(self, r, w):
        need = {}
        for b in r:
            for S, v in b.w.items():
                if v > need.get(S, 0):
                    need[S] = v
        for b in w:
            for S, v in b.w.items():
                if v > need.get(S, 0):
                    need[S] = v
            for S, v in b.r.items():
                if v > need.get(S, 0):
                    need[S] = v
        return need

    def _waits(self, E, need, r):
        for S, v in need.items():
            if S is E.sem:
                continue
            if v > E.waited.get(S, 0):
                E.h.wait_ge(S.h, v)
                E.waited[S] = v
        if E.same_raw:
            raw = 0
            for b in r:
                v = b.w.get(E.sem, 0)
                if v > raw:
                    raw = v
            if raw > E.sem.cnt - 2 and raw > E.waited.get(E.sem, 0):
                E.h.wait_ge(E.sem.h, raw)
                E.waited[E.sem] = raw

    def op(self, E, fn, r=(), w=()):
        self._waits(E, self._need(r, w), r)
        ins = fn()
        E.sem.cnt += 1
        ins.then_inc(E.sem.h, 1)
        v = E.sem.cnt
        for b in r:
            b.r[E.sem] = v
        for b in w:
            b.w[E.sem] = v
        return ins

    def dma(self, out, in_, r=(), w=(), sb=None):
        Q = self.sp
        if sb.dsem is None:
            if self.free_dsems:
                sb.dsem = self.free_dsems.pop()
            else:
                self.uid += 1
                sb.dsem = Sem(self.nc, "d%d" % self.uid)
                self.all_sems.append(sb.dsem)
            self.phase_dsems.append(sb.dsem)
        self._waits(Q, self._need(r, w), ())
        ins = Q.h.dma_start(out=out, in_=in_)
        S = sb.dsem
        S.cnt += 16
        ins.then_inc(S.h, 16)
        for b in r:
            b.r[S] = S.cnt
        for b in w:
            b.w[S] = S.cnt

    def barrier(self):
        for E in self.engs:
            for S in self.all_sems:
                if S is E.sem:
                    continue
                if S.cnt > E.waited.get(S, 0):
                    E.h.wait_ge(S.h, S.cnt)
                    E.waited[S] = S.cnt

    def sb(self, shape, dt, name="t"):
        self.uid += 1
        t = self.st.enter_context(self.nc.sbuf_tensor("%s_%d" % (name, self.uid), list(shape), dt))
        return Buf(t)

    def rot(self, n, shape, dt, name="r"):
        return Rot([self.sb(shape, dt, name) for _ in range(n)])

    def begin_phase(self):
        self.st = ExitStack()
        self.st.__enter__()
        self.rt = self.rot(2, [128, TT], F32, "rt")

    def rsqrt(self, src_ap, src_bufs, out_ap, out_buf, addc, n=TT):
        nc = self.nc
        t = self.rt.next()
        self.op(self.act, lambda: nc.scalar.activation(out=t.t[:, :n], in_=src_ap, func=AF.Ln, bias=self.cbias.t[:, self.cbias_col[addc]:self.cbias_col[addc] + 1]), r=src_bufs + [self.cbias], w=[t])
        self.op(self.act, lambda: nc.scalar.activation(out=out_ap, in_=t.t[:, :n], func=AF.Exp, scale=-0.5), r=[t], w=[out_buf])

    def end_phase(self):
        self.barrier()
        self.free_dsems.extend(self.phase_dsems)
        self.phase_dsems = []
        self.st.__exit__(None, None, None)
        self.st = None

    def psn(self):
        b = self.psb[self.psi % 8]
        self.psi += 1
        return b

    def psn2(self):
        if self.psi % 2:
            self.psi += 1
        a = self.psb[self.psi % 8]
        b = self.psb[(self.psi + 1) % 8]
        self.psi += 2
        return a, b

    def ew(self):
        self.alt += 1
        return self.dve if self.alt % 2 else self.pool

    def mm(self, out, lhsT, rhs, start=True, stop=True):
        return self.nc.tensor.matmul(out, lhsT=lhsT, rhs=rhs, start=start, stop=stop)

    def declare(self):
        nc, NT, NCH = self.nc, self.NT, self.NCH
        di = lambda n, s: nc.dram_tensor(n, list(s), F32, kind="ExternalInput").ap()
        self.x_in = di("x", [NT, D])
        self.p_in = di("p", [self.depth, NT, PLE])
        self.gains = di("gains", [128, 16 * 8])
        self.subln = di("subln", [128, 2])
        self.lamv = di("lamv", [128, 2 * 4 * 64])
        self.decay = di("decay", [128, 16])
        self.w_qkv = di("w_qkv", [2, D, 3 * D])
        self.w_o = di("w_o", [2, D, D])
        self.w_rin = di("w_rin", [2, D, 6 * D])
        self.w_rout = di("w_rout", [2, 2 * D, D])
        self.w_1 = di("w_1", [4, D, DFF])
        self.w_2 = di("w_2", [4, DFF, D])
        self.w_pp = di("w_pp", [4, PLE, D])
        self.w_pg = di("w_pg", [4, D, D])
        self.c_ident = di("c_ident", [128, 128])
        self.c_ropeA = di("c_ropeA", [2, 128, NT])
        self.c_ropeR = di("c_ropeR", [2, 128, NT])
        self.c_maskb = di("c_maskb", [128, self.NTT * (NCH // 2)])
        self.c_tftb = di("c_tftb", [2, 128, 128])
        self.c_pos = di("c_pos", [2, 128, 128])
        self.c_col = di("c_col", [128, 4])
        self.c_mfb = di("c_mfb", [2, 128, NCH])
        self.y_out = nc.dram_tensor("y", [NT, D], F32, kind="ExternalOutput").ap()
        ds = lambda n, s, dt: nc.dram_tensor(n, list(s), dt, kind="ExternalOutput" if self.debug else "Internal").ap()
        self.xT = ds("s_xT", [self.NTT, 128, 8, TT], F32)
        self.QTs = ds("s_QT", [8, 128, NT], BF16)
        self.KTs = ds("s_KT", [8, 128, NT], BF16)
        self.VS = ds("s_V", [NT, D], BF16)
        self.AT = ds("s_AT", [self.NTT, 128, 16, TT], BF16)
        self.H2s = ds("s_H2", [self.NTT, 128, 8, TT], BF16)
        self.Us = ds("s_U", [self.NTT, 128, 32, TT], BF16)
        self.RK = ds("s_RK", [NT, D], BF16)
        self.RV = ds("s_RV", [NT, 2 * D], BF16)
        self.RG = ds("s_RG", [NT, 2 * D], BF16)
        self.OB = ds("s_OB", [NT, 2 * D], BF16)
        mk = lambda n: [Buf() for _ in range(n)]
        self.k_xT = mk(self.NTT)
        self.k_QT = mk(8)
        self.k_KT = mk(8)
        self.k_V = mk(1)
        self.k_AT = mk(self.NTT)
        self.k_H2 = mk(self.NTT)
        self.k_U = mk(self.NTT)
        self.k_R = mk(NCH)
        self.k_RQ = mk(self.NTT)
        self.k_OB = mk(NCH)

    def build(self):
        nc = self.nc
        self.declare()
        with ExitStack() as gst:
            self.st = gst
            ps = nc.alloc_psum_tensor("ps", [128, 8, 512], F32)
            self.ps = ps
            self.psbf = ps[:, :, :].bitcast(BF16)
            self.psb = [Buf(ps[:, b, :]) for b in range(8)]
            self.psi = 0
            self.ones = self.sb([128, 128], BF16, "ones")
            self.cbias = self.sb([128, 4], F32, "cbias")
            self.cbias_col = {D * 1e-6: 0, 128 * 1e-5: 1, 512 * 1e-6: 2}
            for v, cidx in self.cbias_col.items():
                self.op(self.dve, lambda: nc.vector.memset(self.cbias.t[:, cidx:cidx + 1], float(v)), w=[self.cbias])
            self.onesf = self.sb([128, 128], F32, "onesf")
            self.identf = self.sb([128, 128], F32, "identf")
            self.identb = self.sb([128, 128], BF16, "identb")
            self.gs = self.sb([128, 128], F32, "gs")
            self.sgs = self.sb([128, 2], F32, "sgs")
            self.lamt = self.sb([128, 8], F32, "lamt")
            self.lg = self.sb([128, 16], F32, "lg")
            self.maskb = self.sb([128, self.NTT * (self.NCH // 2)], F32, "maskb")
            lv = self.sb([128, 512], F32, "lv")
            pr = self.sb([128, 64], F32, "pr")
            sraw = self.sb([128, 2], F32, "sraw")
            dv, act = self.dve, self.act
            self.dma(out=self.identf.t[:], in_=self.c_ident, w=[self.identf], sb=self.identf)
            self.dma(out=self.gs.t[:], in_=self.gains, w=[self.gs], sb=self.gs)
            self.dma(out=sraw.t[:], in_=self.subln, w=[sraw], sb=sraw)
            self.dma(out=lv.t[:], in_=self.lamv, w=[lv], sb=lv)
            self.dma(out=self.lg.t[:], in_=self.decay, w=[self.lg], sb=self.lg)
            self.dma(out=self.maskb.t[:], in_=self.c_maskb, w=[self.maskb], sb=self.maskb)
            self.op(dv, lambda: nc.vector.memset(self.ones.t[:], 1.0), w=[self.ones])
            self.op(dv, lambda: nc.vector.memset(self.onesf.t[:], 1.0), w=[self.onesf])
            self.op(dv, lambda: nc.vector.tensor_copy(out=self.identb.t[:], in_=self.identf.t[:]), r=[self.identf], w=[self.identb])
            self.op(dv, lambda: nc.vector.tensor_scalar(out=self.gs.t[:], in0=self.gs.t[:], scalar1=32.0, scalar2=None, op0=ALU.mult), r=[self.gs], w=[self.gs])
            self.op(act, lambda: nc.scalar.activation(out=self.lg.t[:], in_=self.lg.t[:], func=AF.Exp), r=[self.lg], w=[self.lg])
            self.op(dv, lambda: nc.vector.tensor_scalar(out=self.lg.t[:], in0=self.lg.t[:], scalar1=-1.0, scalar2=None, op0=ALU.mult), r=[self.lg], w=[self.lg])
            for j in range(2):
                lam_init = 0.8 - 0.6 * math.exp(-0.3 * (2 * j))
                for e in range(2):
                    a0 = (j * 4 + 2 * e) * 64
                    self.op(dv, lambda: nc.vector.tensor_tensor(out=pr.t[:], in0=lv.t[:, a0:a0 + 64], in1=lv.t[:, a0 + 64:a0 + 128], op=ALU.mult), r=[lv], w=[pr])
                    self.op(dv, lambda: nc.vector.reduce_sum(out=self.lamt.t[:, j * 4 + 2 + e:j * 4 + 3 + e], in_=pr.t[:], axis=AX.X), r=[pr], w=[self.lamt])
                self.op(act, lambda: nc.scalar.activation(out=self.lamt.t[:, j * 4 + 2:j * 4 + 4], in_=self.lamt.t[:, j * 4 + 2:j * 4 + 4], func=AF.Exp), r=[self.lamt], w=[self.lamt])
                self.op(dv, lambda: nc.vector.tensor_tensor(out=self.lamt.t[:, j * 4:j * 4 + 1], in0=self.lamt.t[:, j * 4 + 2:j * 4 + 3], in1=self.lamt.t[:, j * 4 + 3:j * 4 + 4], op=ALU.subtract), r=[self.lamt], w=[self.lamt])
                self.op(dv, lambda: nc.vector.tensor_scalar(out=self.lamt.t[:, j * 4:j * 4 + 1], in0=self.lamt.t[:, j * 4:j * 4 + 1], scalar1=lam_init, scalar2=None, op0=ALU.add), r=[self.lamt], w=[self.lamt])
                self.op(dv, lambda: nc.vector.tensor_scalar(out=self.lamt.t[:, j * 4 + 1:j * 4 + 2], in0=self.lamt.t[:, j * 4:j * 4 + 1], scalar1=-1.0, scalar2=None, op0=ALU.mult), r=[self.lamt], w=[self.lamt])
                self.op(dv, lambda: nc.vector.tensor_scalar(out=self.sgs.t[:, j:j + 1], in0=sraw.t[:, j:j + 1], scalar1=math.sqrt(128.0) * (1.0 - lam_init), scalar2=None, op0=ALU.mult), r=[sraw], w=[self.sgs])
            self.barrier()
            for l in range(self.depth):
                if l % 2 == 0:
                    self.phase_PD_attn(l)
                    self.phase_attn(l)
                else:
                    self.phase_PD_ret(l)
                    self.phase_ret(l)
                self.phase_PA(l)
                self.phase_PB(l)
                self.phase_PC(l)
            self.barrier()
            self.st = None
        return nc

    def load_weight(self, Wd, Wb, stage, sw=2048):
        nc = self.nc
        K, Fd = Wd.shape
        i = 0
        for c in range(K // 128):
            for f0 in range(0, Fd, sw):
                fs = min(sw, Fd - f0)
                s = stage.next()
                self.dma(out=s.t[:, :fs], in_=Wd[c * 128:(c + 1) * 128, f0:f0 + fs], w=[s], sb=s)
                k = i % 3
                i += 1
                if k == 0:
                    self.op(self.dve, lambda: nc.vector.tensor_copy(out=Wb.t[:, c, f0:f0 + fs], in_=s.t[:, :fs]), r=[s], w=[Wb])
                elif k == 1:
                    self.op(self.pool, lambda: nc.gpsimd.tensor_copy(out=Wb.t[:, c, f0:f0 + fs], in_=s.t[:, :fs]), r=[s], w=[Wb])
                else:
                    self.op(self.act, lambda: nc.scalar.activation(out=Wb.t[:, c, f0:f0 + fs], in_=s.t[:, :fs], func=AF.Copy), r=[s], w=[Wb])

    def tsl(self, tt):
        return slice(tt * TT, (tt + 1) * TT)

    def load_xT(self, X, tt):
        self.dma(out=X.t[:], in_=self.xT[tt], r=[self.k_xT[tt]], w=[X], sb=X)

    def store_xT(self, X, tt):
        self.dma(out=self.xT[tt], in_=X.t[:], r=[X], w=[self.k_xT[tt]], sb=X)

    def norm_rstd(self, X, nch, Dn, eps, sq, rstd):
        bank = self.norm_ss(X, nch, sq)
        self.rsqrt(bank.t, [bank], rstd.t[:], rstd, Dn * eps)

    def norm_ss(self, X, nch, sq):
        nc = self.nc
        self.op(self.act, lambda: nc.scalar.activation(out=sq.t[:, :nch, :], in_=X.t[:, :nch, :], func=AF.Square), r=[X], w=[sq])
        bank = self.psn()

        def f():
            for c in range(nch):
                ins = self.mm(bank.t, self.ones.t[:], sq.t[:, c, :], c == 0, c == nch - 1)
            return ins
        self.op(self.pe, f, r=[sq, self.ones], w=[bank])
        return bank

    def scale_norm(self, H, X, gcol, rstd):
        nc = self.nc
        for c in range(8):
            E = self.dve
            sc = 32.0 if gcol is None else self.gs.t[:, gcol + c:gcol + c + 1]
            rr = [X, rstd] + ([] if gcol is None else [self.gs])
            self.op(E, lambda: E.h.scalar_tensor_tensor(out=H.t[:, c, :], in0=X.t[:, c, :], scalar=sc, in1=rstd.t[:], op0=ALU.mult, op1=ALU.mult), r=rr, w=[H])

    def resid_norm(self, X, M, gcol, rstd, Mr):
        nc = self.nc
        self.op(self.dve, lambda: nc.vector.tensor_tensor(out=Mr.t[:], in0=M.t[:], in1=rstd.t[:].unsqueeze(1).to_broadcast([128, 8, TT]), op=ALU.mult), r=[M, rstd], w=[Mr])
        for c in range(8):
            E = self.dve
            self.op(E, lambda: E.h.scalar_tensor_tensor(out=X.t[:, c, :], in0=Mr.t[:, c, :], scalar=self.gs.t[:, gcol + c:gcol + c + 1], in1=X.t[:, c, :], op0=ALU.mult, op1=ALU.add), r=[Mr, X, self.gs], w=[X])

    def proj_fm(self, W, nk, A, M, col0=0):
        for dc in range(8):
            self.proj_piece(W, nk, A, M, dc, col0)

    def proj_piece(self, W, nk, A, M, dc, col0=0):
        nc = self.nc
        bank = self.psn()

        def f():
            for k in range(nk):
                ins = self.mm(bank.t, W.t[:, k, col0 + dc * 128:col0 + (dc + 1) * 128], A.t[:, k, :], k == 0, k == nk - 1)
            return ins
        self.op(self.pe, f, r=[W, A], w=[bank])
        self.op(self.act, lambda: nc.scalar.activation(out=M.t[:, dc, :], in_=bank.t, func=AF.Copy), r=[bank], w=[M])

    def interleave(self, pieces, steps):
        n = max(len(pieces), len(steps))
        for k in range(n):
            if k < len(pieces):
                pieces[k]()
            if k < len(steps):
                steps[k]()

    def x_from_input(self, X, tt, xin):
        nc = self.nc
        self.dma(out=xin.t[:], in_=self.x_in[self.tsl(tt), :].rearrange("(s p) f -> p s f", p=128), w=[xin], sb=xin)
        for c in range(8):
            bank = self.psn()

            def f():
                for s in range(4):
                    ins = nc.tensor.transpose(bank.t[:, s * 128:(s + 1) * 128], xin.t[:, s, c * 128:(c + 1) * 128], self.identf.t[:])
                return ins
            self.op(self.pe, f, r=[xin, self.identf], w=[bank])
            if c % 2:
                self.op(self.act, lambda: nc.scalar.activation(out=X.t[:, c, :], in_=bank.t, func=AF.Copy), r=[bank], w=[X])
            else:
                self.op(self.dve, lambda: nc.vector.tensor_copy(out=X.t[:, c, :], in_=bank.t), r=[bank], w=[X])

    def rope_pair(self, bankA, bankB, cs, sn, Ap, Bp, tmp, scale):
        nc = self.nc
        Af, Bf, t1, t2, t3, t4 = [tmp.next() for _ in range(6)]
        self.op(self.act, lambda: nc.scalar.activation(out=Af.t[:], in_=bankA.t, func=AF.Copy, scale=scale), r=[bankA], w=[Af])
        self.op(self.act, lambda: nc.scalar.activation(out=Bf.t[:], in_=bankB.t, func=AF.Copy, scale=scale), r=[bankB], w=[Bf])
        self.op(self.dve, lambda: nc.vector.tensor_tensor(out=t1.t[:], in0=Af.t[:], in1=cs.t[:], op=ALU.mult), r=[Af, cs], w=[t1])
        self.op(self.dve, lambda: nc.vector.tensor_tensor(out=t2.t[:], in0=Bf.t[:], in1=sn.t[:], op=ALU.mult), r=[Bf, sn], w=[t2])
        self.op(self.pool, lambda: nc.gpsimd.tensor_tensor(out=t3.t[:], in0=Bf.t[:], in1=cs.t[:], op=ALU.mult), r=[Bf, cs], w=[t3])
        self.op(self.pool, lambda: nc.gpsimd.tensor_tensor(out=t4.t[:], in0=Af.t[:], in1=sn.t[:], op=ALU.mult), r=[Af, sn], w=[t4])
        self.op(self.dve, lambda: nc.vector.tensor_tensor(out=Ap, in0=t1.t[:], in1=t2.t[:], op=ALU.subtract), r=[t1, t2], w=[self._apbuf])
        self.op(self.pool, lambda: nc.gpsimd.tensor_tensor(out=Bp, in0=t3.t[:], in1=t4.t[:], op=ALU.add), r=[t3, t4], w=[self._bpbuf])

    def phase_PD_attn(self, l):
        nc = self.nc
        j = l // 2
        self.begin_phase()
        W = self.sb([128, 8, 3 * D], BF16, "W")
        stage = self.rot(2, [128, 2048], F32, "stg")
        self.load_weight(self.w_qkv[j], W, stage)
        Xp = self.rot(2, [128, 8, TT], F32, "X")
        xin = self.rot(1, [128, 4, D], F32, "xin") if l == 0 else None
        sq = self.sb([128, 8, TT], BF16, "sq")
        rs = self.rot(2, [128, TT], F32, "rs")
        Hp = self.rot(2, [128, 8, TT], BF16, "H")
        csp = self.rot(2, [128, TT], F32, "cs")
        snp = self.rot(2, [128, TT], F32, "sn")
        tmp = self.rot(12, [128, TT], F32, "tmp")
        ABp = self.rot(4, [128, 2, TT], BF16, "AB")
        Vp = self.rot(2, [128, 4, D], BF16, "Vt")
        def S1(tt):
            ts = self.tsl(tt)
            X = Xp.next()
            if l == 0:
                self.x_from_input(X, tt, xin.next())
                self.store_xT(X, tt)
            else:
                self.load_xT(X, tt)
            r = rs.next()
            self.norm_rstd(X, 8, D, 1e-6, sq, r)
            H = Hp.next()
            self.scale_norm(H, X, (l * 4 + 0) * 8, r)
            return (H,)

        cst = {}

        def S2(tt, H, part):
            ts = self.tsl(tt)
            if part == 0:
                cs, sn = csp.next(), snp.next()
                self.dma(out=cs.t[:], in_=self.c_ropeA[0, :, ts], w=[cs], sb=cs)
                self.dma(out=sn.t[:], in_=self.c_ropeA[1, :, ts], w=[sn], sb=sn)
                cst["cs"], cst["sn"] = cs, sn
            cs, sn = cst["cs"], cst["sn"]
            for which in ((0,) if part == 0 else (1,)):
                dst = self.QTs if which == 0 else self.KTs
                ktok = self.k_QT if which == 0 else self.k_KT
                for jj in range(4):
                    bA, bB = self.psn(), self.psn()
                    for bank, ch in ((bA, 2 * jj), (bB, 2 * jj + 1)):
                        c0 = which * D + ch * 128

                        def f():
                            for k in range(8):
                                ins = self.mm(bank.t, W.t[:, k, c0:c0 + 128], H.t[:, k, :], k == 0, k == 7)
                            return ins
                        self.op(self.pe, f, r=[W, H], w=[bank])
                    AB = ABp.next()
                    self._apbuf = AB
                    self._bpbuf = AB
                    self.rope_pair(bA, bB, cs, sn, AB.t[:, 0, :], AB.t[:, 1, :], tmp, 1.0)
                    for rr in range(4):
                        m = 4 * jj + rr
                        hd, cc = m // 2, m % 2
                        for ab in range(2):
                            self.dma(out=dst[hd, cc * 64 + ab * 32:cc * 64 + ab * 32 + 32, ts], in_=AB.t[32 * rr:32 * rr + 32, ab, :], r=[AB], w=[ktok[hd]], sb=AB)
            if part == 0:
                return
            Vt = Vp.next()
            for s in range(4):
                for hf in range(2):
                    bank = self.psn()
                    c0 = 2 * D + hf * 512

                    def f():
                        for k in range(8):
                            ins = self.mm(bank.t, H.t[:, k, s * 128:(s + 1) * 128], W.t[:, k, c0:c0 + 512], k == 0, k == 7)
                        return ins
                    self.op(self.pe, f, r=[W, H], w=[bank])
                    if hf:
                        self.op(self.act, lambda: nc.scalar.activation(out=Vt.t[:, s, hf * 512:(hf + 1) * 512], in_=bank.t, func=AF.Copy), r=[bank], w=[Vt])
                    else:
                        self.op(self.dve, lambda: nc.vector.tensor_copy(out=Vt.t[:, s, hf * 512:(hf + 1) * 512], in_=bank.t), r=[bank], w=[Vt])
            self.dma(out=self.VS[ts, :].rearrange("(s p) f -> p s f", p=128), in_=Vt.t[:], r=[Vt], w=[self.k_V[0]], sb=Vt)

        cur = S1(0)
        for tt in range(self.NTT):
            S2(tt, cur[0], 0)
            nxt = S1(tt + 1) if tt + 1 < self.NTT else None
            S2(tt, cur[0], 1)
            cur = nxt
        self.end_phase()

    def phase_attn(self, l):
        nc = self.nc
        j = l // 2
        NT, NTT, NCH = self.NT, self.NTT, self.NCH
        NKP = NCH // 2
        dv = self.dve
        self.begin_phase()
        KTp = self.rot(2, [128, 2, NT], BF16, "KT")
        for kb_ in KTp.items:
            self.op(self.pool, lambda: nc.gpsimd.memset(kb_.t[:], 0.0), w=[kb_])
        QTp = self.rot(2, [128, NT], BF16, "QT")
        Vp = self.rot(2, [128, NCH, 128], BF16, "Vh")
        Pp = self.rot(6, [128, 2, TT], BF16, "P")
        Ocp = self.rot(3, [128, TT], F32, "Oc")
        zsp = self.rot(2, [128, TT], F32, "zs")
        rzp = self.rot(3, [128, TT], F32, "rz")
        t0p = self.rot(2, [128, TT], F32, "t0")
        t1p = self.rot(2, [128, TT], F32, "t1")
        ddp = self.rot(2, [128, TT], F32, "dd")
        sqp = self.rot(2, [128, TT], BF16, "sqd")
        rrp = self.rot(2, [128, TT], F32, "rr")
        yp = self.rot(2, [128, TT], BF16, "yt")
        ZEp = [self.sb([128, 2, TT], F32, "ZE") for _ in range(2)]
        ZOp = [[self.sb([128, TT], F32, "ZO") for _ in range(2)] for _ in range(2)]
        slots = [(self.psb[2 * k], self.psb[2 * k + 1], self.ps[:, 2 * k:2 * k + 2, :]) for k in range(3)]
        O, Z = self.psb[6], self.psb[7]
        heads = {}
        slot_of = {}
        sctr = [0]

        def next_slot():
            sl = slots[sctr[0] % 3]
            sctr[0] += 1
            return sl

        def load_head(h):
            KT, QT, V = KTp.next(), QTp.next(), Vp.next()
            self.dma(out=KT.t[0:64, 0, :], in_=self.KTs[h, 0:64, :], r=[self.k_KT[h]], w=[KT], sb=KT)
            self.dma(out=KT.t[64:128, 1, :], in_=self.KTs[h, 64:128, :], r=[self.k_KT[h]], w=[KT], sb=KT)
            self.dma(out=QT.t[:], in_=self.QTs[h], r=[self.k_QT[h]], w=[QT], sb=QT)
            self.dma(out=V.t[:], in_=self.VS[:, h * 128:(h + 1) * 128].rearrange("(kb p) d -> p kb d", p=128), r=[self.k_V[0]], w=[V], sb=V)
            heads[h] = (KT, QT, V)

        items = [(h, qt, c, kp) for h in range(8) for qt in range(NTT) for c in range(2) for kp in range(NKP)]
        state = {"t0": None}
        pending = []
        big = NKP >= 16
        KZ = 6 if big else 5
        D_FZ, D_RZ, D_B, D_C = (2, 3, 4, 6) if big else (1, 1, 1, 2)

        def emit_qk(i):
            h, qt, c, kp = items[i]
            if qt == 0 and c == 0 and kp == 0 and h == 0:
                load_head(0)
            if qt == 0 and c == 0 and kp == 3 and h + 1 < 8:
                load_head(h + 1)
            KT, QT, V = heads[h]
            b0, b1, sap = slot_of[i] = next_slot()

            def f():
                for jx in range(2):
                    kb = 2 * kp + jx
                    ins = self.mm(sap[:, jx, :], KT.t[:, c, kb * 128:(kb + 1) * 128], QT.t[:, qt * TT:(qt + 1) * TT], True, True)
                return ins
            self.op(self.pe, f, r=[KT, QT], w=[b0, b1])

        def emit_exp(i):
            h, qt, c, kp = items[i]
            b0, b1, sap = slot_of.pop(i)
            P = Pp.next()
            col = qt * NKP + kp
            self.op(self.act, lambda: nc.scalar.activation(out=P.t[:], in_=sap, func=AF.Exp, bias=self.maskb.t[:, col:col + 1], scale=0.125), r=[b0, b1, self.maskb], w=[P])
            return P

        def emit_av(i, P):
            h, qt, c, kp = items[i]
            KT, QT, V = heads[h]
            g = (h * NTT + qt) * 2 + c
            ZE, ZO = ZEp[g % 2], ZOp[g % 2]
            peZ = (kp >= KZ)
            if kp == 0:
                self.op(dv, lambda: nc.vector.tensor_copy(out=ZE.t[:], in_=P.t[:]), r=[P], w=[ZE])
            elif not peZ:
                self.op(dv, lambda: nc.vector.tensor_tensor(out=ZE.t[:], in0=ZE.t[:], in1=P.t[:], op=ALU.add), r=[P, ZE], w=[ZE])
            else:
                ZOk = ZO[kp % 2]
                if kp < KZ + 2:
                    self.op(dv, lambda: nc.vector.tensor_copy(out=ZOk.t[:], in_=P.t[:, 0, :]), r=[P], w=[ZOk])
                else:
                    self.op(dv, lambda: nc.vector.tensor_tensor(out=ZOk.t[:], in0=ZOk.t[:], in1=P.t[:, 0, :], op=ALU.add), r=[P, ZOk], w=[ZOk])

            def f():
                for jx in range(2):
                    kb = 2 * kp + jx
                    first = (kp == 0 and jx == 0)
                    last = (kp == NKP - 1 and jx == 1)
                    ins = self.mm(O.t, V.t[:, kb, :], P.t[:, jx, :], first, last)
                if peZ:
                    ins = self.mm(Z.t, self.ones.t[:], P.t[:, 1, :], kp == KZ, False)
                return ins
            self.op(self.pe, f, r=[V, P, self.ones], w=[O, Z] if peZ else [O])
            if kp != NKP - 1:
                return
            Oc = Ocp.next()
            self.op(dv, lambda: nc.vector.tensor_copy(out=Oc.t[:], in_=O.t), r=[O], w=[Oc])
            zs = zsp.next()
            self.op(dv, lambda: nc.vector.tensor_tensor(out=zs.t[:], in0=ZE.t[:, 0, :], in1=ZE.t[:, 1, :], op=ALU.add), r=[ZE], w=[zs])
            for kk in (KZ, KZ + 1):
                if NKP > kk:
                    ZOk = ZO[kk % 2]
                    self.op(dv, lambda: nc.vector.tensor_tensor(out=zs.t[:], in0=zs.t[:], in1=ZOk.t[:], op=ALU.add), r=[zs, ZOk], w=[zs])
            rz = rzp.next()

            def stageFZ():
                self.op(self.pe, lambda: self.mm(Z.t, self.onesf.t[:], zs.t[:], NKP <= KZ, True), r=[zs, self.onesf], w=[Z])

            def stageRZ():
                self.op(dv, lambda: nc.vector.reciprocal(out=rz.t[:], in_=Z.t), r=[Z], w=[rz])
            pending.append((i + D_FZ, stageFZ))
            pending.append((i + D_RZ, stageRZ))

            def stageB():
                if c == 0:
                    t0 = t0p.next()
                    self.op(dv, lambda: nc.vector.tensor_tensor(out=t0.t[:], in0=Oc.t[:], in1=rz.t[:], op=ALU.mult), r=[Oc, rz], w=[t0])
                    state["t0"] = t0
                    return
                t0 = state["t0"]
                t1 = t1p.next()
                self.op(dv, lambda: nc.vector.tensor_tensor(out=t1.t[:], in0=Oc.t[:], in1=rz.t[:], op=ALU.mult), r=[Oc, rz], w=[t1])
                dd = ddp.next()
                self.op(dv, lambda: nc.vector.scalar_tensor_tensor(out=dd.t[:], in0=t1.t[:], scalar=self.lamt.t[:, j * 4 + 1:j * 4 + 2], in1=t0.t[:], op0=ALU.mult, op1=ALU.add), r=[t1, t0, self.lamt], w=[dd])
                sqd = sqp.next()
                self.op(dv, lambda: nc.vector.tensor_tensor(out=sqd.t[:], in0=dd.t[:], in1=dd.t[:], op=ALU.mult), r=[dd], w=[sqd])
                self.op(self.pe, lambda: self.mm(Z.t, self.ones.t[:], sqd.t[:], True, True), r=[sqd, self.ones], w=[Z])

                def stageC():
                    rr = rrp.next()
                    self.rsqrt(Z.t, [Z], rr.t[:], rr, 128 * 1e-5)
                    yt = yp.next()
                    self.op(dv, lambda: nc.vector.scalar_tensor_tensor(out=yt.t[:], in0=dd.t[:], scalar=self.sgs.t[:, j:j + 1], in1=rr.t[:], op0=ALU.mult, op1=ALU.mult), r=[dd, rr, self.sgs], w=[yt])
                    self.dma(out=self.AT[qt, :, h, :], in_=yt.t[:], r=[yt], w=[self.k_AT[qt]], sb=yt)
                pending.append((i + D_C, stageC))
            pending.append((i + D_B, stageB))

        def run_pending(upto):
            pending.sort(key=lambda e: e[0])
            while pending and (upto is None or pending[0][0] <= upto):
                pending.pop(0)[1]()
                pending.sort(key=lambda e: e[0])

        n = len(items)
        emit_qk(0)
        emit_qk(1)
        for i in range(n):
            if i + 2 < n:
                emit_qk(i + 2)
            P = emit_exp(i)
            emit_av(i, P)
            run_pending(i)
        run_pending(None)
        self.end_phase()

    def phase_PD_ret(self, l):
        nc = self.nc
        j = l // 2
        self.begin_phase()
        W = self.sb([128, 8, 6 * D], BF16, "W")
        stage = self.rot(2, [128, 1024], F32, "stg")
        self.load_weight(self.w_rin[j], W, stage, 1024)
        Xp = self.rot(1, [128, 8, TT], F32, "X")
        sq = self.sb([128, 8, TT], BF16, "sq")
        rs = self.rot(2, [128, TT], F32, "rs")
        Hp = self.rot(1, [128, 8, TT], BF16, "H")
        csp = self.rot(1, [128, TT], F32, "cs")
        snp = self.rot(1, [128, TT], F32, "sn")
        tmp = self.rot(6, [128, TT], F32, "tmp")
        QKp = self.rot(2, [128, 8, TT], BF16, "QK")
        Ktp = self.rot(1, [128, 4, D], BF16, "Ktm")
        Vp = self.rot(1, [128, 4, 2 * D], BF16, "Vt")
        for tt in range(self.NTT):
            ts = self.tsl(tt)
            X = Xp.next()
            self.load_xT(X, tt)
            r = rs.next()
            self.norm_rstd(X, 8, D, 1e-6, sq, r)
            H = Hp.next()
            self.scale_norm(H, X, (l * 4 + 0) * 8, r)
            cs, sn = csp.next(), snp.next()
            self.dma(out=cs.t[:], in_=self.c_ropeR[0, :, ts], w=[cs], sb=cs)
            self.dma(out=sn.t[:], in_=self.c_ropeR[1, :, ts], w=[sn], sb=sn)
            KR = None
            for which in range(2):
                dst = self.QTs if which == 0 else self.KTs
                QK = QKp.next()
                for hh in range(4):
                    bA, bB = self.psn(), self.psn()
                    for bank, ch in ((bA, 2 * hh), (bB, 2 * hh + 1)):
                        c0 = which * D + ch * 128

                        def f():
                            for k in range(8):
                                ins = self.mm(bank.t, W.t[:, k, c0:c0 + 128], H.t[:, k, :], k == 0, k == 7)
                            return ins
                        self.op(self.pe, f, r=[W, H], w=[bank])
                    self._apbuf = QK
                    self._bpbuf = QK
                    self.rope_pair(bA, bB, cs, sn, QK.t[:, 2 * hh, :], QK.t[:, 2 * hh + 1, :], tmp, 1.0 if which == 0 else 1.0 / 16.0)
                self.dma(out=dst[:, :, ts].rearrange("c p t -> p c t"), in_=QK.t[:], r=[QK], w=[self.k_RQ[tt]], sb=QK)
                if which == 1:
                    KR = QK
            Ktm = Ktp.next()
            for s in range(4):
                b0, b1 = self.psn2()
                bi = self.psb.index(b0)

                def f():
                    for ch in range(8):
                        ins = nc.tensor.transpose(self.psbf[:, bi, ch * 128:(ch + 1) * 128], KR.t[:, ch, s * 128:(s + 1) * 128], self.identb.t[:])
                    return ins
                self.op(self.pe, f, r=[KR, self.identb], w=[b0])
                self.op(self.dve, lambda: nc.vector.tensor_copy(out=Ktm.t[:, s, :], in_=self.psbf[:, bi, :]), r=[b0], w=[Ktm])
            self.dma(out=self.RK[ts, :].rearrange("(s p) f -> p s f", p=128), in_=Ktm.t[:], r=[Ktm], w=[self.k_R[tt * 4 + q] for q in range(4)], sb=Ktm)
            for which, dst in ((2, self.RV), (3, self.RG)):
                Vt = Vp.next()
                for s in range(4):
                    for hf in range(4):
                        bank = self.psn()
                        c0 = (2 * D if which == 2 else 4 * D) + hf * 512

                        def f():
                            for k in range(8):
                                ins = self.mm(bank.t, H.t[:, k, s * 128:(s + 1) * 128], W.t[:, k, c0:c0 + 512], k == 0, k == 7)
                            return ins
                        self.op(self.pe, f, r=[W, H], w=[bank])
                        if which == 3:
                            self.op(self.act, lambda: nc.scalar.activation(out=Vt.t[:, s, hf * 512:(hf + 1) * 512], in_=bank.t, func=AF.Silu), r=[bank], w=[Vt])
                        elif hf % 2:
                            self.op(self.act, lambda: nc.scalar.activation(out=Vt.t[:, s, hf * 512:(hf + 1) * 512], in_=bank.t, func=AF.Copy), r=[bank], w=[Vt])
                        else:
                            self.op(self.dve, lambda: nc.vector.tensor_copy(out=Vt.t[:, s, hf * 512:(hf + 1) * 512], in_=bank.t), r=[bank], w=[Vt])
                self.dma(out=dst[ts, :].rearrange("(s p) f -> p s f", p=128), in_=Vt.t[:], r=[Vt], w=[self.k_R[tt * 4 + q] for q in range(4)], sb=Vt)
        self.end_phase()

    def phase_ret(self, l):
        nc = self.nc
        j = l // 2
        NCH = self.NCH
        dv, act, pool, pe = self.dve, self.act, self.pool, self.pe
        self.begin_phase()
        TF = self.sb([128, 128], F32, "TF")
        TB = self.sb([128, 128], F32, "TB")
        P1 = self.sb([128, 128], F32, "P1")
        P2 = self.sb([128, 128], F32, "P2")
        CC = self.sb([128, 4], F32, "CC")
        MF = self.sb([128, NCH], F32, "MF")
        MB = self.sb([128, NCH], F32, "MB")
        for bfr, src in ((TF, self.c_tftb[0]), (TB, self.c_tftb[1]), (P1, self.c_pos[0]), (P2, self.c_pos[1]), (CC, self.c_col), (MF, self.c_mfb[0]), (MB, self.c_mfb[1])):
            self.dma(out=bfr.t[:], in_=src, w=[bfr], sb=bfr)
        DT = self.sb([128, 4, 128], F32, "DT")
        e1 = self.sb([128, 128], F32, "e1")
        e2 = self.sb([128, 128], F32, "e2")
        XIF = self.sb([128, 8, 128], BF16, "XIF")
        XIB = self.sb([128, 8, 128], BF16, "XIB")
        ZF = self.sb([128, 4, NCH], F32, "ZF")
        ZB = self.sb([128, 4, NCH], F32, "ZB")
        CDF = self.sb([128, 4, NCH], F32, "CDF")
        CDB = self.sb([128, 4, NCH], F32, "CDB")
        sc = self.sb([128, 16], F32, "sc")
        for h in range(4):
            lf = self.lg.t[:, j * 8 + h:j * 8 + h + 1]
            lb = self.lg.t[:, j * 8 + 4 + h:j * 8 + 4 + h + 1]
            self.op(act, lambda: nc.scalar.activation(out=e1.t[:], in_=TF.t[:], func=AF.Exp, scale=lf), r=[TF, self.lg], w=[e1])
            self.op(act, lambda: nc.scalar.activation(out=e2.t[:], in_=TB.t[:], func=AF.Exp, scale=lb), r=[TB, self.lg], w=[e2])
            self.op(dv, lambda: nc.vector.tensor_tensor(out=DT.t[:, h, :], in0=e1.t[:], in1=e2.t[:], op=ALU.add), r=[e1, e2], w=[DT])
            for q in range(2):
                self.op(act, lambda: nc.scalar.activation(out=XIF.t[:, 2 * h + q, :], in_=P1.t[:], func=AF.Exp, scale=lf), r=[P1, self.lg], w=[XIF])
                self.op(act, lambda: nc.scalar.activation(out=XIB.t[:, 2 * h + q, :], in_=P2.t[:], func=AF.Exp, scale=lb), r=[P2, self.lg], w=[XIB])
            self.op(act, lambda: nc.scalar.activation(out=sc.t[:, 4 * h + 0:4 * h + 1], in_=CC.t[:, 0:1], func=AF.Exp, scale=lf), r=[CC, self.lg], w=[sc])
            self.op(act, lambda: nc.scalar.activation(out=sc.t[:, 4 * h + 1:4 * h + 2], in_=CC.t[:, 2:3], func=AF.Exp, scale=lf), r=[CC, self.lg], w=[sc])
            self.op(act, lambda: nc.scalar.activation(out=sc.t[:, 4 * h + 2:4 * h + 3], in_=CC.t[:, 1:2], func=AF.Exp, scale=lb), r=[CC, self.lg], w=[sc])
            self.op(act, lambda: nc.scalar.activation(out=sc.t[:, 4 * h + 3:4 * h + 4], in_=CC.t[:, 2:3], func=AF.Exp, scale=lb), r=[CC, self.lg], w=[sc])
            for k, (dst, msk) in enumerate(((ZF, MF), (CDF, MF), (ZB, MB), (CDB, MB))):
                self.op(dv, lambda: nc.vector.tensor_scalar(out=dst.t[:, h, :], in0=msk.t[:], scalar1=sc.t[:, 4 * h + k:4 * h + k + 1], scalar2=None, op0=ALU.mult), r=[msk, sc], w=[dst])

        Rf = [self.sb([128, 2, 512], F32, "Rf") for _ in range(4)]
        Rb = [self.sb([128, 2, 512], BF16, "Rb") for _ in range(4)]
        QGp = self.rot(2, [128, 8, TT], BF16, "QG")
        KGp = self.rot(2, [128, 8, TT], BF16, "KG")
        Ktp = self.rot(3, [128, D], BF16, "Ktm")
        Vcp = self.rot(3, [128, 2 * D], BF16, "Vc")
        SGp = self.rot(3, [128, 2 * D], BF16, "SG")
        OBp = self.rot(3, [128, 2 * D], BF16, "OBt")
        Qxp = self.rot(2, [128, 8, 128], BF16, "Qx")
        Kzp = self.rot(2, [128, D], BF16, "Kz")
        Smp = self.rot(4, [128, 128], BF16, "Sm")
        Ofp = self.rot(3, [128, 4, 512], F32, "Of")
        junk = self.sb([128, 512], BF16, "junk")
        ssp = self.rot(2, [128, 4], F32, "ss")
        r1p = self.rot(2, [128, 4], F32, "r1")
        Yp = self.rot(2, [128, 2 * D], BF16, "Y")
        YTp = self.rot(2, [128, 16, 128], BF16, "YT")
        B = self.psb

        def make_kz(c, Ktm, Ztab):
            Kz = Kzp.next()
            for h in range(4):
                self.op(act, lambda: nc.scalar.activation(out=Kz.t[:, h * 256:(h + 1) * 256], in_=Ktm.t[:, h * 256:(h + 1) * 256], func=AF.Copy, scale=Ztab.t[:, h, c:c + 1]), r=[Ktm, Ztab], w=[Kz])
            return Kz

        def update_state(c, Kz, Vc, CDtab):
            for h in range(4):
                bi = 2 * (h % 2)
                b0, b1 = B[bi], B[bi + 1]

                def f():
                    for dk in range(2):
                        ins = self.mm(self.ps[:, bi + dk, :], Kz.t[:, h * 256 + dk * 128:h * 256 + (dk + 1) * 128], Vc.t[:, h * 512:(h + 1) * 512], True, True)
                    return ins
                self.op(pe, f, r=[Kz, Vc], w=[b0, b1])
                self.op(dv, lambda: nc.vector.scalar_tensor_tensor(out=Rf[h].t[:], in0=Rf[h].t[:], scalar=CDtab.t[:, h, c:c + 1], in1=self.ps[:, bi:bi + 2, :], op0=ALU.mult, op1=ALU.add), r=[Rf[h], b0, b1, CDtab], w=[Rf[h]])
                self.op(act, lambda: nc.scalar.activation(out=Rb[h].t[:], in_=Rf[h].t[:], func=AF.Copy), r=[Rf[h]], w=[Rb[h]])

        for sweep in (0, 1):
            for h in range(4):
                self.op(dv, lambda: nc.vector.memset(Rf[h].t[:], 0.0), w=[Rf[h]])
                self.op(dv, lambda: nc.vector.memset(Rb[h].t[:], 0.0), w=[Rb[h]])
            order = list(range(NCH - 1, -1, -1)) if sweep == 0 else list(range(NCH))
            curg, QG, KG = -1, None, None
            prev_post = [None]
            for c in order:
                g, off = c // 4, (c % 4) * 128
                csl = slice(c * 128, (c + 1) * 128)
                if g != curg:
                    curg = g
                    QG = QGp.next()
                    self.dma(out=QG.t[:], in_=self.QTs[:, :, self.tsl(g)].rearrange("c p t -> p c t"), r=[self.k_RQ[g]], w=[QG], sb=QG)
                    if sweep == 1:
                        KG = KGp.next()
                        self.dma(out=KG.t[:], in_=self.KTs[:, :, self.tsl(g)].rearrange("c p t -> p c t"), r=[self.k_RQ[g]], w=[KG], sb=KG)
                Ktm, Vc = Ktp.next(), Vcp.next()
                self.dma(out=Ktm.t[:], in_=self.RK[csl, :], r=[self.k_R[c]], w=[Ktm], sb=Ktm)
                self.dma(out=Vc.t[:], in_=self.RV[csl, :], r=[self.k_R[c]], w=[Vc], sb=Vc)
                if sweep == 1:
                    SG, OBt = SGp.next(), OBp.next()
                    self.dma(out=SG.t[:], in_=self.RG[csl, :], r=[self.k_R[c]], w=[SG], sb=SG)
                    self.dma(out=OBt.t[:], in_=self.OB[csl, :], r=[self.k_OB[c]], w=[OBt], sb=OBt)
                Qx = Qxp.next()
                XI = XIB if sweep == 0 else XIF
                self.op(dv, lambda: nc.vector.tensor_tensor(out=Qx.t[:], in0=QG.t[:, :, off:off + 128], in1=XI.t[:], op=ALU.mult), r=[QG, XI], w=[Qx])
                Kz = make_kz(c, Ktm, ZB if sweep == 0 else ZF)
                if sweep == 0:
                    OBt = OBp.next()
                    for h in range(4):
                        bank = B[4 + h]

                        def f():
                            for dk in range(2):
                                ins = self.mm(bank.t, Qx.t[:, 2 * h + dk, :], Rb[h].t[:, dk, :], dk == 0, dk == 1)
                            return ins
                        self.op(pe, f, r=[Qx, Rb[h]], w=[bank])
                    update_state(c, Kz, Vc, CDB)
                    for h in range(4):
                        bank = B[4 + h]
                        if h % 2:
                            self.op(act, lambda: nc.scalar.activation(out=OBt.t[:, h * 512:(h + 1) * 512], in_=bank.t, func=AF.Copy), r=[bank], w=[OBt])
                        else:
                            self.op(dv, lambda: nc.vector.tensor_copy(out=OBt.t[:, h * 512:(h + 1) * 512], in_=bank.t), r=[bank], w=[OBt])
                    self.dma(out=self.OB[csl, :], in_=OBt.t[:], r=[OBt], w=[self.k_OB[c]], sb=OBt)
                    continue
                Sms = []
                for h in range(4):
                    bS = B[h]

                    def f():
                        for dk in range(2):
                            ins = self.mm(bS.t[:, 0:128], KG.t[:, 2 * h + dk, off:off + 128], QG.t[:, 2 * h + dk, off:off + 128], dk == 0, dk == 1)
                        return ins
                    self.op(pe, f, r=[KG, QG], w=[bS])
                for h in range(4):
                    Sm = Smp.next()
                    self.op(dv, lambda: nc.vector.tensor_tensor(out=Sm.t[:], in0=B[h].t[:, 0:128], in1=DT.t[:, h, :], op=ALU.mult), r=[B[h], DT], w=[Sm])
                    Sms.append(Sm)
                for h in range(4):
                    bO, Sm = B[4 + h], Sms[h]

                    def f2():
                        self.mm(bO.t, Sm.t[:], Vc.t[:, h * 512:(h + 1) * 512], True, False)
                        for dk in range(2):
                            ins = self.mm(bO.t, Qx.t[:, 2 * h + dk, :], Rb[h].t[:, dk, :], False, dk == 1)
                        return ins
                    self.op(pe, f2, r=[Sm, Vc, Qx, Rb[h]], w=[bO])
                Of, ss = Ofp.next(), ssp.next()
                for h in range(4):
                    bO = B[4 + h]
                    self.op(dv, lambda: nc.vector.tensor_tensor(out=Of.t[:, h, :], in0=bO.t, in1=OBt.t[:, h * 512:(h + 1) * 512], op=ALU.add), r=[bO, OBt], w=[Of])
                if prev_post[0] is not None:
                    prev_post[0]()
                update_state(c, Kz, Vc, CDF)

                def post(Of=Of, ss=ss, SG=SG, g=g, off=off):
                    for h in range(4):
                        self.op(act, lambda: nc.scalar.activation(out=junk.t[:], in_=Of.t[:, h, :], func=AF.Square, accum_out=ss.t[:, h:h + 1]), r=[Of], w=[junk, ss])
                    r1 = r1p.next()
                    self.rsqrt(ss.t[:], [ss], r1.t[:], r1, 512 * 1e-6, 4)
                    self.op(dv, lambda: nc.vector.tensor_scalar(out=r1.t[:], in0=r1.t[:], scalar1=math.sqrt(512.0), scalar2=None, op0=ALU.mult), r=[r1], w=[r1])
                    Y = Yp.next()
                    for h in range(4):
                        self.op(dv, lambda: nc.vector.scalar_tensor_tensor(out=Y.t[:, h * 512:(h + 1) * 512], in0=Of.t[:, h, :], scalar=r1.t[:, h:h + 1], in1=SG.t[:, h * 512:(h + 1) * 512], op0=ALU.mult, op1=ALU.mult), r=[Of, r1, SG], w=[Y])
                    YT = YTp.next()
                    b0, b1 = B[0], B[1]

                    def f3():
                        for q in range(16):
                            ins = nc.tensor.transpose(self.psbf[:, q // 8, (q % 8) * 128:(q % 8 + 1) * 128], Y.t[:, q * 128:(q + 1) * 128], self.identb.t[:])
                        return ins
                    self.op(pe, f3, r=[Y, self.identb], w=[b0, b1])
                    self.op(act, lambda: nc.scalar.activation(out=YT.t[:, 0:8, :], in_=self.psbf[:, 0, :], func=AF.Copy), r=[b0], w=[YT])
                    self.op(dv, lambda: nc.vector.tensor_copy(out=YT.t[:, 8:16, :], in_=self.psbf[:, 1, :]), r=[b1], w=[YT])
                    self.dma(out=self.AT[g, :, :, off:off + 128], in_=YT.t[:], r=[YT], w=[self.k_AT[g]], sb=YT)
                prev_post[0] = post
            if sweep == 1 and prev_post[0] is not None:
                prev_post[0]()
                prev_post[0] = None
        self.end_phase()

    def phase_PA(self, l):
        nc = self.nc
        j = l // 2
        attn = (l % 2 == 0)
        nin = 8 if attn else 16
        self.begin_phase()
        W = self.sb([128, nin, D], BF16, "W")
        stage = self.rot(2, [128, 2048], F32, "stg")
        self.load_weight(self.w_o[j] if attn else self.w_rout[j], W, stage)
        Ap = self.rot(2, [128, nin, TT], BF16, "A")
        Xp = self.rot(2, [128, 8, TT], F32, "X")
        Mp = self.rot(2, [128, 8, TT], F32, "M")
        sq = self.sb([128, 8, TT], BF16, "sq")
        rs = self.rot(2, [128, TT], F32, "rs")
        Hp = self.rot(2, [128, 8, TT], BF16, "H")

        def S1(tt):
            ts = self.tsl(tt)
            st = {}

            def p0():
                A = Ap.next()
                self.dma(out=A.t[:], in_=self.AT[tt, :, 0:nin, :], r=[self.k_AT[tt]], w=[A], sb=A)
                X = Xp.next()
                self.load_xT(X, tt)
                st["A"], st["X"], st["M"] = A, X, Mp.next()
            pcs = [p0] + [(lambda dc=dc: self.proj_piece(W, nin, st["A"], st["M"], dc)) for dc in range(8)]
            return pcs, st

        def S2(tt, st):
            ts = self.tsl(tt)
            c2 = {}

            def a():
                c2["b1"] = self.norm_ss(st["M"], 8, sq)

            def b():
                X, M = st["X"], st["M"]
                r1 = rs.next()
                self.rsqrt(c2["b1"].t, [c2["b1"]], r1.t[:], r1, D * 1e-6)
                self.resid_norm(X, M, (l * 4 + 1) * 8, r1, M)
                self.store_xT(X, tt)
                c2["b2"] = self.norm_ss(X, 8, sq)

            def c():
                X = st["X"]
                r2 = rs.next()
                self.rsqrt(c2["b2"].t, [c2["b2"]], r2.t[:], r2, D * 1e-6)
                H = Hp.next()
                self.scale_norm(H, X, (l * 4 + 2) * 8, r2)
                self.dma(out=self.H2s[tt], in_=H.t[:], r=[H], w=[self.k_H2[tt]], sb=H)
            return [lambda: None, lambda: None, a, lambda: None, b, lambda: None, lambda: None, c]

        pcs, st = S1(0)
        for p_ in pcs:
            p_()
        for tt in range(self.NTT):
            if tt + 1 < self.NTT:
                npcs, nst = S1(tt + 1)
            else:
                npcs, nst = [], None
            self.interleave(npcs, S2(tt, st))
            st = nst
        self.end_phase()

    def phase_PB(self, l):
        nc = self.nc
        self.begin_phase()
        W = self.sb([128, 8, DFF], BF16, "W")
        stage = self.rot(3, [128, 2048], F32, "stg")
        self.load_weight(self.w_1[l], W, stage)
        Hp = self.rot(2, [128, 8, TT], BF16, "H")
        Up = self.rot(3, [128, 8, TT], BF16, "U")
        tp = self.rot(4, [128, TT], F32, "tmp")
        for tt in range(self.NTT):
            ts = self.tsl(tt)
            H = Hp.next()
            self.dma(out=H.t[:], in_=self.H2s[tt], r=[self.k_H2[tt]], w=[H], sb=H)
            for fg in range(4):
                U = Up.next()
                for fi in range(8):
                    fc = fg * 8 + fi
                    bank = self.psn()

                    def f():
                        for k in range(8):
                            ins = self.mm(bank.t, W.t[:, k, fc * 128:(fc + 1) * 128], H.t[:, k, :], k == 0, k == 7)
                        return ins
                    self.op(self.pe, f, r=[W, H], w=[bank])
                    t = tp.next()
                    self.op(self.act, lambda: nc.scalar.activation(out=t.t[:], in_=bank.t, func=AF.Relu), r=[bank], w=[t])
                    E = self.ew()
                    self.op(E, lambda: E.h.tensor_tensor(out=U.t[:, fi, :], in0=t.t[:], in1=t.t[:], op=ALU.mult), r=[t], w=[U])
                self.dma(out=self.Us[tt, :, fg * 8:(fg + 1) * 8, :], in_=U.t[:], r=[U], w=[self.k_U[tt]], sb=U)
        self.end_phase()

    def phase_PC(self, l):
        nc = self.nc
        final = (l == self.depth - 1)
        self.begin_phase()
        W2 = self.sb([128, 32, D], BF16, "W2")
        Wg = self.sb([128, 8, D], BF16, "Wg")
        Wp = self.sb([128, 2, D], BF16, "Wp")
        stage = self.rot(2, [128, 512], F32, "stg")
        self.load_weight(self.w_2[l], W2, stage, 512)
        self.load_weight(self.w_pg[l], Wg, stage, 512)
        self.load_weight(self.w_pp[l], Wp, stage, 512)
        Up = self.rot(1, [128, 32, TT], BF16, "U")
        Xp = self.rot(1, [128, 8, TT], F32, "X")
        Mp = self.rot(2, [128, 8, TT], F32, "M")
        sq = self.sb([128, 8, TT], BF16, "sq")
        rs = self.rot(2, [128, TT], F32, "rs")
        pin = self.rot(1, [128, 4, PLE], F32, "pin")
        pTp = self.rot(1, [128, 2, TT], BF16, "pT")
        tg = self.rot(1, [128, TT], F32, "tg")
        tq = self.rot(1, [128, TT], F32, "tq")

        def S1(tt):
            ts = self.tsl(tt)
            st = {}

            def p0():
                U = Up.next()
                for fg in range(4):
                    self.dma(out=U.t[:, fg * 8:(fg + 1) * 8, :], in_=self.Us[tt, :, fg * 8:(fg + 1) * 8, :], r=[self.k_U[tt]], w=[U], sb=U)
                st["U"], st["M"] = U, Mp.next()
            pcs = [p0] + [(lambda dc=dc: self.proj_piece(W2, 32, st["U"], st["M"], dc)) for dc in range(8)]
            return pcs, st

        def S2(tt, st):
            ts = self.tsl(tt)
            M = st["M"]
            c2 = {}

            def a():
                X = Xp.next()
                self.load_xT(X, tt)
                pi = pin.next()
                self.dma(out=pi.t[:], in_=self.p_in[l, ts, :].rearrange("(s p) f -> p s f", p=128), w=[pi], sb=pi)
                pT = pTp.next()
                for kc in range(2):
                    bank = self.psn()

                    def f():
                        for s_ in range(4):
                            ins = nc.tensor.transpose(bank.t[:, s_ * 128:(s_ + 1) * 128], pi.t[:, s_, kc * 128:(kc + 1) * 128], self.identf.t[:])
                        return ins
                    self.op(self.pe, f, r=[pi, self.identf], w=[bank])
                    self.op(self.dve, lambda: nc.vector.tensor_copy(out=pT.t[:, kc, :], in_=bank.t), r=[bank], w=[pT])
                c2["X"], c2["pT"] = X, pT
                c2["b1"] = self.norm_ss(M, 8, sq)

            def b():
                X = c2["X"]
                r1 = rs.next()
                self.rsqrt(c2["b1"].t, [c2["b1"]], r1.t[:], r1, D * 1e-6)
                self.resid_norm(X, M, (l * 4 + 3) * 8, r1, M)
                c2["b2"] = self.norm_ss(X, 8, sq)

            def c():
                X = c2["X"]
                r2 = rs.next()
                self.rsqrt(c2["b2"].t, [c2["b2"]], r2.t[:], r2, D * 1e-6)
                self.scale_norm(sq, X, None, r2)

            def gate(dc):
                X, pT = c2["X"], c2["pT"]
                bg, bp = self.psn(), self.psn()

                def f():
                    for k in range(8):
                        ins = self.mm(bg.t, Wg.t[:, k, dc * 128:(dc + 1) * 128], sq.t[:, k, :], k == 0, k == 7)
                    return ins
                self.op(self.pe, f, r=[Wg, sq], w=[bg])

                def f2():
                    for k in range(2):
                        ins = self.mm(bp.t, Wp.t[:, k, dc * 128:(dc + 1) * 128], pT.t[:, k, :], k == 0, k == 1)
                    return ins
                self.op(self.pe, f2, r=[Wp, pT], w=[bp])
                g = tg.next()
                self.op(self.act, lambda: nc.scalar.activation(out=g.t[:], in_=bg.t, func=AF.Sigmoid), r=[bg], w=[g])
                q = tq.next()
                self.op(self.dve, lambda: nc.vector.tensor_tensor(out=q.t[:], in0=bp.t, in1=g.t[:], op=ALU.mult), r=[bp, g], w=[q])
                self.op(self.dve, lambda: nc.vector.tensor_tensor(out=X.t[:, dc, :], in0=X.t[:, dc, :], in1=q.t[:], op=ALU.add), r=[X, q], w=[X])

            def fin():
                X = c2["X"]
                if not final:
                    self.store_xT(X, tt)
                    return
                Yo = M
                for s_ in range(4):
                    for hf in range(2):
                        bank = self.psn()

                        def f():
                            for q4 in range(4):
                                ins = nc.tensor.transpose(bank.t[:, q4 * 128:(q4 + 1) * 128], X.t[:, hf * 4 + q4, s_ * 128:(s_ + 1) * 128], self.identf.t[:])
                            return ins
                        self.op(self.pe, f, r=[X, self.identf], w=[bank])
                        o_ap = Yo.t[:, 2 * s_ + hf, :]
                        if hf:
                            self.op(self.act, lambda: nc.scalar.activation(out=o_ap, in_=bank.t, func=AF.Copy), r=[bank], w=[Yo])
                        else:
                            self.op(self.dve, lambda: nc.vector.tensor_copy(out=o_ap, in_=bank.t), r=[bank], w=[Yo])
                self.dma(out=self.y_out[ts, :].rearrange("(s p) (h f) -> p s h f", p=128, h=2), in_=Yo.t[:].rearrange("p (s h) f -> p s h f", h=2), r=[Yo], w=[], sb=Yo)

            def g2(d0):
                def run():
                    for dc in range(d0, d0 + 2):
                        gate(dc)
                    if d0 == 6:
                        fin()
                return run
            return [a, lambda: None, b, lambda: None, c, g2(0), g2(2), g2(4), g2(6)]

        pcs, st = S1(0)
        for p_ in pcs:
            p_()
        for tt in range(self.NTT):
            if tt + 1 < self.NTT:
                npcs, nst = S1(tt + 1)
            else:
                npcs, nst = [], None
            self.interleave(npcs, S2(tt, st))
            st = nst
        self.end_phase()


def _const_tables(NT, seqs):
    assert sum(seqs) == NT
    pos = np.concatenate([np.arange(s) for s in seqs]).astype(np.float32)
    sid = np.concatenate([np.full(s, i) for i, s in enumerate(seqs)])
    inv = (1.0 / (np.float32(10000.0) ** (np.arange(0, 64, 2, dtype=np.float32) / np.float32(64)))).astype(np.float32)
    angA = pos[None, :] * inv[np.arange(128) % 32][:, None]
    ropeA = np.stack([np.cos(angA), np.sin(angA)]).astype(np.float32)
    angle = (1.0 / (np.float32(10000.0) ** np.linspace(0.0, 1.0, 128, dtype=np.float32))).astype(np.float32)
    angR = pos[None, :] * angle[:, None]
    ropeR = np.stack([np.cos(angR), np.sin(angR)]).astype(np.float32)
    NTT, NCH = NT // 512, NT // 128
    NKP = NCH // 2
    qs = sid[np.arange(NTT) * 512]
    ks = sid[np.arange(NKP) * 256]
    mb = np.where(qs[:, None] == ks[None, :], 0.0, NEG).astype(np.float32).reshape(1, -1)
    maskb = np.repeat(mb, 128, axis=0)
    k = np.arange(128)[:, None].astype(np.float32)
    q = np.arange(128)[None, :].astype(np.float32)
    BIG = np.float32(1.0e6)
    TFm = np.where(q >= k, q - k, BIG).astype(np.float32)
    TBm = np.where(k > q, k - q, BIG).astype(np.float32)
    posq = np.stack([np.repeat(q + 1.0, 128, axis=0), np.repeat(128.0 - q, 128, axis=0)]).astype(np.float32)
    pcol = np.arange(128, dtype=np.float32)
    col = np.stack([127.0 - pcol, pcol, np.full(128, 128.0, np.float32), np.zeros(128, np.float32)], axis=1).astype(np.float32)
    csid = sid[np.arange(NCH) * 128]
    last = np.ones(NCH, bool)
    last[:-1] = csid[1:] != csid[:-1]
    first = np.ones(NCH, bool)
    first[1:] = csid[1:] != csid[:-1]
    mfb = np.stack([np.repeat(np.where(last, 0.0, 1.0)[None, :], 128, axis=0), np.repeat(np.where(first, 0.0, 1.0)[None, :], 128, axis=0)]).astype(np.float32)
    return dict(c_ropeA=ropeA, c_ropeR=ropeR, c_maskb=maskb, c_tftb=np.stack([TFm, TBm]), c_pos=posq, c_col=col, c_mfb=mfb,
                c_ident=np.eye(128, dtype=np.float32))


def _perm_attn():
    idx = []
    for jj in range(4):
        for half in range(2):
            for r in range(4):
                m = 4 * jj + r
                idx.extend(range(m * 64 + half * 32, m * 64 + half * 32 + 32))
    return np.array(idx)


def _perm_ret():
    idx = []
    for h in range(4):
        idx.extend(range(h * 256, (h + 1) * 256, 2))
        idx.extend(range(h * 256 + 1, (h + 1) * 256, 2))
    return np.array(idx)


def _shared_inputs(inp):
    f = lambda a: np.ascontiguousarray(np.asarray(a, dtype=np.float32))
    pa, prr = _perm_attn(), _perm_ret()
    wqkv = f(inp["attn_w_qkv"])
    wqkv = np.concatenate([wqkv[:, :, 0:D][:, :, pa], wqkv[:, :, D:2 * D][:, :, pa], wqkv[:, :, 2 * D:]], axis=2)
    wrin = f(inp["ret_w_in"])
    wrin = np.concatenate([wrin[:, :, 0:D][:, :, prr], wrin[:, :, D:2 * D][:, :, prr], wrin[:, :, 2 * D:]], axis=2)
    norms = [f(inp[k]) for k in ("norm_pre_mix", "norm_post_mix", "norm_pre_mlp", "norm_post_mlp")]
    gains = np.zeros((128, 128), np.float32)
    for l in range(4):
        for k in range(4):
            gains[:, (l * 4 + k) * 8:(l * 4 + k + 1) * 8] = norms[k][l].reshape(8, 128).T
    lam = np.stack([f(inp[k]) for k in ("attn_lambda_q1", "attn_lambda_k1", "attn_lambda_q2", "attn_lambda_k2")], axis=1)
    lamv = np.repeat(lam.reshape(1, -1), 128, axis=0)
    decay = np.repeat(f(inp["ret_decay"]).reshape(1, -1), 128, axis=0)
    return dict(gains=gains, subln=np.ascontiguousarray(f(inp["attn_subln"]).T), lamv=np.ascontiguousarray(lamv),
                decay=np.ascontiguousarray(decay), w_qkv=np.ascontiguousarray(wqkv), w_o=f(inp["attn_w_o"]),
                w_rin=np.ascontiguousarray(wrin), w_rout=f(inp["ret_w_out"]), w_1=f(inp["mlp_w_in"]), w_2=f(inp["mlp_w_out"]),
                w_pp=f(inp["ple_w_proj"]), w_pg=f(inp["ple_w_gate"]))


_NC_CACHE = {}


def run_cores(inp, core_specs, NT, depth=4, debug=False):
    key = (NT, depth, debug)
    if key not in _NC_CACHE:
        _NC_CACHE[key] = KB(NT, depth, debug).build()
    nc = _NC_CACHE[key]
    shared = _shared_inputs(inp)
    in_maps = []
    for x, p, seqs in core_specs:
        m = dict(shared)
        m.update(_const_tables(NT, seqs))
        m["x"] = np.ascontiguousarray(x, dtype=np.float32)
        m["p"] = np.ascontiguousarray(p, dtype=np.float32)
        in_maps.append(m)
    res = run_bass_kernel_spmd(nc, in_maps, core_ids=list(range(len(in_maps))))
    if debug:
        return res.results
    return [r["y"] for r in res.results]


def kernel(**inputs):
    xp = np.asarray(inputs["x_prompt"], dtype=np.float32)
    xs = np.asarray(inputs["x_sample"], dtype=np.float32)
    pp = np.asarray(inputs["p_prompt"], dtype=np.float32)
    psm = np.asarray(inputs["p_sample"], dtype=np.float32)
    NT = 8192
    specs = []
    for c in range(4):
        specs.append((xp[4 * c:4 * c + 4].reshape(NT, D), pp[:, 4 * c:4 * c + 4].reshape(4, NT, PLE), [2048] * 4))
    for c in range(4):
        specs.append((xs[c], psm[:, c], [8192]))
    ys = run_cores(inputs, specs, NT, 4)
    y_prompt = np.stack([ys[c].reshape(4, 2048, D) for c in range(4)]).reshape(16, 2048, D).astype(np.float32)
    y_sample = np.stack([ys[4 + c] for c in range(4)]).astype(np.float32)
    return (y_prompt, y_sample)
```
